# Optimizing a Trainium2 kernel written in Bass

```python
import math
import jax, jax.numpy as jnp
from jax import lax
import numpy as np

D_MODEL = 1024
BATCH = 8
SEQ = 8192
DEPTH = 4

CTX_LEN = 256
GRID_W = 64
EPS = 1e-6

ATT_HEADS = 8
ATT_KV_HEADS = 2
HEAD_DIM = 64
ATT_GROUP = ATT_HEADS // ATT_KV_HEADS
ATT_WIDTH = ATT_HEADS * HEAD_DIM
KV_WIDTH = ATT_KV_HEADS * HEAD_DIM
Q_BLOCK = 128
ROPE_THETA = 10000.0

RNN_WIDTH = D_MODEL // 2
RNN_BLOCKS = 8
RNN_BLOCK_DIM = RNN_WIDTH // RNN_BLOCKS
CONV_WIDTH = 4
LRU_C = 8.0

S5_WIDTH = D_MODEL
S5_GROUP = 16
S5_GROUPS = S5_WIDTH // S5_GROUP
S5_STATE = 64
DT_MIN = 1e-3
DT_MAX = 1e-1

EVEN_IN = ATT_WIDTH + 2 * KV_WIDTH + ATT_WIDTH + 2 * RNN_WIDTH
EVEN_SPLITS = (ATT_WIDTH, ATT_WIDTH + KV_WIDTH, ATT_WIDTH + 2 * KV_WIDTH,
               2 * ATT_WIDTH + 2 * KV_WIDTH, 2 * ATT_WIDTH + 2 * KV_WIDTH + RNN_WIDTH)
EVEN_MIX = ATT_WIDTH + RNN_WIDTH
N_EVEN = (DEPTH + 1) // 2
N_ODD = DEPTH // 2

kernel_name = "hybrid_gqa_rglru_s5_ctx_prefix_dit"


def rms_norm(x, gain=None):
    xf = x.astype(jnp.float32)
    y = xf * lax.rsqrt(jnp.mean(xf * xf, axis=-1, keepdims=True) + EPS)
    if gain is not None:
        y = y * gain.astype(jnp.float32)
    return y.astype(x.dtype)


def ada_mod(cond, w, b):
    mod = jnp.dot(jax.nn.silu(cond), w) + b
    return jnp.split(mod, 3, axis=-1)


def axial_rope_tables(n_tokens):
    rows = n_tokens // GRID_W
    row = jnp.repeat(jnp.arange(rows, dtype=jnp.float32), GRID_W)
    col = jnp.tile(jnp.arange(GRID_W, dtype=jnp.float32), rows)
    n_freq = HEAD_DIM // 4
    inv = ROPE_THETA ** (-jnp.arange(n_freq, dtype=jnp.float32) / n_freq)
    ang = jnp.concatenate([row[:, None] * inv, col[:, None] * inv], axis=-1)
    return jnp.cos(ang), jnp.sin(ang)


def apply_rope(x, cos, sin):
    xf = x.astype(jnp.float32).reshape(*x.shape[:-1], HEAD_DIM // 2, 2)
    x0, x1 = xf[..., 0], xf[..., 1]
    cs, sn = cos[None, :, None, :], sin[None, :, None, :]
    out = jnp.stack([x0 * cs - x1 * sn, x0 * sn + x1 * cs], axis=-1)
    return out.reshape(x.shape).astype(x.dtype)


def _attend(q, k, v):
    s = jnp.einsum('bqhgd,bkhd->bhgqk', q, k).astype(jnp.float32) * (HEAD_DIM ** -0.5)
    p = jax.nn.softmax(s, axis=-1).astype(v.dtype)
    return jnp.einsum('bhgqk,bkhd->bqhgd', p, v)


def blocked_attention(q, k, v):
    b, s = q.shape[0], q.shape[1]
    nblk = s // Q_BLOCK
    qb = q.reshape(b, nblk, Q_BLOCK, *q.shape[2:]).swapaxes(0, 1)
    out = lax.map(lambda qi: _attend(qi, k, v), qb)
    return out.swapaxes(0, 1).reshape(b, s, ATT_WIDTH)


def attention_mixer(qc, kc, vc, qx, kx, vx, q_gain, k_gain, cos, sin):
    def heads(t, n):
        return t.reshape(*t.shape[:-1], n, HEAD_DIM)
    b, lc = qc.shape[0], qc.shape[1]
    qc = rms_norm(heads(qc, ATT_HEADS), q_gain)
    kc = rms_norm(heads(kc, ATT_KV_HEADS), k_gain)
    vc = heads(vc, ATT_KV_HEADS)
    qx = apply_rope(rms_norm(heads(qx, ATT_HEADS), q_gain), cos, sin)
    kx = apply_rope(rms_norm(heads(kx, ATT_KV_HEADS), k_gain), cos, sin)
    vx = heads(vx, ATT_KV_HEADS)
    qc5 = qc.reshape(b, lc, ATT_KV_HEADS, ATT_GROUP, HEAD_DIM)
    out_c = _attend(qc5, kc, vc).reshape(b, lc, ATT_WIDTH)
    k_all = jnp.concatenate([kx, kc], axis=1)
    v_all = jnp.concatenate([vx, vc], axis=1)
    qx5 = qx.reshape(b, qx.shape[1], ATT_KV_HEADS, ATT_GROUP, HEAD_DIM)
    out_x = blocked_attention(qx5, k_all, v_all)
    return out_c, out_x


def depthwise_conv(x, w, b):
    ch = x.shape[-1]
    y = lax.conv_general_dilated(x, w[:, None, :].astype(x.dtype), window_strides=(1,),
                                 padding=[(2, 1)], dimension_numbers=('NWC', 'WIO', 'NWC'),
                                 feature_group_count=ch)
    return y + b


def _combine_real(e1, e2):
    a1, b1 = e1
    a2, b2 = e2
    return a1 * a2, a2 * b1 + b2


def linear_scan(a, b, h0, reverse):
    idx = -1 if reverse else 0
    b = b.at[:, idx].add(a[:, idx] * h0)
    _, h = lax.associative_scan(_combine_real, (a, b), reverse=reverse, axis=1)
    return h


def rglru_gates(x, wa, ba, wx, bx, lam):
    xb = x.reshape(*x.shape[:-1], RNN_BLOCKS, RNN_BLOCK_DIM)
    r = jax.nn.sigmoid(jnp.einsum('blhi,hij->blhj', xb, wa) + ba).reshape(x.shape)
    i = jax.nn.sigmoid(jnp.einsum('blhi,hij->blhj', xb, wx) + bx).reshape(x.shape)
    log_a = (-LRU_C * jax.nn.softplus(-lam.astype(jnp.float32))) * r.astype(jnp.float32)
    a = jnp.exp(log_a)
    bterm = jnp.sqrt(-jnp.expm1(2.0 * log_a)) * (i * x).astype(jnp.float32)
    return a, bterm


def rglru_mixer(uc, ux, conv_w, conv_b, wa, ba, wx, bx, lam):
    uc = depthwise_conv(uc, conv_w, conv_b)
    ux = depthwise_conv(ux, conv_w, conv_b)
    yc = jnp.zeros(uc.shape, jnp.float32)
    yx = jnp.zeros(ux.shape, jnp.float32)
    for direction, reverse in enumerate((False, True)):
        ac, bc = rglru_gates(uc, wa[direction], ba[direction], wx[direction], bx[direction], lam[direction])
        hc = linear_scan(ac, bc, jnp.zeros_like(bc[:, 0]), reverse)
        h_last = hc[:, 0] if reverse else hc[:, -1]
        ax, bxt = rglru_gates(ux, wa[direction], ba[direction], wx[direction], bx[direction], lam[direction])
        hx = linear_scan(ax, bxt, h_last, reverse)
        yc = yc + hc
        yx = yx + hx
    return yc.astype(uc.dtype), yx.astype(ux.dtype)


def s5_discretize(lam_re, lam_im, log_step, b_re, b_im):
    dt = jnp.exp(log_step)[:, None]
    mag = jnp.exp(lam_re * dt)
    ab_re = mag * jnp.cos(lam_im * dt)
    ab_im = mag * jnp.sin(lam_im * dt)
    den = lam_re * lam_re + lam_im * lam_im
    nr, ni = ab_re - 1.0, ab_im
    f_re = (nr * lam_re + ni * lam_im) / den
    f_im = (ni * lam_re - nr * lam_im) / den
    bb_re = f_re[..., None] * b_re - f_im[..., None] * b_im
    bb_im = f_re[..., None] * b_im + f_im[..., None] * b_re
    return ab_re, ab_im, bb_re, bb_im


def _combine_complex(e1, e2):
    ar1, ai1, br1, bi1 = e1
    ar2, ai2, br2, bi2 = e2
    return (ar1 * ar2 - ai1 * ai2, ar1 * ai2 + ai1 * ar2,
            ar2 * br1 - ai2 * bi1 + br2, ar2 * bi1 + ai2 * br1 + bi2)


def complex_scan(a_re, a_im, b_re, b_im, h_re, h_im, reverse):
    idx = -1 if reverse else 0
    ar0, ai0 = a_re[:, idx], a_im[:, idx]
    b_re = b_re.at[:, idx].add(ar0 * h_re - ai0 * h_im)
    b_im = b_im.at[:, idx].add(ar0 * h_im + ai0 * h_re)
    _, _, s_re, s_im = lax.associative_scan(_combine_complex, (a_re, a_im, b_re, b_im),
                                            reverse=reverse, axis=1)
    return s_re, s_im


def s5_direction(u, ab_re, ab_im, bb_re, bb_im, c_re, c_im, h_re, h_im, reverse):
    n = u.shape[1]
    bu_re = jnp.einsum('blgj,gpj->blgp', u, bb_re)
    bu_im = jnp.einsum('blgj,gpj->blgp', u, bb_im)
    shape = (1, n) + ab_re.shape
    s_re, s_im = complex_scan(jnp.broadcast_to(ab_re, shape), jnp.broadcast_to(ab_im, shape),
                              bu_re, bu_im, h_re, h_im, reverse)
    y = jnp.einsum('blgp,gjp->blgj', s_re, c_re) - jnp.einsum('blgp,gjp->blgj', s_im, c_im)
    idx = 0 if reverse else -1
    return y, s_re[:, idx], s_im[:, idx]


def s5_mixer(uc, ux, lam_re, lam_im, log_step, b_re, b_im, c_re, c_im, d_skip):
    f32 = jnp.float32

    def grouped(u):
        return u.astype(f32).reshape(*u.shape[:-1], S5_GROUPS, S5_GROUP)

    ucg, uxg = grouped(uc), grouped(ux)
    yc = jnp.zeros_like(ucg)
    yx = jnp.zeros_like(uxg)
    h0 = jnp.zeros((uc.shape[0], S5_GROUPS, S5_STATE), f32)
    for direction, reverse in enumerate((False, True)):
        ab_re, ab_im, bb_re, bb_im = s5_discretize(
            lam_re[direction].astype(f32), lam_im[direction].astype(f32),
            log_step[direction].astype(f32), b_re[direction].astype(f32), b_im[direction].astype(f32))
        cr, ci = c_re[direction].astype(f32), c_im[direction].astype(f32)
        y_c, h_re, h_im = s5_direction(ucg, ab_re, ab_im, bb_re, bb_im, cr, ci, h0, h0, reverse)
        y_x, _, _ = s5_direction(uxg, ab_re, ab_im, bb_re, bb_im, cr, ci, h_re, h_im, reverse)
        yc = yc + y_c
        yx = yx + y_x
    dd = d_skip.astype(f32)
    yc = yc.reshape(uc.shape) + dd * uc.astype(f32)
    yx = yx.reshape(ux.shape) + dd * ux.astype(f32)
    return yc.astype(uc.dtype), yx.astype(ux.dtype)


def even_mixer(hc, hx, w_in, w_out, q_gain, k_gain, conv_w, conv_b, wa, ba, wx, bx, lam, cos, sin):
    qc, kc, vc, gac, uc, grc = jnp.split(hc @ w_in, EVEN_SPLITS, axis=-1)
    qx, kx, vx, gax, ux, grx = jnp.split(hx @ w_in, EVEN_SPLITS, axis=-1)
    att_c, att_x = attention_mixer(qc, kc, vc, qx, kx, vx, q_gain, k_gain, cos, sin)
    rnn_c, rnn_x = rglru_mixer(uc, ux, conv_w, conv_b, wa, ba, wx, bx, lam)
    mix_c = jnp.concatenate([att_c * jax.nn.silu(gac), rnn_c * jax.nn.silu(grc)], axis=-1)
    mix_x = jnp.concatenate([att_x * jax.nn.silu(gax), rnn_x * jax.nn.silu(grx)], axis=-1)
    return mix_c @ w_out, mix_x @ w_out


def odd_mixer(hc, hx, w_in, lam_re, lam_im, log_step, b_re, b_im, c_re, c_im, d_skip, glu_w, glu_b, w_out):
    uc, gc = jnp.split(hc @ w_in, 2, axis=-1)
    ux, gx = jnp.split(hx @ w_in, 2, axis=-1)
    yc, yx = s5_mixer(uc, ux, lam_re, lam_im, log_step, b_re, b_im, c_re, c_im, d_skip)

    def finish(y, g):
        a, bgate = jnp.split(jax.nn.gelu(y) @ glu_w + glu_b, 2, axis=-1)
        return (a * jax.nn.sigmoid(bgate) * jax.nn.silu(g)) @ w_out

    return finish(yc, gc), finish(yx, gx)


def setup_inputs(seed: int = 0) -> dict:
    key = jax.random.key(seed)
    k = jax.random.split(key, 30)
    f32 = jnp.float32
    D = D_MODEL

    def nrm(i, shape, scale):
        return jax.random.normal(k[i], shape, f32) * scale

    a_pow = jax.random.uniform(k[16], (N_EVEN, 2, RNN_WIDTH), f32, minval=0.9, maxval=0.999) ** (1.0 / LRU_C)
    n_idx = jnp.arange(S5_STATE, dtype=f32)
    return {
        "x": nrm(0, (BATCH, SEQ, D), 1.0),
        "c": nrm(1, (BATCH, D), 1.0),
        "ctx": nrm(2, (BATCH, CTX_LEN, D), 1.0),
        "c_ctx": nrm(3, (D,), 1.0),
        "ada_w": nrm(4, (DEPTH, D, 3 * D), 0.5 * D ** -0.5),
        "ada_b": nrm(5, (DEPTH, 3 * D), 0.02),
        "ev_w_in": nrm(6, (N_EVEN, D, EVEN_IN), D ** -0.5),
        "ev_w_out": nrm(7, (N_EVEN, EVEN_MIX, D), EVEN_MIX ** -0.5),
        "q_norm_w": 1.0 + nrm(8, (N_EVEN, HEAD_DIM), 0.02),
        "k_norm_w": 1.0 + nrm(9, (N_EVEN, HEAD_DIM), 0.02),
        "rg_conv_w": nrm(10, (N_EVEN, CONV_WIDTH, RNN_WIDTH), CONV_WIDTH ** -0.5),
        "rg_conv_b": nrm(11, (N_EVEN, RNN_WIDTH), 0.02),
        "rg_wa": nrm(12, (N_EVEN, 2, RNN_BLOCKS, RNN_BLOCK_DIM, RNN_BLOCK_DIM), RNN_BLOCK_DIM ** -0.5),
        "rg_ba": nrm(13, (N_EVEN, 2, RNN_BLOCKS, RNN_BLOCK_DIM), 0.02),
        "rg_wx": nrm(14, (N_EVEN, 2, RNN_BLOCKS, RNN_BLOCK_DIM, RNN_BLOCK_DIM), RNN_BLOCK_DIM ** -0.5),
        "rg_bx": nrm(15, (N_EVEN, 2, RNN_BLOCKS, RNN_BLOCK_DIM), 0.02),
        "rg_lambda": jnp.log(a_pow) - jnp.log1p(-a_pow),
        "od_w_in": nrm(17, (N_ODD, D, 2 * S5_WIDTH), D ** -0.5),
        "s5_lambda_re": -0.5 + nrm(18, (N_ODD, 2, S5_GROUPS, S5_STATE), 0.01),
        "s5_lambda_im": math.pi * n_idx + nrm(19, (N_ODD, 2, S5_GROUPS, S5_STATE), 0.01),
        "s5_log_step": jax.random.uniform(k[20], (N_ODD, 2, S5_GROUPS), f32,
                                          minval=math.log(DT_MIN), maxval=math.log(DT_MAX)),
        "s5_b_re": nrm(21, (N_ODD, 2, S5_GROUPS, S5_STATE, S5_GROUP), (2 * S5_GROUP) ** -0.5),
        "s5_b_im": nrm(22, (N_ODD, 2, S5_GROUPS, S5_STATE, S5_GROUP), (2 * S5_GROUP) ** -0.5),
        "s5_c_re": nrm(23, (N_ODD, 2, S5_GROUPS, S5_GROUP, S5_STATE), S5_STATE ** -0.5),
        "s5_c_im": nrm(24, (N_ODD, 2, S5_GROUPS, S5_GROUP, S5_STATE), S5_STATE ** -0.5),
        "s5_d": nrm(25, (N_ODD, S5_WIDTH), 1.0),
        "glu_w": nrm(26, (N_ODD, S5_WIDTH, 2 * S5_WIDTH), S5_WIDTH ** -0.5),
        "glu_b": nrm(27, (N_ODD, 2 * S5_WIDTH), 0.02),
        "od_w_out": nrm(28, (N_ODD, S5_WIDTH, D), S5_WIDTH ** -0.5),
        "final_norm_w": 1.0 + nrm(29, (D,), 0.02),
    }


def reference(x, c, ctx, c_ctx, ada_w, ada_b, ev_w_in, ev_w_out, q_norm_w, k_norm_w,
              rg_conv_w, rg_conv_b, rg_wa, rg_ba, rg_wx, rg_bx, rg_lambda,
              od_w_in, s5_lambda_re, s5_lambda_im, s5_log_step, s5_b_re, s5_b_im,
              s5_c_re, s5_c_im, s5_d, glu_w, glu_b, od_w_out, final_norm_w):
    cos, sin = axial_rope_tables(x.shape[1])
    xc = ctx
    for layer in range(DEPTH):
        shift_x, scale_x, gate_x = ada_mod(c, ada_w[layer], ada_b[layer])
        shift_c, scale_c, gate_c = ada_mod(c_ctx, ada_w[layer], ada_b[layer])
        hx = rms_norm(x) * (1.0 + scale_x[:, None, :]) + shift_x[:, None, :]
        hc = rms_norm(xc) * (1.0 + scale_c) + shift_c
        j = layer // 2
        if layer % 2 == 0:
            oc, ox = even_mixer(hc, hx, ev_w_in[j], ev_w_out[j], q_norm_w[j], k_norm_w[j],
                                rg_conv_w[j], rg_conv_b[j], rg_wa[j], rg_ba[j], rg_wx[j], rg_bx[j],
                                rg_lambda[j], cos, sin)
        else:
            oc, ox = odd_mixer(hc, hx, od_w_in[j], s5_lambda_re[j], s5_lambda_im[j], s5_log_step[j],
                               s5_b_re[j], s5_b_im[j], s5_c_re[j], s5_c_im[j], s5_d[j],
                               glu_w[j], glu_b[j], od_w_out[j])
        x = x + gate_x[:, None, :] * ox
        if layer < DEPTH - 1:
            xc = xc + gate_c * oc
    return rms_norm(x, final_norm_w)
```

```python
from contextlib import ExitStack
import numpy as np
import concourse.bass as bass
import concourse.mybir as mybir
from concourse.bass_utils import run_bass_kernel_spmd

F32 = mybir.dt.float32
BF16 = mybir.dt.bfloat16
ALU = mybir.AluOpType
AF = mybir.ActivationFunctionType
AX = mybir.AxisListType


class Res:
    def __init__(self, name):
        self.name = name
        self.lw = None
        self.rd = {}


class Tile(Res):
    def __init__(self, name, t):
        super().__init__(name)
        self.t = t

    def ap(self):
        return self.t[:]


class Prog:
    COMPUTE = ("pe", "dve", "act", "pool")
    RING = {"sp": 12, "pool": 6, "act": 6}

    def __init__(self, nc):
        self.nc = nc
        self.stack = ExitStack()
        self.ops = {e: [] for e in ("pe", "dve", "act", "pool", "sp")}
        self.sem = {}
        self.cnt = {}
        for e in self.COMPUTE:
            self.sem[e] = self.stack.enter_context(nc.semaphore("s_" + e))
            self.cnt[e] = 0
        self.ring = {}
        self.ring_pos = {}
        for q, n in self.RING.items():
            names = []
            for i in range(n):
                nm = "d_%s%d" % (q, i)
                self.sem[nm] = self.stack.enter_context(nc.semaphore(nm))
                self.cnt[nm] = 0
                names.append(nm)
            self.ring[q] = names
            self.ring_pos[q] = 0
        self.seen = {e: {} for e in self.ops}
        self.n_ops = 0

    def sbuf(self, name, shape, dtype):
        self.n_alloc = getattr(self, "n_alloc", 0) + 1
        name = "%s_%d" % (name, self.n_alloc)
        t = self.stack.enter_context(self.nc.sbuf_tensor(name, list(shape), dtype))
        return Tile(name, t)

    def psum(self, name, shape, dtype):
        t = self.stack.enter_context(self.nc.psum_tensor(name, list(shape), dtype))
        return Tile(name, t)

    def _deps(self, eng, tok_sem, reads, writes, same_eng_all=False):
        need = {}

        def add(tok):
            if tok is None:
                return
            s, v = tok
            if need.get(s, 0) < v:
                need[s] = v

        for r in reads:
            add(r.lw)
        for w in writes:
            add(w.lw)
            for s, v in w.rd.items():
                add((s, v))
        waits = []
        for s, v in need.items():
            if s == tok_sem and not same_eng_all and eng == "pe":
                continue
            if self.seen[eng].get(s, 0) >= v:
                continue
            self.seen[eng][s] = v
            waits.append((s, v))
        return waits

    def _commit(self, tok, reads, writes):
        for w in writes:
            w.lw = tok
            w.rd = {}
        for r in reads:
            if r in writes:
                continue
            if r.rd.get(tok[0], 0) < tok[1]:
                r.rd[tok[0]] = tok[1]

    def op(self, eng, fn, reads=(), writes=()):
        reads = list(reads)
        writes = list(writes)
        waits = self._deps(eng, eng, reads, writes)
        self.cnt[eng] += 1
        tok = (eng, self.cnt[eng])
        self._commit(tok, reads, writes)
        self.ops[eng].append((fn, waits, (eng, 1)))
        self.n_ops += 1
        return tok

    def pe(self, fn, reads=(), writes=()):
        return self.op("pe", fn, reads, writes)

    def dve(self, fn, reads=(), writes=()):
        return self.op("dve", fn, reads, writes)

    def act(self, fn, reads=(), writes=()):
        return self.op("act", fn, reads, writes)

    def pool(self, fn, reads=(), writes=()):
        return self.op("pool", fn, reads, writes)

    def dma(self, out, in_, reads=(), writes=(), q="sp", **kw):
        reads = list(reads)
        writes = list(writes)
        ring = self.ring[q]
        slot = ring[self.ring_pos[q] % len(ring)]
        self.ring_pos[q] += 1
        waits = self._deps(q, slot, reads, writes, same_eng_all=True)
        prev = self.cnt[slot]
        if prev > 0 and self.seen[q].get(slot, 0) < prev:
            self.seen[q][slot] = prev
            waits.append((slot, prev))
        self.cnt[slot] += 16
        tok = (slot, self.cnt[slot])
        self._commit(tok, reads, writes)
        self.ops[q].append((lambda e: e.dma_start(out=out, in_=in_, **kw), waits, (slot, 16)))
        self.n_ops += 1
        return tok

    def all_tokens(self):
        return [(s, v) for s, v in self.cnt.items() if v > 0]

    def flush(self):
        nc = self.nc
        final_waits = self.all_tokens()
        ops, sem = self.ops, self.sem

        def replay(name, eng):
            for fn, waits, inc in ops[name]:
                for s, v in waits:
                    eng.wait_ge(sem[s], v)
                ins = fn(eng)
                ins.then_inc(sem[inc[0]], inc[1])
            for s, v in final_waits:
                eng.wait_ge(sem[s], v)
                self.seen[name][s] = v

        with nc.Block() as block:
            @block.tensor
            def _(e):
                replay("pe", e)

            @block.vector
            def _(e):
                replay("dve", e)

            @block.scalar
            def _(e):
                replay("act", e)

            @block.gpsimd
            def _(e):
                replay("pool", e)

            @block.sync
            def _(e):
                replay("sp", e)
        self.ops = {e: [] for e in self.ops}

    def finish(self):
        self.flush()
        self.stack.close()


class Rot:
    def __init__(self, tiles):
        self.tiles = tiles
        self.i = 0

    def next(self):
        t = self.tiles[self.i % len(self.tiles)]
        self.i += 1
        return t


D = 1024
CTX = 256
EPS = 1e-6


class G_:
    pass


def MM(P, ot, oap, lt, lap, rt, rap, start, stop):
    P.pe(lambda e: e.matmul(oap, lhsT=lap, rhs=rap, start=start, stop=stop), reads=[lt, rt], writes=[ot])


def TR(P, ot, oap, it, iap, ident):
    P.pe(lambda e: e.transpose(oap, iap, ident.ap()[0:iap.shape[0], 0:iap.shape[0]]), reads=[it, ident], writes=[ot])


def ACT(P, ot, oap, it, iap, func, bias=None, scale=None, accum=None, extra_r=(), extra_w=()):
    kw = {}
    if bias is not None:
        kw["bias"] = bias
    if scale is not None:
        kw["scale"] = scale
    if accum is not None:
        kw["accum_out"] = accum
    P.act(lambda e: e.activation(out=oap, in_=iap, func=func, **kw), reads=[it] + list(extra_r),
          writes=[ot] + list(extra_w))


def TT(P, eng, ot, oap, at, aap, bt, bap, op):
    P.op(eng, lambda e: e.tensor_tensor(out=oap, in0=aap, in1=bap, op=op), reads=[at, bt], writes=[ot])


def TS(P, eng, ot, oap, it, iap, s1, s2, op0, op1=None, extra_r=()):
    if op1 is None:
        P.op(eng, lambda e: e.tensor_scalar(out=oap, in0=iap, scalar1=s1, scalar2=None, op0=op0),
             reads=[it] + list(extra_r), writes=[ot])
    else:
        P.op(eng, lambda e: e.tensor_scalar(out=oap, in0=iap, scalar1=s1, scalar2=s2, op0=op0, op1=op1),
             reads=[it] + list(extra_r), writes=[ot])


def STT(P, eng, ot, oap, at, aap, scal, bt, bap, op0, op1, extra_r=()):
    P.op(eng, lambda e: e.scalar_tensor_tensor(out=oap, in0=aap, scalar=scal, in1=bap, op0=op0, op1=op1),
         reads=[at, bt] + list(extra_r), writes=[ot])


def CP(P, eng, ot, oap, it, iap):
    if eng == "act":
        P.act(lambda e: e.copy(out=oap, in_=iap), reads=[it], writes=[ot])
    else:
        P.op(eng, lambda e: e.tensor_copy(out=oap, in_=iap), reads=[it], writes=[ot])


def RSQRT(P, t, ap, mul, add):
    TS(P, "dve", t, ap, t, ap, mul, add, ALU.mult, ALU.add)
    ACT(P, t, ap, t, ap, AF.Sqrt)
    P.dve(lambda e: e.reciprocal(out=ap, in_=ap), reads=[t], writes=[t])


def phase(P):
    from contextlib import contextmanager

    @contextmanager
    def cm():
        outer = P.stack
        P.stack = ExitStack()
        try:
            yield
            P.flush()
        finally:
            P.stack.close()
            P.stack = outer
    return cm()


def run_window(gens, width):
    gens = list(gens)
    active = []
    while gens or active:
        while gens and len(active) < width:
            active.append(gens.pop(0))
        for g_ in list(active):
            try:
                next(g_)
            except StopIteration:
                active.remove(g_)


def rots(P, name, shape, dtype, n):
    return Rot([P.sbuf("%s%d" % (name, i), shape, dtype) for i in range(n)])


def load_mod(P, G):
    G.modx = P.sbuf("modx", [128, 3 * D], F32)
    G.modc = P.sbuf("modc", [128, 3 * D], F32)
    P.dma(G.modx.ap(), G.MOD[0].partition_broadcast(128), reads=[G.MOD_r], writes=[G.modx])
    P.dma(G.modc.ap(), G.MOD[1].partition_broadcast(128), reads=[G.MOD_r], writes=[G.modc])


def ada_phase(P, G, L):
    with phase(P):
        G.modx = P.sbuf("modx", [128, 3 * D], F32)
        G.modc = P.sbuf("modc", [128, 3 * D], F32)
        ccol = P.sbuf("ccol", [128, 16], F32)
        P.dma(ccol.t[:, 0:8], G.c.rearrange("(k p) -> p k", p=128), writes=[ccol], allow_slow_non_contiguous=True)
        P.dma(ccol.t[:, 8:16], G.c_ctx.rearrange("(k p) -> p k", p=128), writes=[ccol],
              allow_slow_non_contiguous=True)
        sil = P.sbuf("sil", [128, 16], F32)
        ACT(P, sil, sil.ap(), ccol, ccol.ap(), AF.Silu)
        ones = P.sbuf("ones", [128, 128], F32)
        P.dve(lambda e: e.memset(ones.ap(), 1.0), writes=[ones])
        sbc = P.sbuf("sbc", [128, 16, 128], F32)
        for k in range(16):
            TS(P, "dve", sbc, sbc.t[:, k, :], ones, ones.ap(), sil.t[:, k:k + 1], None, ALU.mult, extra_r=[sil])
        bias = P.sbuf("adab", [128, 3 * D], F32)
        P.dma(bias.ap(), G.ada_b[L].partition_broadcast(128), writes=[bias])
        wrot = rots(P, "adaw", [128, 8, 512], F32, 2)
        for n in range(6):
            w = wrot.next()
            P.dma(w.ap(), G.ada_w[L][:, n * 512:(n + 1) * 512].rearrange("(k p) n -> p k n", p=128), writes=[w])
            for which, mod in ((0, G.modx), (1, G.modc)):
                ps = G.ps.next()
                for k in range(8):
                    MM(P, ps, ps.t[:, :], sbc, sbc.t[:, which * 8 + k, :], w, w.t[:, k, :], k == 0, k == 7)
                TT(P, "dve", mod, mod.t[:, n * 512:(n + 1) * 512], ps, ps.t[:, :], bias,
                   bias.t[:, n * 512:(n + 1) * 512], ALU.add)
        for mi, mod in enumerate((G.modx, G.modc)):
            TS(P, "dve", mod, mod.t[:, D:2 * D], mod, mod.t[:, D:2 * D], 1.0, None, ALU.add)
            P.dma(G.MOD[mi:mi + 1, :], mod.t[0:1, :], reads=[mod], writes=[G.MOD_r], q="pool")


def norm_block(P, G, W, blk, hT):
    t0, ntok = blk
    nt = ntok // 128

    def stage1(tl):
        tt = t0 // 128 + tl
        xt = W.xrot.next()
        sap, sres = G.xsrc(tt)
        P.dma(xt.ap(), sap, reads=[sres], writes=[xt])
        ss = W.ssrot.next()
        ACT(P, W.junk, W.junk.ap(), xt, xt.ap(), AF.Square, accum=ss.t[:, 0:1], extra_w=[ss])
        RSQRT(P, ss, ss.t[:, 0:1], 1.0 / D, EPS)
        return xt, ss

    def stage2(tl, xt, ss):
        tt = t0 // 128 + tl
        mod = G.modc if tt < 2 else G.modx
        tmp = W.tmprot.next()
        STT(P, "dve", tmp, tmp.ap(), xt, xt.ap(), ss.t[:, 0:1], mod, mod.t[:, D:2 * D], ALU.mult, ALU.mult,
            extra_r=[ss])
        h = W.hrot.next()
        TT(P, "dve", h, h.ap(), tmp, tmp.ap(), mod, mod.t[:, 0:D], ALU.add)
        tp = G.ps.next()
        tpb = tp.t[:].bitcast(BF16)
        for k in range(8):
            TR(P, tp, tpb[:, k * 128:(k + 1) * 128], h, h.t[:, k * 128:(k + 1) * 128], G.identb)
        CP(P, "act", hT, hT.t[:, :, tl * 128:(tl + 1) * 128], tp, tpb.rearrange("p (k t) -> p k t", k=8))

    pend = [stage1(0)]
    for tl in range(nt):
        if tl + 1 < nt:
            pend.append(stage1(tl + 1))
        xt, ss = pend.pop(0)
        stage2(tl, xt, ss)


def norm_work(P):
    W = G_()
    W.xrot = rots(P, "xt", [128, D], F32, 3)
    W.ssrot = rots(P, "ss", [128, 1], F32, 4)
    W.junk = P.sbuf("junk", [128, D], F32)
    W.tmprot = rots(P, "tmp", [128, D], F32, 2)
    W.hrot = rots(P, "h", [128, D], BF16, 2)
    W.hTrot = rots(P, "hT", [128, 8, 512], BF16, 3)
    return W


def qk_post(P, G, W, ps, psap, nh, gain, cs, ob):
    n = nh * 64
    sq = W.sq
    ACT(P, sq, sq.t[:, 0:n], ps, psap, AF.Square)
    ssh = W.sshrot.next()
    P.dve(lambda e: e.tensor_reduce(out=ssh.t[:, 0:nh], in_=sq.t[:, 0:n].rearrange("p (h d) -> p h d", h=nh),
                                    axis=AX.X, op=ALU.add), reads=[sq], writes=[ssh])
    RSQRT(P, ssh, ssh.t[:, 0:nh], 1.0 / 64, EPS)
    qg = W.qgrot.next()
    v3 = lambda t, ap: ap.rearrange("p (h d) -> p h d", h=nh)
    TT(P, "dve", qg, v3(qg, qg.t[:, 0:n]), ps, v3(ps, psap), gain,
       gain.t[:, 0:64].unsqueeze(1).to_broadcast([128, nh, 64]), ALU.mult)
    src = qg
    if cs is not None:
        ro = W.rorot.next()
        q4 = qg.t[:, 0:n].rearrange("p (h i two) -> p h i two", h=nh, two=2)
        r4 = ro.t[:, 0:n].rearrange("p (h i two) -> p h i two", h=nh, two=2)
        x0, x1 = q4[:, :, :, 0], q4[:, :, :, 1]
        cosb = cs.t[:, 0:32].unsqueeze(1).to_broadcast([128, nh, 32])
        sinb = cs.t[:, 32:64].unsqueeze(1).to_broadcast([128, nh, 32])
        ta, tb = W.ropet.next(), W.ropet.next()
        a3 = lambda t: t.t[:, 0:nh * 32].rearrange("p (h i) -> p h i", h=nh)
        TT(P, "dve", ta, a3(ta), qg, x0, cs, cosb, ALU.mult)
        TT(P, "dve", tb, a3(tb), qg, x1, cs, sinb, ALU.mult)
        TT(P, "dve", ro, r4[:, :, :, 0], ta, a3(ta), tb, a3(tb), ALU.subtract)
        tc_, td = W.ropet.next(), W.ropet.next()
        TT(P, "dve", tc_, a3(tc_), qg, x0, cs, sinb, ALU.mult)
        TT(P, "dve", td, a3(td), qg, x1, cs, cosb, ALU.mult)
        TT(P, "dve", ro, r4[:, :, :, 1], tc_, a3(tc_), td, a3(td), ALU.add)
        src = ro
    TT(P, "dve", ob, v3(ob, ob.t[:, 0:n]), src, v3(src, src.t[:, 0:n]), ssh,
       ssh.t[:, 0:nh].unsqueeze(2).to_broadcast([128, nh, 64]), ALU.mult)


def even_phase1(P, G, L, j):
    with phase(P):
        load_mod(P, G)
        W = norm_work(P)
        wbf = P.sbuf("winbf", [128, 8, 2304], BF16)
        stg = rots(P, "wstg", [128, 2304], F32, 2)
        for k in range(8):
            s = stg.next()
            P.dma(s.ap(), G.ev_w_in[j][k * 128:(k + 1) * 128, :], writes=[s])
            CP(P, "act", wbf, wbf.t[:, k, 0:512].rearrange("q (p r d) -> q p r d", p=4, r=2),
               s, s.t[:, 0:512].rearrange("q (r p d) -> q p r d", r=2, p=4))
            CP(P, "act", wbf, wbf.t[:, k, 512:2304], s, s.t[:, 512:2304])
        gq = P.sbuf("gq", [128, 64], F32)
        gk = P.sbuf("gk", [128, 64], F32)
        P.dma(gq.ap(), G.q_norm_w[j].partition_broadcast(128), writes=[gq])
        P.dma(gk.ap(), G.k_norm_w[j].partition_broadcast(128), writes=[gk])
        W.sq = P.sbuf("sq", [128, 512], F32)
        W.sshrot = rots(P, "ssh", [128, 8], F32, 4)
        W.qgrot = rots(P, "qg", [128, 512], F32, 2)
        W.rorot = rots(P, "ro", [128, 512], F32, 2)
        W.ropet = rots(P, "ropet", [128, 256], F32, 8)
        csrot = rots(P, "cs", [128, 64], F32, 2)
        gorot = rots(P, "go", [128, 512], BF16, 3)
        uorot = rots(P, "uo", [128, 512], F32, 2)
        varot = rots(P, "va", [128, 2, 192], BF16, 2)
        for va in varot.tiles:
            P.dve(lambda e, va=va: e.memset(va.ap(), 1.0), writes=[va])
        qbrot = rots(P, "qb", [128, 512], BF16, 2)
        kbrot = rots(P, "kb", [128, 128], BF16, 2)
        qsrot = rots(P, "qs", [128, 640], BF16, 2)
        qkrot = Rot(G.banks[0:4])
        ps_save = G.ps
        G.ps = Rot(G.banks[4:8])

        def fm(bi, blk, hT, f):
            t0, ntok = blk
            col = 768 + 128 * f
            ps = G.ps.next()
            for k in range(8):
                MM(P, ps, ps.t[:, 0:ntok], wbf, wbf.t[:, k, col:col + 128], hT, hT.t[:, k, 0:ntok], k == 0, k == 7)
            if f < 4 or f >= 8:
                gi = f if f < 4 else f - 4
                o = gorot.next()
                ACT(P, o, o.t[:, 0:ntok], ps, ps.t[:, 0:ntok], AF.Silu)
                P.dma(G.GT[gi, :, t0:t0 + ntok], o.t[:, 0:ntok], reads=[o], writes=[G.GT_r[gi][bi]], q="pool")
            else:
                ui = f - 4
                o = uorot.next()
                CP(P, "act", o, o.t[:, 0:ntok], ps, ps.t[:, 0:ntok])
                P.dma(G.UT[ui, :, t0:t0 + ntok], o.t[:, 0:ntok], reads=[o], writes=[G.UT_r[ui][bi]], q="pool")

        def stageM(blk, hT, tl):
            t0, ntok = blk
            tt = t0 // 128 + tl
            psq = qkrot.next()
            pskv = qkrot.next()
            for k in range(8):
                MM(P, psq, psq.t[:, 0:512], hT, hT.t[:, k, tl * 128:(tl + 1) * 128], wbf, wbf.t[:, k, 0:512],
                   k == 0, k == 7)
            for k in range(8):
                MM(P, pskv, pskv.t[:, 0:256], hT, hT.t[:, k, tl * 128:(tl + 1) * 128], wbf, wbf.t[:, k, 512:768],
                   k == 0, k == 7)
            return psq, pskv

        def stageQ(blk, tl, psq, pskv):
            t0, ntok = blk
            tt = t0 // 128 + tl
            va = varot.next()
            CP(P, "act", va, va.t[:, :, 64:128], pskv, pskv.t[:, 128:256].rearrange("p (r d) -> p r d", r=2))
            P.dma(G.VS[tt], va.ap(), reads=[va], writes=[G.VS_r[tt]], q="pool")
            cs = None
            if tt >= 2:
                cs = csrot.next()
                P.dma(cs.ap(), G.rope[(tt - 2) * 128:(tt - 1) * 128, :], writes=[cs])
            qb = qbrot.next()
            kb = kbrot.next()
            qk_post(P, G, W, psq, psq.t[:, 0:512], 8, gq, cs, qb)
            qk_post(P, G, W, pskv, pskv.t[:, 0:128], 2, gk, cs, kb)
            tp = G.ps.next()
            tpb = tp.t[:].bitcast(BF16)
            for p in range(4):
                TR(P, tp, tpb[:, p * 128:(p + 1) * 128], qb, qb.t[:, p * 128:(p + 1) * 128], G.identb)
            TR(P, tp, tpb[:, 512:640], kb, kb.t[:, 0:128], G.identb)
            qs = qsrot.next()
            CP(P, "dve", qs, qs.t[:, 0:640], tp, tpb[:, 0:640])
            P.dma(G.QT[:, :, tt * 128:(tt + 1) * 128], qs.t[:, 0:512].rearrange("p (a t) -> p a t", a=4),
                  reads=[qs], writes=[G.QT_r[tt]], q="pool")
            P.dma(G.KT[:, tt * 128:(tt + 1) * 128], qs.t[:, 512:640], reads=[qs], writes=[G.KT_r[tt]], q="pool")

        NB = len(G.blocks)
        hTs = {0: W.hTrot.next()}
        norm_block(P, G, W, G.blocks[0], hTs[0])
        for bi, blk in enumerate(G.blocks):
            hT = hTs.pop(bi)
            nt = blk[1] // 128
            fl = [list(range(0, 4)), list(range(4, 8)), list(range(8, 12))]
            live = {}
            for tl in range(min(2, nt)):
                live[tl] = stageM(blk, hT, tl)
            for step in range(max(nt, 3)):
                if step < 3:
                    for f in fl[step]:
                        fm(bi, blk, hT, f)
                if step < nt:
                    stageQ(blk, step, *live.pop(step))
                    if step + 2 < nt:
                        live[step + 2] = stageM(blk, hT, step + 2)
                if step == 0 and bi + 1 < NB:
                    hTs[bi + 1] = W.hTrot.next()
                    norm_block(P, G, W, G.blocks[bi + 1], hTs[bi + 1])
        G.ps = ps_save


def even_phase2(P, G, L, j):
    with phase(P):
        T, NT = G.T, G.NT
        KT = P.sbuf("KTs", [128, T], BF16)
        P.dma(KT.ap(), G.KT, reads=G.KT_r, writes=[KT])
        VA = P.sbuf("VAs", [128, NT, 2, 192], BF16)
        P.dma(VA.ap(), G.VS.rearrange("n p r c -> p n r c"), reads=G.VS_r, writes=[VA])
        SKEW = 2
        srot = Rot(G.banks[0:4])
        accrot = Rot(G.banks[4:8])
        qrot = rots(P, "qtb", [128, 8, 512], BF16, 2)
        for qt_ in qrot.tiles:
            P.pool(lambda e, qt_=qt_: e.memset(qt_.ap(), 0.0), writes=[qt_])
        ptrot = rots(P, "pt", [128, 512], BF16, 5)
        rrot = rots(P, "rr", [128, 512], F32, 2)
        mrot = rots(P, "mm", [128, 512], F32, 2)
        grot = rots(P, "gg", [128, 512], BF16, 2)
        orot = rots(P, "oo", [128, 512], BF16, 2)
        for bi, (t0, ntok) in enumerate(G.blocks):
            keys = [0, 1] if bi == 0 else list(range(NT))
            tts = list(range(t0 // 128, (t0 + ntok) // 128))
            QTb = qrot.next()
            for r_ in range(2):
                P.dma(QTb.t[r_ * 64:(r_ + 1) * 64, r_ * 4:(r_ + 1) * 4, 0:ntok],
                      G.QT[r_ * 64:(r_ + 1) * 64, :, t0:t0 + ntok], reads=[G.QT_r[tt] for tt in tts], writes=[QTb])
            for c in range(4):
                accs = []
                for hh in range(2):
                    h = 2 * c + hh
                    p, r = h % 4, h // 4
                    acc = accrot.next()
                    vsl = slice(64, 192) if hh == 0 else slice(0, 128)

                    def pv(pt, kt, acc=acc, r=r, vsl=vsl):
                        MM(P, acc, acc.t[:, 0:ntok], VA, VA.t[:, kt, r, vsl], pt, pt.t[:, 0:ntok],
                           kt == keys[0], kt == keys[-1])
                    pend = []
                    for kt in keys:
                        sps = srot.next()
                        MM(P, sps, sps.t[:, 0:ntok], KT, KT.t[:, kt * 128:(kt + 1) * 128],
                           QTb, QTb.t[:, h, 0:ntok], True, True)
                        if len(pend) >= SKEW:
                            pv(*pend.pop(0))
                        pt = ptrot.next()
                        ACT(P, pt, pt.t[:, 0:ntok], sps, sps.t[:, 0:ntok], AF.Exp, scale=0.125)
                        pend.append((pt, kt))
                    for pp_ in pend:
                        pv(*pp_)
                    accs.append(acc)
                A, B = accs
                Rr = rrot.next()
                P.dve(lambda e, Rr=Rr, A=A: e.reciprocal(out=Rr.t[0:64, 0:ntok], in_=A.t[64:128, 0:ntok]),
                      reads=[A], writes=[Rr])
                P.dve(lambda e, Rr=Rr, B=B: e.reciprocal(out=Rr.t[64:128, 0:ntok], in_=B.t[0:64, 0:ntok]),
                      reads=[B], writes=[Rr])
                M = mrot.next()
                TT(P, "dve", M, M.t[0:64, 0:ntok], A, A.t[0:64, 0:ntok], Rr, Rr.t[0:64, 0:ntok], ALU.mult)
                TT(P, "dve", M, M.t[64:128, 0:ntok], B, B.t[64:128, 0:ntok], Rr, Rr.t[64:128, 0:ntok], ALU.mult)
                g = grot.next()
                P.dma(g.t[:, 0:ntok], G.GT[c, :, t0:t0 + ntok], reads=[G.GT_r[c][bi]], writes=[g])
                o = orot.next()
                TT(P, "pool", o, o.t[:, 0:ntok], M, M.t[:, 0:ntok], g, g.t[:, 0:ntok], ALU.mult)
                P.dma(G.MIXT[c, :, t0:t0 + ntok], o.t[:, 0:ntok], reads=[o], writes=[G.MIXT_r[c][bi]], q="pool")


def even_phase3(P, G, L, j):
    T = G.T
    segs_f = [(a, a + n) for (a, n) in G.blocks]
    segs_r = [segs_f[0]] + segs_f[:0:-1]
    with phase(P):
        u = P.sbuf("ru", [128, T], F32)
        y = P.sbuf("ry", [128, T], F32)
        yb = P.sbuf("ryb", [128, T], BF16)
        H = P.sbuf("rH", [128, T], F32)
        HB = u
        wrot = rots(P, "rw", [128, 512], F32, 2)
        wrd = [rots(P, "rw%d_" % d_, [128, 512], F32, 10) for d_ in range(2)]
        cols = rots(P, "rcol", [128, 17], F32, 2)
        wstg = rots(P, "rwst", [128, 128], F32, 2)
        wbd = rots(P, "rwbd", [128, 128], BF16, 4)
        grot = rots(P, "rg", [128, 512], BF16, 2)
        orot = rots(P, "ro_", [128, 512], BF16, 2)
        for c in range(4):
            cs_ = slice(c * 128, (c + 1) * 128)
            P.dma(u.ap(), G.UT[c], reads=G.UT_r[c], writes=[u])
            col = cols.next()
            P.dma(col.t[:, 0:4], G.rg_conv_w[j][:, cs_].rearrange("k p -> p k"), writes=[col],
                  allow_slow_non_contiguous=True)
            P.dma(col.t[:, 4:5], G.rg_conv_b[j][cs_].rearrange("(p o) -> p o", o=1), writes=[col])
            for d in range(2):
                P.dma(col.t[:, 5 + d:6 + d], G.rg_ba[j, d].rearrange("h i -> (h i)")[cs_].rearrange("(p o) -> p o", o=1),
                      writes=[col])
                P.dma(col.t[:, 7 + d:8 + d], G.rg_bx[j, d].rearrange("h i -> (h i)")[cs_].rearrange("(p o) -> p o", o=1),
                      writes=[col])
                P.dma(col.t[:, 9 + d:10 + d], G.rg_lambda[j, d][cs_].rearrange("(p o) -> p o", o=1), writes=[col])
            ACT(P, col, col.t[:, 11:13], col, col.t[:, 9:11], AF.Exp, scale=-1.0)
            ACT(P, col, col.t[:, 11:13], col, col.t[:, 11:13], AF.Ln, bias=G.onecol.t[:, 0:1], extra_r=[G.onecol])
            TS(P, "dve", col, col.t[:, 11:13], col, col.t[:, 11:13], -8.0, None, ALU.mult)
            for (a, b) in (segs_f[0], (CTX, T)):
                TS(P, "dve", y, y.t[:, a:b], u, u.t[:, a:b], col.t[:, 2:3], col.t[:, 4:5], ALU.mult, ALU.add,
                   extra_r=[col])
                STT(P, "dve", y, y.t[:, a + 2:b], u, u.t[:, a:b - 2], col.t[:, 0:1], y, y.t[:, a + 2:b], ALU.mult,
                    ALU.add, extra_r=[col])
                STT(P, "dve", y, y.t[:, a + 1:b], u, u.t[:, a:b - 1], col.t[:, 1:2], y, y.t[:, a + 1:b], ALU.mult,
                    ALU.add, extra_r=[col])
                STT(P, "dve", y, y.t[:, a:b - 1], u, u.t[:, a + 1:b], col.t[:, 3:4], y, y.t[:, a:b - 1], ALU.mult,
                    ALU.add, extra_r=[col])
            CP(P, "pool", yb, yb.ap(), y, y.ap())
            TS(P, "dve", col, col.t[:, 13:17], col, col.t[:, 5:9], -1.0, None, ALU.mult)

            def dir_gen(d):
                wts = []
                for wsrc in (G.rg_wa, G.rg_wx):
                    st = wstg.next()
                    P.pool(lambda e, st=st: e.memset(st.ap(), 0.0), writes=[st])
                    for hb in range(2):
                        P.dma(st.t[hb * 64:(hb + 1) * 64, hb * 64:(hb + 1) * 64], wsrc[j, d, 2 * c + hb], writes=[st])
                    wb = wbd.next()
                    CP(P, "pool", wb, wb.ap(), st, st.ap())
                    wts.append(wb)
                yield
                wr = wrd[d]
                for (a, b) in (segs_f if d == 0 else segs_r):
                    n = b - a
                    psr, psi = G.ps.next(), G.ps.next()
                    MM(P, psr, psr.t[:, 0:n], wts[0], wts[0].ap(), yb, yb.t[:, a:b], True, True)
                    MM(P, psi, psi.t[:, 0:n], wts[1], wts[1].ap(), yb, yb.t[:, a:b], True, True)
                    rr, ii, aa, a2, bt = [wr.next() for _ in range(5)]
                    ACT(P, rr, rr.t[:, 0:n], psr, psr.t[:, 0:n], AF.Exp, scale=-1.0, bias=col.t[:, 13 + d:14 + d],
                        extra_r=[col])
                    ACT(P, ii, ii.t[:, 0:n], psi, psi.t[:, 0:n], AF.Exp, scale=-1.0, bias=col.t[:, 15 + d:16 + d],
                        extra_r=[col])
                    for t_ in (rr, ii):
                        ACT(P, t_, t_.t[:, 0:n], t_, t_.t[:, 0:n], AF.Ln, bias=G.onecol.t[:, 0:1], extra_r=[G.onecol])
                        ACT(P, t_, t_.t[:, 0:n], t_, t_.t[:, 0:n], AF.Exp, scale=-1.0)
                    ACT(P, aa, aa.t[:, 0:n], rr, rr.t[:, 0:n], AF.Exp, scale=col.t[:, 11 + d:12 + d], extra_r=[col])
                    TT(P, "pool", a2, a2.t[:, 0:n], aa, aa.t[:, 0:n], aa, aa.t[:, 0:n], ALU.mult)
                    TS(P, "pool", a2, a2.t[:, 0:n], a2, a2.t[:, 0:n], -1.0, 1.0, ALU.mult, ALU.add)
                    ACT(P, a2, a2.t[:, 0:n], a2, a2.t[:, 0:n], AF.Ln)
                    ACT(P, a2, a2.t[:, 0:n], a2, a2.t[:, 0:n], AF.Exp, scale=0.5)
                    TT(P, "dve", bt, bt.t[:, 0:n], ii, ii.t[:, 0:n], y, y.t[:, a:b], ALU.mult)
                    TT(P, "dve", bt, bt.t[:, 0:n], bt, bt.t[:, 0:n], a2, a2.t[:, 0:n], ALU.mult)
                    if d == 0:
                        init = 0.0 if a == 0 else H.t[:, a - 1:a]
                        P.dve(lambda e, a=a, b=b, n=n, aa=aa, bt=bt, init=init: e.tensor_tensor_scan(
                            out=H.t[:, a:b], data0=aa.t[:, 0:n], data1=bt.t[:, 0:n], initial=init,
                            op0=ALU.mult, op1=ALU.add), reads=[aa, bt, H], writes=[H])
                    else:
                        if a == 0:
                            init = 0.0
                        elif b == T:
                            init = HB.t[:, 0:1]
                        else:
                            init = HB.t[:, b:b + 1]
                        P.dve(lambda e, a=a, b=b, n=n, aa=aa, bt=bt, init=init: e.tensor_tensor_scan(
                            out=HB.t[:, a:b][:, ::-1], data0=aa.t[:, 0:n][:, ::-1], data1=bt.t[:, 0:n][:, ::-1],
                            initial=init, op0=ALU.mult, op1=ALU.add), reads=[aa, bt, HB], writes=[HB])
                    yield

            run_window([dir_gen(0), dir_gen(1)], 2)
            for bi, (t0, ntok) in enumerate(G.blocks):
                sm = wrot.next()
                TT(P, "dve", sm, sm.t[:, 0:ntok], H, H.t[:, t0:t0 + ntok], HB, HB.t[:, t0:t0 + ntok], ALU.add)
                g = grot.next()
                P.dma(g.t[:, 0:ntok], G.GT[4 + c, :, t0:t0 + ntok], reads=[G.GT_r[4 + c][bi]], writes=[g])
                o = orot.next()
                TT(P, "pool", o, o.t[:, 0:ntok], sm, sm.t[:, 0:ntok], g, g.t[:, 0:ntok], ALU.mult)
                P.dma(G.MIXT[4 + c, :, t0:t0 + ntok], o.t[:, 0:ntok], reads=[o], writes=[G.MIXT_r[4 + c][bi]], q="pool")


def odd_phase1(P, G, L, j):
    with phase(P):
        load_mod(P, G)
        W = norm_work(P)
        wbf = P.sbuf("owin", [128, 8, 2048], BF16)
        stg = rots(P, "owstg", [128, 2048], F32, 2)
        for k in range(8):
            s = stg.next()
            P.dma(s.ap(), G.od_w_in[j][k * 128:(k + 1) * 128, :], writes=[s])
            CP(P, "act", wbf, wbf.ap()[:, k, :], s, s.ap())
        gorot = rots(P, "ogo", [128, 512], BF16, 3)
        uorot = rots(P, "ouo", [128, 512], F32, 3)
        def blk_gen(bi, blk):
            t0, ntok = blk
            hT = W.hTrot.next()
            norm_block(P, G, W, blk, hT)
            yield
            for f in range(16):
                if f % 4 == 0 and f > 0:
                    yield
                ps = G.ps.next()
                for k in range(8):
                    MM(P, ps, ps.t[:, 0:ntok], wbf, wbf.t[:, k, f * 128:(f + 1) * 128], hT, hT.t[:, k, 0:ntok],
                       k == 0, k == 7)
                if f < 8:
                    o = uorot.next()
                    nch = ntok // 8
                    CP(P, "act", o, o.t[:, 0:ntok].rearrange("p (i c) -> p i c", i=8), ps,
                       ps.t[:, 0:ntok].rearrange("p (c i) -> p i c", i=8))
                    P.dma(G.UO[f, :, :, t0 // 8:t0 // 8 + nch], o.t[:, 0:ntok].rearrange("p (i c) -> p i c", i=8),
                          reads=[o], writes=[G.UO_r[f][bi]], q="pool")
                else:
                    o = gorot.next()
                    ACT(P, o, o.t[:, 0:ntok], ps, ps.t[:, 0:ntok], AF.Silu)
                    P.dma(G.GT[f - 8, :, t0:t0 + ntok], o.t[:, 0:ntok], reads=[o], writes=[G.GT_r[f - 8][bi]], q="pool")

        run_window([blk_gen(bi, blk) for bi, blk in enumerate(G.blocks)], 2)


def odd_phase2(P, G, L, j):
    S = G.S
    NX = S // 8
    NCH = 32 + NX
    NS = min(512, NX)
    XB = 34
    segs_x = [(a, min(a + NS, NX)) for a in range(0, NX, NS)]
    seg_c = (0, 32, 0)
    segs_f = [seg_c] + [(32 + a, 32 + b, XB + a) for (a, b) in segs_x]
    segs_r = [seg_c] + [(32 + a, 32 + b, XB + a) for (a, b) in segs_x[::-1]]
    NLV = 13
    with phase(P):
        lre = P.sbuf("s_lre", [128, 64], F32)
        lim = P.sbuf("s_lim", [128, 64], F32)
        dtt = P.sbuf("s_dt", [128, 64], F32)
        for d in range(2):
            for h in range(2):
                dst = (slice(h * 64, (h + 1) * 64), slice(d * 32, (d + 1) * 32))
                P.dma(lre.t[dst], G.s5_lambda_re[j, d].rearrange("(q h) p -> h p q", h=2)[h], writes=[lre],
                      allow_slow_non_contiguous=True)
                P.dma(lim.t[dst], G.s5_lambda_im[j, d].rearrange("(q h) p -> h p q", h=2)[h], writes=[lim],
                      allow_slow_non_contiguous=True)
                P.dma(dtt.t[dst], G.s5_log_step[j, d].rearrange("(q h) -> h q", h=2)[h].partition_broadcast(64),
                      writes=[dtt], allow_slow_non_contiguous=True)
        ACT(P, dtt, dtt.ap(), dtt, dtt.ap(), AF.Exp)
        mag = P.sbuf("s_mag", [128, 64], F32)
        th = P.sbuf("s_th", [128, 64], F32)
        xx = P.sbuf("s_xx", [128, 64], F32)
        TT(P, "dve", xx, xx.ap(), lre, lre.ap(), dtt, dtt.ap(), ALU.mult)
        TS(P, "dve", mag, mag.ap(), xx, xx.ap(), 1.0 / 720, None, ALU.mult)
        for cf in (1.0 / 120, 1.0 / 24, 1.0 / 6, 0.5, 1.0):
            STT(P, "dve", mag, mag.ap(), mag, mag.ap(), cf, xx, xx.ap(), ALU.add, ALU.mult)
        TS(P, "dve", mag, mag.ap(), mag, mag.ap(), 1.0, None, ALU.add)
        TT(P, "dve", th, th.ap(), lim, lim.ap(), dtt, dtt.ap(), ALU.mult)
        cc = P.sbuf("s_cc", [128, 64], F32)
        sn = P.sbuf("s_sn", [128, 64], F32)
        t1 = P.sbuf("s_t1", [128, 64], F32)
        t2 = P.sbuf("s_t2", [128, 64], F32)
        ACT(P, sn, sn.ap(), th, th.ap(), AF.Sin, scale=1.0 / 64)
        ACT(P, cc, cc.ap(), th, th.ap(), AF.Sin, scale=1.0 / 64, bias=G.hpicol.t[:, 0:1], extra_r=[G.hpicol])

        def renorm(c_t, c_ap, s_t, s_ap):
            TT(P, "dve", t1, t1.ap(), c_t, c_ap, c_t, c_ap, ALU.mult)
            TT(P, "dve", t2, t2.ap(), s_t, s_ap, s_t, s_ap, ALU.mult)
            TT(P, "dve", t1, t1.ap(), t1, t1.ap(), t2, t2.ap(), ALU.add)
            TS(P, "dve", t1, t1.ap(), t1, t1.ap(), -0.5, 1.5, ALU.mult, ALU.add)
            TT(P, "dve", c_t, c_ap, c_t, c_ap, t1, t1.ap(), ALU.mult)
            TT(P, "dve", s_t, s_ap, s_t, s_ap, t1, t1.ap(), ALU.mult)

        def square_cis(c_t, c_ap, s_t, s_ap, oc_t, oc_ap, os_t, os_ap):
            TT(P, "dve", t1, t1.ap(), c_t, c_ap, c_t, c_ap, ALU.mult)
            TT(P, "dve", t2, t2.ap(), s_t, s_ap, s_t, s_ap, ALU.mult)
            STT(P, "dve", os_t, os_ap, c_t, c_ap, 2.0, s_t, s_ap, ALU.mult, ALU.mult)
            TT(P, "dve", oc_t, oc_ap, t1, t1.ap(), t2, t2.ap(), ALU.subtract)
            renorm(oc_t, oc_ap, os_t, os_ap)
        c2 = P.sbuf("s_c2", [128, 64], F32)
        s2 = P.sbuf("s_s2", [128, 64], F32)
        cur = (cc, sn)
        nxt = (c2, s2)
        for _ in range(6):
            square_cis(cur[0], cur[0].ap(), cur[1], cur[1].ap(), nxt[0], nxt[0].ap(), nxt[1], nxt[1].ap())
            cur, nxt = nxt, cur
        cth, sth = cur
        pwc = P.sbuf("s_pwc", [128, NLV, 64], F32)
        pws = P.sbuf("s_pws", [128, NLV, 64], F32)
        npws = P.sbuf("s_npws", [128, NLV, 64], F32)
        CP(P, "dve", pwc, pwc.t[:, 0, :], cth, cth.ap())
        CP(P, "dve", pws, pws.t[:, 0, :], sth, sth.ap())
        for k in range(NLV - 1):
            square_cis(pwc, pwc.t[:, k, :], pws, pws.t[:, k, :], pwc, pwc.t[:, k + 1, :], pws, pws.t[:, k + 1, :])
        TS(P, "dve", npws, npws.ap(), pws, pws.ap(), -1.0, None, ALU.mult)
        PR = P.sbuf("s_PR", [128, 9, 64], F32)
        PI = P.sbuf("s_PI", [128, 9, 64], F32)
        NPR = P.sbuf("s_NPR", [128, 9, 64], F32)
        NPI = P.sbuf("s_NPI", [128, 9, 64], F32)
        are = P.sbuf("s_are", [128, 64], F32)
        aim = P.sbuf("s_aim", [128, 64], F32)
        TT(P, "dve", are, are.ap(), mag, mag.ap(), cth, cth.ap(), ALU.mult)
        TT(P, "dve", aim, aim.ap(), mag, mag.ap(), sth, sth.ap(), ALU.mult)
        P.dve(lambda e: e.memset(PR.t[:, 0, :], 1.0), writes=[PR])
        P.dve(lambda e: e.memset(PI.t[:, 0, :], 0.0), writes=[PI])
        for tau in range(8):
            TT(P, "dve", t1, t1.ap(), PR, PR.t[:, tau, :], are, are.ap(), ALU.mult)
            TT(P, "dve", t2, t2.ap(), PI, PI.t[:, tau, :], aim, aim.ap(), ALU.mult)
            TT(P, "dve", PR, PR.t[:, tau + 1, :], t1, t1.ap(), t2, t2.ap(), ALU.subtract)
            TT(P, "dve", t1, t1.ap(), PR, PR.t[:, tau, :], aim, aim.ap(), ALU.mult)
            TT(P, "dve", t2, t2.ap(), PI, PI.t[:, tau, :], are, are.ap(), ALU.mult)
            TT(P, "dve", PI, PI.t[:, tau + 1, :], t1, t1.ap(), t2, t2.ap(), ALU.add)
        TS(P, "dve", NPR, NPR.ap(), PR, PR.ap(), -1.0, None, ALU.mult)
        TS(P, "dve", NPI, NPI.ap(), PI, PI.ap(), -1.0, None, ALU.mult)
        mag8 = P.sbuf("s_mag8", [128, 64], F32)
        TT(P, "dve", mag8, mag8.ap(), mag, mag.ap(), mag, mag.ap(), ALU.mult)
        TT(P, "dve", mag8, mag8.ap(), mag8, mag8.ap(), mag8, mag8.ap(), ALU.mult)
        TT(P, "dve", mag8, mag8.ap(), mag8, mag8.ap(), mag8, mag8.ap(), ALU.mult)
        nr = P.sbuf("s_nr", [128, 64], F32)
        TS(P, "dve", nr, nr.ap(), are, are.ap(), -1.0, None, ALU.add)
        den = P.sbuf("s_den", [128, 64], F32)
        TT(P, "dve", den, den.ap(), lre, lre.ap(), lre, lre.ap(), ALU.mult)
        TT(P, "dve", t1, t1.ap(), lim, lim.ap(), lim, lim.ap(), ALU.mult)
        TT(P, "dve", den, den.ap(), den, den.ap(), t1, t1.ap(), ALU.add)
        P.dve(lambda e: e.reciprocal(out=den.ap(), in_=den.ap()), reads=[den], writes=[den])
        fre = P.sbuf("s_fre", [128, 64], F32)
        fim = P.sbuf("s_fim", [128, 64], F32)
        nfim = P.sbuf("s_nfim", [128, 64], F32)
        TT(P, "dve", t1, t1.ap(), nr, nr.ap(), lre, lre.ap(), ALU.mult)
        TT(P, "dve", t2, t2.ap(), aim, aim.ap(), lim, lim.ap(), ALU.mult)
        TT(P, "dve", fre, fre.ap(), t1, t1.ap(), t2, t2.ap(), ALU.add)
        TT(P, "dve", fre, fre.ap(), fre, fre.ap(), den, den.ap(), ALU.mult)
        TT(P, "dve", t1, t1.ap(), aim, aim.ap(), lre, lre.ap(), ALU.mult)
        TT(P, "dve", t2, t2.ap(), nr, nr.ap(), lim, lim.ap(), ALU.mult)
        TT(P, "dve", fim, fim.ap(), t1, t1.ap(), t2, t2.ap(), ALU.subtract)
        TT(P, "dve", fim, fim.ap(), fim, fim.ap(), den, den.ap(), ALU.mult)
        TS(P, "dve", nfim, nfim.ap(), fim, fim.ap(), -1.0, None, ALU.mult)

        ud = P.sbuf("s_ud", [128, 8, NCH], F32)
        ubd = P.sbuf("s_ubd", [128, 8, NCH], BF16)
        dcol = rots(P, "s_dcol", [128, 1], F32, 2)
        kacc = P.sbuf("s_kacc", [128, 15, 128], F32)
        kkb = P.sbuf("s_kkb", [128, 15, 128], BF16)
        ecos = rots(P, "s_ecos", [128, NS], F32, 2)
        esin = rots(P, "s_esin", [128, NS], F32, 2)
        rho = rots(P, "s_rho", [128, NS], F32, 2)
        braw = rots(P, "s_braw", [128, 2, 32], F32, 2)
        bbt = rots(P, "s_bbt", [128, 2, 32], F32, 2)
        craw = rots(P, "s_craw", [32, 2, 128], F32, 2)
        ctt = rots(P, "s_ct", [128, 2, 32], F32, 2)
        tmpA = rots(P, "s_tmpA", [128, 9, 32], F32, 8)
        baf = rots(P, "s_baf", [128, 8, 2, 32], BF16, 2)
        caf = rots(P, "s_caf", [128, 9, 2, 32], F32, 2)
        om = rots(P, "s_om", [128, 8, 2, 128], BF16, 2)
        ccp = rots(P, "s_ccp", [128, 9, 2, 128], BF16, 4)
        hsr = rots(P, "s_hs", [128, 2, NCH + 4], BF16, 4)
        wkd = [rots(P, "s_wk%d_" % d_, [128, NS], F32, 8) for d_ in range(2)]
        gl = rots(P, "s_gl", [128, 2], F32, 8)
        ygr = rots(P, "s_yg", [128, 512], BF16, 2)
        ygt = rots(P, "s_ygt", [128, 512], F32, 2)

        def lvl(n):
            k = n.bit_length() - 1
            assert (1 << k) == n
            return k + 3

        for c in range(8):
            P.dma(ud.ap(), G.UO[c], reads=G.UO_r[c], writes=[ud])
            CP(P, "act", ubd, ubd.ap(), ud, ud.ap())
            dc = dcol.next()
            P.dma(dc.ap(), G.s5_d[j][c * 128:(c + 1) * 128].rearrange("(p o) -> p o", o=1), writes=[dc])
            TS(P, "dve", ud, ud.ap(), ud, ud.ap(), dc.t[:, 0:1], None, ALU.mult, extra_r=[dc])
            P.pool(lambda e: e.memset(kacc.ap(), 0.0), writes=[kacc])
            def stream_gen(q, d, outl):
                gq_ = 8 * c + 2 * q
                pq = 4 * c + q
                rows = slice(32 * q, 32 * q + 32)
                colx = d * 32 + pq
                br = braw.next()
                P.pool(lambda e, br=br: e.memset(br.ap(), 0.0), writes=[br])
                for h in range(2):
                    P.dma(br.t[h * 64:(h + 1) * 64, 0, h * 16:(h + 1) * 16], G.s5_b_re[j, d, gq_ + h], writes=[br])
                    P.dma(br.t[h * 64:(h + 1) * 64, 1, h * 16:(h + 1) * 16], G.s5_b_im[j, d, gq_ + h], writes=[br])
                bt_ = bbt.next()
                frc, fic, nfic = fre.t[:, colx:colx + 1], fim.t[:, colx:colx + 1], nfim.t[:, colx:colx + 1]
                TS(P, "dve", bt_, bt_.t[:, 0, :], br, br.t[:, 0, :], frc, None, ALU.mult, extra_r=[fre])
                STT(P, "dve", bt_, bt_.t[:, 0, :], br, br.t[:, 1, :], nfic, bt_, bt_.t[:, 0, :], ALU.mult, ALU.add,
                    extra_r=[nfim])
                TS(P, "dve", bt_, bt_.t[:, 1, :], br, br.t[:, 1, :], frc, None, ALU.mult, extra_r=[fre])
                STT(P, "dve", bt_, bt_.t[:, 1, :], br, br.t[:, 0, :], fic, bt_, bt_.t[:, 1, :], ALU.mult, ALU.add,
                    extra_r=[fim])
                cr = craw.next()
                P.pool(lambda e, cr=cr: e.memset(cr.ap(), 0.0), writes=[cr])
                for h in range(2):
                    P.dma(cr.t[h * 16:(h + 1) * 16, 0, h * 64:(h + 1) * 64], G.s5_c_re[j, d, gq_ + h], writes=[cr])
                    P.dma(cr.t[h * 16:(h + 1) * 16, 1, h * 64:(h + 1) * 64], G.s5_c_im[j, d, gq_ + h], writes=[cr])
                tp2 = G.ps.next()
                TR(P, tp2, tp2.t[:, 0:32], cr, cr.t[:, 0, :], G.identf)
                TR(P, tp2, tp2.t[:, 32:64], cr, cr.t[:, 1, :], G.identf)
                ct = ctt.next()
                CP(P, "act", ct, ct.ap(), tp2, tp2.t[:, 0:64].rearrange("p (a m) -> p a m", a=2))
                def bc_p(tab, n):
                    return tab.t[:, 0:n, colx:colx + 1].to_broadcast([128, n, 32])

                def bc_x(t, a, n):
                    return t.t[:, a, :].unsqueeze(1).to_broadcast([128, n, 32])
                ba = baf.next()
                ca = caf.next()
                x1, x2 = tmpA.next(), tmpA.next()
                TT(P, "dve", x1, x1.t[:, 0:8, :], bt_, bc_x(bt_, 0, 8), PR, bc_p(PR, 8), ALU.mult)
                TT(P, "dve", x2, x2.t[:, 0:8, :], bt_, bc_x(bt_, 1, 8), PI, bc_p(PI, 8), ALU.mult)
                TT(P, "dve", ba, ba.t[:, :, 0, :], x1, x1.t[:, 0:8, :], x2, x2.t[:, 0:8, :], ALU.subtract)
                x3, x4 = tmpA.next(), tmpA.next()
                TT(P, "pool", x3, x3.t[:, 0:8, :], bt_, bc_x(bt_, 0, 8), PI, bc_p(PI, 8), ALU.mult)
                TT(P, "pool", x4, x4.t[:, 0:8, :], bt_, bc_x(bt_, 1, 8), PR, bc_p(PR, 8), ALU.mult)
                TT(P, "pool", ba, ba.t[:, :, 1, :], x3, x3.t[:, 0:8, :], x4, x4.t[:, 0:8, :], ALU.add)
                y1, y2 = tmpA.next(), tmpA.next()
                TT(P, "dve", y1, y1.ap(), ct, bc_x(ct, 0, 9), PR, bc_p(PR, 9), ALU.mult)
                TT(P, "dve", y2, y2.ap(), ct, bc_x(ct, 1, 9), PI, bc_p(PI, 9), ALU.mult)
                TT(P, "dve", ca, ca.t[:, :, 0, :], y1, y1.ap(), y2, y2.ap(), ALU.subtract)
                y3, y4 = tmpA.next(), tmpA.next()
                TT(P, "pool", y3, y3.ap(), ct, bc_x(ct, 0, 9), NPI, bc_p(NPI, 9), ALU.mult)
                TT(P, "pool", y4, y4.ap(), ct, bc_x(ct, 1, 9), NPR, bc_p(NPR, 9), ALU.mult)
                TT(P, "pool", ca, ca.t[:, :, 1, :], y3, y3.ap(), y4, y4.ap(), ALU.add)
                omt = om.next()
                P.act(lambda e, omt=omt: e.memzero(omt.ap()), writes=[omt])
                for half in range(2):
                    tpo = G.ps.next()
                    tpb = tpo.t[:].bitcast(BF16)
                    for tt_ in range(4):
                        tau = half * 4 + tt_
                        for a_ in range(2):
                            TR(P, tpo, tpb[0:32, (tt_ * 2 + a_) * 128:(tt_ * 2 + a_ + 1) * 128], ba, ba.t[:, tau, a_, :],
                               G.identb)
                    CP(P, "act", omt, omt.t[rows, half * 4:(half + 1) * 4, :, :], tpo,
                       tpb[0:32, 0:1024].rearrange("p (t a m) -> p t a m", t=4, a=2))
                cpt = ccp.next()
                P.act(lambda e, cpt=cpt: e.memzero(cpt.ap()), writes=[cpt])
                CP(P, "pool", cpt, cpt.t[:, :, :, rows], ca, ca.ap())
                psk = G.ps.next()
                for tau in range(8):
                    sl = slice(tau * 32, (tau + 1) * 32)
                    MM(P, psk, psk.t[0:32, sl], bt_, bt_.t[:, 0, :], ca, ca.t[:, tau, 0, :], tau == 0, False)
                for tau in range(8):
                    sl = slice(tau * 32, (tau + 1) * 32)
                    MM(P, psk, psk.t[0:32, sl], bt_, bt_.t[:, 1, :], ca, ca.t[:, tau, 1, :], False, tau == 7)
                if d == 0:
                    CP(P, "act", kacc, kacc.t[rows, 0, rows], psk, psk.t[0:32, 0:32])
                else:
                    TT(P, "dve", kacc, kacc.t[rows, 0, rows], psk, psk.t[0:32, 0:32], kacc, kacc.t[rows, 0, rows],
                       ALU.add)
                kb = 1 + 7 * d
                CP(P, "act", kacc, kacc.t[rows, kb:kb + 7, rows], psk,
                   psk.t[0:32, 32:256].rearrange("p (t m) -> p t m", t=7))
                yield
                ec, es = ecos.next(), esin.next()
                P.dve(lambda e, ec=ec: e.memset(ec.t[:, 0:1], 1.0), writes=[ec])
                P.dve(lambda e, es=es: e.memset(es.t[:, 0:1], 0.0), writes=[es])
                m = 1
                k = 3
                while m < NS:
                    ck, sk = pwc.t[:, k, colx:colx + 1], pws.t[:, k, colx:colx + 1]
                    nsk = npws.t[:, k, colx:colx + 1]
                    w1, w2 = wkd[d].next(), wkd[d].next()
                    w3 = wkd[d].next()
                    if m < 32:
                        TS(P, "dve", w1, w1.t[:, 0:m], ec, ec.t[:, 0:m], ck, None, ALU.mult, extra_r=[pwc])
                        TS(P, "dve", w2, w2.t[:, 0:m], es, es.t[:, 0:m], ck, None, ALU.mult, extra_r=[pwc])
                        TS(P, "dve", w3, w3.t[:, 0:m], ec, ec.t[:, 0:m], sk, None, ALU.mult, extra_r=[pws])
                    else:
                        ACT(P, w1, w1.t[:, 0:m], ec, ec.t[:, 0:m], AF.Copy, scale=ck, extra_r=[pwc])
                        ACT(P, w2, w2.t[:, 0:m], es, es.t[:, 0:m], AF.Copy, scale=ck, extra_r=[pwc])
                        ACT(P, w3, w3.t[:, 0:m], ec, ec.t[:, 0:m], AF.Copy, scale=sk, extra_r=[pws])
                    STT(P, "dve", ec, ec.t[:, m:2 * m], es, es.t[:, 0:m], nsk, w1, w1.t[:, 0:m], ALU.mult, ALU.add,
                        extra_r=[npws])
                    TT(P, "dve", es, es.t[:, m:2 * m], w2, w2.t[:, 0:m], w3, w3.t[:, 0:m], ALU.add)
                    m *= 2
                    k += 1
                rh = rho.next()
                ACT(P, rh, rh.ap(), G.ones512, G.ones512.t[:, 0:NS], AF.Copy, scale=mag8.t[:, colx:colx + 1],
                    extra_r=[mag8])
                yield
                hs = hsr.next()
                P.act(lambda e, hs=hs: e.memzero(hs.ap()), writes=[hs])
                prev = None
                for si, (a, b, hcol) in enumerate(segs_f if d == 0 else segs_r):
                    if si > 0:
                        yield
                    n = b - a
                    rv = (lambda ap: ap) if d == 0 else (lambda ap: ap[:, ::-1])
                    pvr, pvi = G.ps.next(), G.ps.next()
                    for ip in range(8):
                        tau = 7 - ip if d == 0 else ip
                        MM(P, pvr, pvr.t[:, 0:n], omt, omt.t[:, tau, 0, :], ubd, ubd.t[:, ip, a:b], ip == 0, ip == 7)
                    for ip in range(8):
                        tau = 7 - ip if d == 0 else ip
                        MM(P, pvi, pvi.t[:, 0:n], omt, omt.t[:, tau, 1, :], ubd, ubd.t[:, ip, a:b], ip == 0, ip == 7)
                    m1, m2, m3, m4, gre, gim = [wkd[d].next() for _ in range(6)]
                    wre, wim = m1, m3
                    TT(P, "dve", m1, m1.t[:, 0:n], pvr, rv(pvr.t[:, 0:n]), ec, ec.t[:, 0:n], ALU.mult)
                    TT(P, "dve", m2, m2.t[:, 0:n], pvi, rv(pvi.t[:, 0:n]), es, es.t[:, 0:n], ALU.mult)
                    TT(P, "pool", wre, wre.t[:, 0:n], m1, m1.t[:, 0:n], m2, m2.t[:, 0:n], ALU.add)
                    TT(P, "dve", m3, m3.t[:, 0:n], pvi, rv(pvi.t[:, 0:n]), ec, ec.t[:, 0:n], ALU.mult)
                    TT(P, "dve", m4, m4.t[:, 0:n], pvr, rv(pvr.t[:, 0:n]), es, es.t[:, 0:n], ALU.mult)
                    TT(P, "pool", wim, wim.t[:, 0:n], m3, m3.t[:, 0:n], m4, m4.t[:, 0:n], ALU.subtract)
                    if prev is None:
                        ire, iim = 0.0, 0.0
                        init_r = []
                    else:
                        pre, pim, pn = prev
                        kk = lvl(pn)
                        ck, sk = pwc.t[:, kk, colx:colx + 1], pws.t[:, kk, colx:colx + 1]
                        g_ = gl.next()
                        t_ = gl.next()
                        lr, li = pre.t[:, pn - 1:pn], pim.t[:, pn - 1:pn]
                        TS(P, "dve", t_, t_.t[:, 0:1], pre, lr, ck, None, ALU.mult, extra_r=[pwc])
                        TS(P, "dve", t_, t_.t[:, 1:2], pim, li, ck, None, ALU.mult, extra_r=[pwc])
                        TS(P, "dve", g_, g_.t[:, 0:1], pim, li, sk, None, ALU.mult, extra_r=[pws])
                        TT(P, "dve", g_, g_.t[:, 0:1], t_, t_.t[:, 0:1], g_, g_.t[:, 0:1], ALU.subtract)
                        STT(P, "dve", g_, g_.t[:, 1:2], pre, lr, sk, t_, t_.t[:, 1:2], ALU.mult, ALU.add,
                            extra_r=[pws])
                        ire, iim = g_.t[:, 0:1], g_.t[:, 1:2]
                        init_r = [g_]
                    P.dve(lambda e, gre=gre, wre=wre, rh=rh, ire=ire, n=n: e.tensor_tensor_scan(
                        out=gre.t[:, 0:n], data0=rh.t[:, 0:n], data1=wre.t[:, 0:n], initial=ire,
                        op0=ALU.mult, op1=ALU.add), reads=[rh, wre] + init_r, writes=[gre])
                    P.dve(lambda e, gim=gim, wim=wim, rh=rh, iim=iim, n=n: e.tensor_tensor_scan(
                        out=gim.t[:, 0:n], data0=rh.t[:, 0:n], data1=wim.t[:, 0:n], initial=iim,
                        op0=ALU.mult, op1=ALU.add), reads=[rh, wim] + init_r, writes=[gim])
                    prev = (gre, gim, n)
                    o1, o2, o3, o4 = m2, m4, m1, m3
                    off = hcol + (1 if d == 0 else 0)
                    TT(P, "pool", o1, o1.t[:, 0:n], gre, gre.t[:, 0:n], ec, ec.t[:, 0:n], ALU.mult)
                    TT(P, "pool", o2, o2.t[:, 0:n], gim, gim.t[:, 0:n], es, es.t[:, 0:n], ALU.mult)
                    TT(P, "dve", hs, rv(hs.t[:, 0, off:off + n]), o1, o1.t[:, 0:n], o2, o2.t[:, 0:n], ALU.subtract)
                    TT(P, "pool", o3, o3.t[:, 0:n], gre, gre.t[:, 0:n], es, es.t[:, 0:n], ALU.mult)
                    TT(P, "pool", o4, o4.t[:, 0:n], gim, gim.t[:, 0:n], ec, ec.t[:, 0:n], ALU.mult)
                    TT(P, "dve", hs, rv(hs.t[:, 1, off:off + n]), o3, o3.t[:, 0:n], o4, o4.t[:, 0:n], ALU.add)
                    if si == 0:
                        if d == 0:
                            CP(P, "dve", hs, hs.t[:, :, XB:XB + 1], hs, hs.t[:, :, 32:33])
                        else:
                            CP(P, "dve", hs, hs.t[:, :, XB + NX:XB + NX + 1], hs, hs.t[:, :, 0:1])
                outl.append((hs, cpt))

            def gelu_blocks(bis):
                for bi in bis:
                    t0, ntok = G.blocks[bi]
                    nch = ntok // 8
                    ysl = ud.t[:, :, t0 // 8:t0 // 8 + nch]
                    g1, g2 = ygt.next(), ygt.next()
                    v3 = lambda t: t.t[:, 0:ntok].rearrange("p (i c) -> p i c", i=8)
                    ACT(P, g1, v3(g1), ud, ysl, AF.Square)
                    ACT(P, g1, g1.t[:, 0:ntok], g1, g1.t[:, 0:ntok], AF.Identity, scale=0.044715,
                        bias=G.onecol.t[:, 0:1], extra_r=[G.onecol])
                    TT(P, "pool", g1, v3(g1), g1, v3(g1), ud, ysl, ALU.mult)
                    ACT(P, g2, g2.t[:, 0:ntok], g1, g1.t[:, 0:ntok], AF.Sigmoid, scale=1.5957691216057308)
                    o = ygr.next()
                    TT(P, "dve", o, o.t[:, 0:ntok].rearrange("p (c i) -> p i c", i=8), g2, v3(g2), ud, ysl, ALU.mult)
                    P.dma(G.YG[c, :, t0:t0 + ntok], o.t[:, 0:ntok], reads=[o], writes=[G.YG_r[c][bi]], q="pool")

            def stageB_gen(q, streams):
                if q == 3:
                    CP(P, "pool", kkb, kkb.ap(), kacc, kacc.ap())
                order = segs_f if q < 3 else segs_f[1:] + segs_f[:1]
                for (a, b, hcol) in order:
                    n = b - a
                    for i in range(8):
                        py = G.ps.next()
                        mms = []
                        for d in range(2):
                            hs, cpt = streams[d]
                            tau = i + 1 if d == 0 else 8 - i
                            ro = hcol + (0 if d == 0 else 1)
                            mms.append((cpt, cpt.t[:, tau, 0, :], hs, hs.t[:, 0, ro:ro + n]))
                            mms.append((cpt, cpt.t[:, tau, 1, :], hs, hs.t[:, 1, ro:ro + n]))
                        if q == 3:
                            for ip in range(8):
                                if ip == i:
                                    ki = 0
                                elif ip < i:
                                    ki = i - ip
                                else:
                                    ki = 7 + (ip - i)
                                mms.append((kkb, kkb.t[:, ki, :], ubd, ubd.t[:, ip, a:b]))
                        for mi, (lt, lap, rt, rap) in enumerate(mms):
                            MM(P, py, py.t[:, 0:n], lt, lap, rt, rap, mi == 0, mi == len(mms) - 1)
                        TT(P, "dve", ud, ud.t[:, i, a:b], py, py.t[:, 0:n], ud, ud.t[:, i, a:b], ALU.add)
                        yield
                    if q == 3:
                        gelu_blocks([bi for bi, (t0, ntok) in enumerate(G.blocks) if a * 8 <= t0 < b * 8])

            def run_rr(gens):
                gens = list(gens)
                while gens:
                    for g_ in list(gens):
                        try:
                            next(g_)
                        except StopIteration:
                            gens.remove(g_)

            pendB = None
            for q in range(4):
                o0, o1 = [], []
                gl_ = [stream_gen(q, 0, o0), stream_gen(q, 1, o1)]
                if pendB is not None:
                    gl_.append(pendB)
                run_rr(gl_)
                pendB = stageB_gen(q, [o0[0], o1[0]])
            run_rr([pendB])
            if "YD" in G.debug:
                if c == 0:
                    G.YD = G.nc.dram_tensor("YD", [8, 128, 8, NCH], F32, kind="ExternalOutput").ap()
                P.dma(G.YD[c], ud.ap(), reads=[ud])
def odd_phase3(P, G, L, j):
    with phase(P):
        wbf = P.sbuf("gluw", [128, 8, 2048], BF16)
        stg = rots(P, "gwstg", [128, 2048], F32, 2)
        for k in range(8):
            s = stg.next()
            P.dma(s.ap(), G.glu_w[j][k * 128:(k + 1) * 128, :], writes=[s])
            CP(P, "pool", wbf, wbf.t[:, k, :], s, s.ap())
        gb = P.sbuf("glub", [128, 16], F32)
        P.dma(gb.ap(), G.glu_b[j].rearrange("(f p) -> p f", p=128), writes=[gb], allow_slow_non_contiguous=True)
        yrot = rots(P, "gyg", [128, 8, 512], BF16, 2)
        srot = rots(P, "gsg", [128, 8, 512], BF16, 2)
        arot = rots(P, "ga", [128, 512], F32, 3)
        brot = rots(P, "gb", [128, 512], F32, 3)
        orot = rots(P, "gout", [128, 512], BF16, 3)
        for bi, (t0, ntok) in enumerate(G.blocks):
            yg = yrot.next()
            P.dma(yg.t[:, :, 0:ntok], G.YG[:, :, t0:t0 + ntok].rearrange("c p t -> p c t"),
                  reads=[G.YG_r[c][bi] for c in range(8)], writes=[yg])
            sg = srot.next()
            P.dma(sg.t[:, :, 0:ntok], G.GT[:, :, t0:t0 + ntok].rearrange("c p t -> p c t"),
                  reads=[G.GT_r[c][bi] for c in range(8)], writes=[sg])
            for f in range(8):
                pa, pb = G.ps.next(), G.ps.next()
                for k in range(8):
                    MM(P, pa, pa.t[:, 0:ntok], wbf, wbf.t[:, k, f * 128:(f + 1) * 128], yg, yg.t[:, k, 0:ntok],
                       k == 0, k == 7)
                for k in range(8):
                    MM(P, pb, pb.t[:, 0:ntok], wbf, wbf.t[:, k, 1024 + f * 128:1024 + (f + 1) * 128], yg,
                       yg.t[:, k, 0:ntok], k == 0, k == 7)
                a_, b_ = arot.next(), brot.next()
                ACT(P, a_, a_.t[:, 0:ntok], pa, pa.t[:, 0:ntok], AF.Identity, bias=gb.t[:, f:f + 1], extra_r=[gb])
                ACT(P, b_, b_.t[:, 0:ntok], pb, pb.t[:, 0:ntok], AF.Sigmoid, bias=gb.t[:, 8 + f:9 + f], extra_r=[gb])
                TT(P, "dve", a_, a_.t[:, 0:ntok], a_, a_.t[:, 0:ntok], b_, b_.t[:, 0:ntok], ALU.mult)
                o = orot.next()
                TT(P, "pool", o, o.t[:, 0:ntok], a_, a_.t[:, 0:ntok], sg, sg.t[:, f, 0:ntok], ALU.mult)
                P.dma(G.MIXT[f, :, t0:t0 + ntok], o.t[:, 0:ntok], reads=[o], writes=[G.MIXT_r[f][bi]], q="pool")


def out_phase(P, G, L, w_dram, last):
    with phase(P):
        load_mod(P, G)
        wo = P.sbuf("wo", [128, 8, D], BF16)
        stg = rots(P, "wostg", [128, D], F32, 2)
        for k in range(8):
            s = stg.next()
            P.dma(s.ap(), w_dram[k * 128:(k + 1) * 128, :], writes=[s])
            CP(P, "pool", wo, wo.t[:, k, :], s, s.ap())
        mrot = rots(P, "mixT", [128, 8, 512], BF16, 2)
        xrot = rots(P, "oxt", [128, D], F32, 2)
        trot = rots(P, "otmp", [128, 512], F32, 2)
        orot = rots(P, "oxo", [128, D], F32, 2)
        for bi, (t0, ntok) in enumerate(G.blocks):
            if last and bi == 0:
                continue
            mixT = mrot.next()
            P.dma(mixT.t[:, :, 0:ntok], G.MIXT[:, :, t0:t0 + ntok].rearrange("c p t -> p c t"),
                  reads=[G.MIXT_r[c][bi] for c in range(8)], writes=[mixT])
            for tl in range(ntok // 128):
                tt = t0 // 128 + tl
                mod = G.modc if tt < 2 else G.modx
                xt = xrot.next()
                sap, sres = G.xsrc(tt)
                P.dma(xt.ap(), sap, reads=[sres], writes=[xt])
                xo = orot.next()
                for nh in range(2):
                    ps = G.ps.next()
                    for k in range(8):
                        MM(P, ps, ps.t[:, :], mixT, mixT.t[:, k, tl * 128:(tl + 1) * 128], wo,
                           wo.t[:, k, nh * 512:(nh + 1) * 512], k == 0, k == 7)
                    tmp = trot.next()
                    TT(P, "dve", tmp, tmp.ap(), ps, ps.t[:, :], mod,
                       mod.t[:, 2 * D + nh * 512:2 * D + (nh + 1) * 512], ALU.mult)
                    TT(P, "pool", xo, xo.t[:, nh * 512:(nh + 1) * 512], tmp, tmp.ap(), xt,
                       xt.t[:, nh * 512:(nh + 1) * 512], ALU.add)
                P.dma(G.XS[tt * 128:(tt + 1) * 128, :], xo.ap(), reads=[xo], writes=[G.XS_r[tt]], q="pool")


def final_phase(P, G):
    with phase(P):
        fw = P.sbuf("fw", [128, D], F32)
        P.dma(fw.ap(), G.final_norm_w.partition_broadcast(128), writes=[fw])
        xrot = rots(P, "fxt", [128, D], F32, 3)
        orot = rots(P, "fxo", [128, D], F32, 3)
        junk = P.sbuf("fjunk", [128, D], F32)
        ssrot = rots(P, "fss", [128, 1], F32, 4)
        for tt in range(2, G.NT):
            xt = xrot.next()
            P.dma(xt.ap(), G.XS[tt * 128:(tt + 1) * 128, :], reads=[G.XS_r[tt]], writes=[xt])
            ss = ssrot.next()
            ACT(P, junk, junk.ap(), xt, xt.ap(), AF.Square, accum=ss.t[:, 0:1], extra_w=[ss])
            RSQRT(P, ss, ss.t[:, 0:1], 1.0 / D, EPS)
            xo = orot.next()
            STT(P, "dve", xo, xo.ap(), xt, xt.ap(), ss.t[:, 0:1], fw, fw.ap(), ALU.mult, ALU.mult, extra_r=[ss])
            P.dma(G.out[(tt - 2) * 128:(tt - 1) * 128, :], xo.ap(), reads=[xo], q="pool")


USED_INPUTS = []


def build(S, depth=4, debug=(), stop_after=None):
    nc = bass.Bass("TRN2", target_bir_lowering=False)
    T = CTX + S
    NT = T // 128
    G = G_()
    G.debug = debug
    G.nc = nc
    G.S, G.T, G.NT = S, T, NT
    G.blocks = [(0, CTX)] + [(CTX + 512 * i, 512) for i in range(S // 512)]
    NB = len(G.blocks)

    def din(name, shape, dt=F32):
        if name not in USED_INPUTS:
            USED_INPUTS.append(name)
        return nc.dram_tensor(name, list(shape), dt, kind="ExternalInput").ap()

    def dscr(name, shape, dt):
        kind = "ExternalOutput" if name in debug else "Internal"
        return nc.dram_tensor(name, list(shape), dt, kind=kind).ap()

    G.x = din("x", [S, D])
    G.c = din("c", [D])
    G.ctx = din("ctx", [CTX, D])
    G.c_ctx = din("c_ctx", [D])
    G.ada_w = din("ada_w", [4, D, 3 * D])
    G.ada_b = din("ada_b", [4, 3 * D])
    G.ev_w_in = din("ev_w_in", [2, D, 2304])
    G.ev_w_out = din("ev_w_out", [2, D, D])
    G.q_norm_w = din("q_norm_w", [2, 64])
    G.k_norm_w = din("k_norm_w", [2, 64])
    G.rg_conv_w = din("rg_conv_w", [2, 4, 512])
    G.rg_conv_b = din("rg_conv_b", [2, 512])
    G.rg_wa = din("rg_wa", [2, 2, 8, 64, 64])
    G.rg_ba = din("rg_ba", [2, 2, 8, 64])
    G.rg_wx = din("rg_wx", [2, 2, 8, 64, 64])
    G.rg_bx = din("rg_bx", [2, 2, 8, 64])
    G.rg_lambda = din("rg_lambda", [2, 2, 512])
    G.final_norm_w = din("final_norm_w", [D])
    G.od_w_in = din("od_w_in", [2, D, 2048])
    G.s5_lambda_re = din("s5_lambda_re", [2, 2, 64, 64])
    G.s5_lambda_im = din("s5_lambda_im", [2, 2, 64, 64])
    G.s5_log_step = din("s5_log_step", [2, 2, 64])
    G.s5_b_re = din("s5_b_re", [2, 2, 64, 64, 16])
    G.s5_b_im = din("s5_b_im", [2, 2, 64, 64, 16])
    G.s5_c_re = din("s5_c_re", [2, 2, 64, 16, 64])
    G.s5_c_im = din("s5_c_im", [2, 2, 64, 16, 64])
    G.s5_d = din("s5_d", [2, D])
    G.glu_w = din("glu_w", [2, D, 2048])
    G.glu_b = din("glu_b", [2, 2048])
    G.od_w_out = din("od_w_out", [2, D, D])
    G.rope = din("rope", [S, 64])
    G.ident_in = din("ident", [128, 128])
    G.out = nc.dram_tensor("out", [S, D], F32, kind="ExternalOutput").ap()

    G.XS = dscr("XS", [T, D], F32)
    G.XS_r = [Res("XS%d" % i) for i in range(NT)]
    G.QT = dscr("QT", [128, 4, T], BF16)
    G.QT_r = [Res("QT%d" % i) for i in range(NT)]
    G.KT = dscr("KT", [128, T], BF16)
    G.KT_r = [Res("KT%d" % i) for i in range(NT)]
    G.VS = dscr("VS", [NT, 128, 2, 192], BF16)
    G.VS_r = [Res("VS%d" % i) for i in range(NT)]
    G.GT = dscr("GT", [8, 128, T], BF16)
    G.GT_r = [[Res("GT%d_%d" % (g, b)) for b in range(NB)] for g in range(8)]
    G.UT = dscr("UT", [4, 128, T], F32)
    G.UT_r = [[Res("UT%d_%d" % (g, b)) for b in range(NB)] for g in range(4)]
    G.UO = dscr("UO", [8, 128, 8, T // 8], F32)
    G.UO_r = [[Res("UO%d_%d" % (g, b)) for b in range(NB)] for g in range(8)]
    G.YG = dscr("YG", [8, 128, T], BF16)
    G.YG_r = [[Res("YG%d_%d" % (g, b)) for b in range(NB)] for g in range(8)]
    G.MIXT = dscr("MIXT", [8, 128, T], BF16)
    G.MIXT_r = [[Res("MX%d_%d" % (g, b)) for b in range(NB)] for g in range(8)]
    in_res = Res("inputs")

    P = Prog(nc)
    G.banks = [P.psum("ps%d" % i, [128, 512], F32) for i in range(8)]
    G.ps = Rot(G.banks)
    G.onecol = P.sbuf("onecol", [128, 1], F32)
    P.dve(lambda e: e.memset(G.onecol.ap(), 1.0), writes=[G.onecol])
    G.hpicol = P.sbuf("hpicol", [128, 1], F32)
    P.dve(lambda e: e.memset(G.hpicol.ap(), float(np.pi / 2)), writes=[G.hpicol])
    G.ones512 = P.sbuf("ones512", [128, 512], F32)
    P.dve(lambda e: e.memset(G.ones512.ap(), 1.0), writes=[G.ones512])
    G.MOD = dscr("MOD", [2, 3 * D], F32)
    G.MOD_r = Res("MOD")
    identf = P.sbuf("identf", [128, 128], F32)
    G.identb = P.sbuf("identb", [128, 128], BF16)
    P.dma(identf.ap(), G.ident_in, writes=[identf])
    CP(P, "dve", G.identb, G.identb.ap(), identf, identf.ap())
    G.identf = identf

    layer0 = [True]

    def xsrc(tt):
        if layer0[0]:
            if tt < 2:
                return G.ctx[tt * 128:(tt + 1) * 128, :], in_res
            return G.x[(tt - 2) * 128:(tt - 1) * 128, :], in_res
        return G.XS[tt * 128:(tt + 1) * 128, :], G.XS_r[tt]
    G.xsrc = xsrc

    def run_layers():
        for L in range(depth):
            layer0[0] = (L == 0)
            j = L // 2
            last = (L == depth - 1)
            if L % 2 == 0:
                seq = [("ada", lambda: ada_phase(P, G, L)), ("e1", lambda: even_phase1(P, G, L, j)),
                       ("e2", lambda: even_phase2(P, G, L, j)), ("e3", lambda: even_phase3(P, G, L, j)),
                       ("out", lambda: out_phase(P, G, L, G.ev_w_out[j], last))]
            else:
                seq = [("ada", lambda: ada_phase(P, G, L)), ("o1", lambda: odd_phase1(P, G, L, j)),
                       ("o2", lambda: odd_phase2(P, G, L, j)), ("o3", lambda: odd_phase3(P, G, L, j)),
                       ("out", lambda: out_phase(P, G, L, G.od_w_out[j], last))]
            for name, fn in seq:
                fn()
                if stop_after == (name, L):
                    return
    run_layers()
    if stop_after is None:
        final_phase(P, G)
    P.finish()
    return nc


def rope_table(S):
    t = np.arange(S)
    row = (t // 64).astype(np.float32)
    col = (t % 64).astype(np.float32)
    inv = (10000.0 ** (-np.arange(16, dtype=np.float32) / 16)).astype(np.float32)
    ang = np.concatenate([row[:, None] * inv, col[:, None] * inv], axis=-1).astype(np.float32)
    return np.concatenate([np.cos(ang), np.sin(ang)], axis=-1).astype(np.float32)


PER_BATCH = ("x", "c", "ctx")
_NC_CACHE = {}


def make_in_map(inp, b, S):
    m = {}
    for k, v in inp.items():
        v = np.asarray(v)
        if k == "x":
            m[k] = np.ascontiguousarray(v[b, :S])
        elif k in ("c", "ctx"):
            m[k] = np.ascontiguousarray(v[b])
        else:
            m[k] = np.ascontiguousarray(v)
    m["rope"] = rope_table(S)
    m["ident"] = np.eye(128, dtype=np.float32)
    return m


def kernel(**inputs):
    S = inputs["x"].shape[1]
    B = inputs["x"].shape[0]
    if S not in _NC_CACHE:
        _NC_CACHE[S] = build(S)
    nc = _NC_CACHE[S]
    names = set(USED_INPUTS)
    in_maps = []
    for b in range(B):
        m = make_in_map(inputs, b, S)
        in_maps.append({k: v for k, v in m.items() if k in names})
    res = run_bass_kernel_spmd(nc, in_maps, core_ids=list(range(B)))
    return np.stack([np.asarray(r["out"]) for r in res.results], axis=0).astype(np.float32)
```

```python
from contextlib import ExitStack
import numpy as np
import concourse.bass as bass
import concourse.mybir as mybir
from concourse.bass_utils import run_bass_kernel_spmd

F32 = mybir.dt.float32
BF16 = mybir.dt.bfloat16
ALU = mybir.AluOpType
AF = mybir.ActivationFunctionType
AX = mybir.AxisListType


class Res:
    def __init__(self, name):
        self.name = name
        self.lw = None
        self.rd = {}


class Tile(Res):
    def __init__(self, name, t):
        super().__init__(name)
        self.t = t

    def ap(self):
        return self.t[:]


class Prog:
    COMPUTE = ("pe", "dve", "act", "pool")
    RING = {"sp": 12, "pool": 6, "act": 6}

    def __init__(self, nc):
        self.nc = nc
        self.stack = ExitStack()
        self.ops = {e: [] for e in ("pe", "dve", "act", "pool", "sp")}
        self.sem = {}
        self.cnt = {}
        for e in self.COMPUTE:
            self.sem[e] = self.stack.enter_context(nc.semaphore("s_" + e))
            self.cnt[e] = 0
        self.ring = {}
        self.ring_pos = {}
        for q, n in self.RING.items():
            names = []
            for i in range(n):
                nm = "d_%s%d" % (q, i)
                self.sem[nm] = self.stack.enter_context(nc.semaphore(nm))
                self.cnt[nm] = 0
                names.append(nm)
            self.ring[q] = names
            self.ring_pos[q] = 0
        self.seen = {e: {} for e in self.ops}
        self.n_ops = 0

    def sbuf(self, name, shape, dtype):
        self.n_alloc = getattr(self, "n_alloc", 0) + 1
        name = "%s_%d" % (name, self.n_alloc)
        t = self.stack.enter_context(self.nc.sbuf_tensor(name, list(shape), dtype))
        return Tile(name, t)

    def psum(self, name, shape, dtype):
        t = self.stack.enter_context(self.nc.psum_tensor(name, list(shape), dtype))
        return Tile(name, t)

    def _deps(self, eng, tok_sem, reads, writes, same_eng_all=False):
        need = {}

        def add(tok):
            if tok is None:
                return
            s, v = tok
            if need.get(s, 0) < v:
                need[s] = v

        for r in reads:
            add(r.lw)
        for w in writes:
            add(w.lw)
            for s, v in w.rd.items():
                add((s, v))
        waits = []
        for s, v in need.items():
            if s == tok_sem and not same_eng_all and eng == "pe":
                continue
            if self.seen[eng].get(s, 0) >= v:
                continue
            self.seen[eng][s] = v
            waits.append((s, v))
        return waits

    def _commit(self, tok, reads, writes):
        for w in writes:
            w.lw = tok
            w.rd = {}
        for r in reads:
            if r in writes:
                continue
            if r.rd.get(tok[0], 0) < tok[1]:
                r.rd[tok[0]] = tok[1]

    def op(self, eng, fn, reads=(), writes=()):
        reads = list(reads)
        writes = list(writes)
        waits = self._deps(eng, eng, reads, writes)
        self.cnt[eng] += 1
        tok = (eng, self.cnt[eng])
        self._commit(tok, reads, writes)
        self.ops[eng].append((fn, waits, (eng, 1)))
        self.n_ops += 1
        return tok

    def pe(self, fn, reads=(), writes=()):
        return self.op("pe", fn, reads, writes)

    def dve(self, fn, reads=(), writes=()):
        return self.op("dve", fn, reads, writes)

    def act(self, fn, reads=(), writes=()):
        return self.op("act", fn, reads, writes)

    def pool(self, fn, reads=(), writes=()):
        return self.op("pool", fn, reads, writes)

    def dma(self, out, in_, reads=(), writes=(), q="sp", **kw):
        reads = list(reads)
        writes = list(writes)
        ring = self.ring[q]
        slot = ring[self.ring_pos[q] % len(ring)]
        self.ring_pos[q] += 1
        waits = self._deps(q, slot, reads, writes, same_eng_all=True)
        prev = self.cnt[slot]
        if prev > 0 and self.seen[q].get(slot, 0) < prev:
            self.seen[q][slot] = prev
            waits.append((slot, prev))
        self.cnt[slot] += 16
        tok = (slot, self.cnt[slot])
        self._commit(tok, reads, writes)
        self.ops[q].append((lambda e: e.dma_start(out=out, in_=in_, **kw), waits, (slot, 16)))
        self.n_ops += 1
        return tok

    def all_tokens(self):
        return [(s, v) for s, v in self.cnt.items() if v > 0]

    def flush(self):
        nc = self.nc
        final_waits = self.all_tokens()
        ops, sem = self.ops, self.sem

        def replay(name, eng):
            for fn, waits, inc in ops[name]:
                for s, v in waits:
                    eng.wait_ge(sem[s], v)
                ins = fn(eng)
                ins.then_inc(sem[inc[0]], inc[1])
            for s, v in final_waits:
                eng.wait_ge(sem[s], v)
                self.seen[name][s] = v

        with nc.Block() as block:
            @block.tensor
            def _(e):
                replay("pe", e)

            @block.vector
            def _(e):
                replay("dve", e)

            @block.scalar
            def _(e):
                replay("act", e)

            @block.gpsimd
            def _(e):
                replay("pool", e)

            @block.sync
            def _(e):
                replay("sp", e)
        self.ops = {e: [] for e in self.ops}

    def finish(self):
        self.flush()
        self.stack.close()


class Rot:
    def __init__(self, tiles):
        self.tiles = tiles
        self.i = 0

    def next(self):
        t = self.tiles[self.i % len(self.tiles)]
        self.i += 1
        return t


D = 1024
CTX = 256
EPS = 1e-6


class G_:
    pass


def MM(P, ot, oap, lt, lap, rt, rap, start, stop):
    P.pe(lambda e: e.matmul(oap, lhsT=lap, rhs=rap, start=start, stop=stop), reads=[lt, rt], writes=[ot])


def TR(P, ot, oap, it, iap, ident):
    P.pe(lambda e: e.transpose(oap, iap, ident.ap()[0:iap.shape[0], 0:iap.shape[0]]), reads=[it, ident], writes=[ot])


def ACT(P, ot, oap, it, iap, func, bias=None, scale=None, accum=None, extra_r=(), extra_w=()):
    kw = {}
    if bias is not None:
        kw["bias"] = bias
    if scale is not None:
        kw["scale"] = scale
    if accum is not None:
        kw["accum_out"] = accum
    P.act(lambda e: e.activation(out=oap, in_=iap, func=func, **kw), reads=[it] + list(extra_r),
          writes=[ot] + list(extra_w))


def TT(P, eng, ot, oap, at, aap, bt, bap, op):
    P.op(eng, lambda e: e.tensor_tensor(out=oap, in0=aap, in1=bap, op=op), reads=[at, bt], writes=[ot])


def TS(P, eng, ot, oap, it, iap, s1, s2, op0, op1=None, extra_r=()):
    if op1 is None:
        P.op(eng, lambda e: e.tensor_scalar(out=oap, in0=iap, scalar1=s1, scalar2=None, op0=op0),
             reads=[it] + list(extra_r), writes=[ot])
    else:
        P.op(eng, lambda e: e.tensor_scalar(out=oap, in0=iap, scalar1=s1, scalar2=s2, op0=op0, op1=op1),
             reads=[it] + list(extra_r), writes=[ot])


def STT(P, eng, ot, oap, at, aap, scal, bt, bap, op0, op1, extra_r=()):
    P.op(eng, lambda e: e.scalar_tensor_tensor(out=oap, in0=aap, scalar=scal, in1=bap, op0=op0, op1=op1),
         reads=[at, bt] + list(extra_r), writes=[ot])


def CP(P, eng, ot, oap, it, iap):
    if eng == "act":
        P.act(lambda e: e.copy(out=oap, in_=iap), reads=[it], writes=[ot])
    else:
        P.op(eng, lambda e: e.tensor_copy(out=oap, in_=iap), reads=[it], writes=[ot])


def RSQRT(P, t, ap, mul, add):
    TS(P, "dve", t, ap, t, ap, mul, add, ALU.mult, ALU.add)
    ACT(P, t, ap, t, ap, AF.Sqrt)
    P.dve(lambda e: e.reciprocal(out=ap, in_=ap), reads=[t], writes=[t])


def phase(P):
    from contextlib import contextmanager

    @contextmanager
    def cm():
        outer = P.stack
        P.stack = ExitStack()
        try:
            yield
            P.flush()
        finally:
            P.stack.close()
            P.stack = outer
    return cm()


def run_window(gens, width):
    gens = list(gens)
    active = []
    while gens or active:
        while gens and len(active) < width:
            active.append(gens.pop(0))
        for g_ in list(active):
            try:
                next(g_)
            except StopIteration:
                active.remove(g_)


def rots(P, name, shape, dtype, n):
    return Rot([P.sbuf("%s%d" % (name, i), shape, dtype) for i in range(n)])


def load_mod(P, G):
    G.modx = P.sbuf("modx", [128, 3 * D], F32)
    G.modc = P.sbuf("modc", [128, 3 * D], F32)
    P.dma(G.modx.ap(), G.MOD[0].partition_broadcast(128), reads=[G.MOD_r], writes=[G.modx])
    P.dma(G.modc.ap(), G.MOD[1].partition_broadcast(128), reads=[G.MOD_r], writes=[G.modc])


def ada_phase(P, G, L):
    with phase(P):
        G.modx = P.sbuf("modx", [128, 3 * D], F32)
        G.modc = P.sbuf("modc", [128, 3 * D], F32)
        ccol = P.sbuf("ccol", [128, 16], F32)
        P.dma(ccol.t[:, 0:8], G.c.rearrange("(k p) -> p k", p=128), writes=[ccol], allow_slow_non_contiguous=True)
        P.dma(ccol.t[:, 8:16], G.c_ctx.rearrange("(k p) -> p k", p=128), writes=[ccol],
              allow_slow_non_contiguous=True)
        sil = P.sbuf("sil", [128, 16], F32)
        ACT(P, sil, sil.ap(), ccol, ccol.ap(), AF.Silu)
        ones = P.sbuf("ones", [128, 128], F32)
        P.dve(lambda e: e.memset(ones.ap(), 1.0), writes=[ones])
        sbc = P.sbuf("sbc", [128, 16, 128], F32)
        for k in range(16):
            TS(P, "dve", sbc, sbc.t[:, k, :], ones, ones.ap(), sil.t[:, k:k + 1], None, ALU.mult, extra_r=[sil])
        bias = P.sbuf("adab", [128, 3 * D], F32)
        P.dma(bias.ap(), G.ada_b[L].partition_broadcast(128), writes=[bias])
        wrot = rots(P, "adaw", [128, 8, 512], F32, 2)
        for n in range(6):
            w = wrot.next()
            P.dma(w.ap(), G.ada_w[L][:, n * 512:(n + 1) * 512].rearrange("(k p) n -> p k n", p=128), writes=[w])
            for which, mod in ((0, G.modx), (1, G.modc)):
                ps = G.ps.next()
                for k in range(8):
                    MM(P, ps, ps.t[:, :], sbc, sbc.t[:, which * 8 + k, :], w, w.t[:, k, :], k == 0, k == 7)
                TT(P, "dve", mod, mod.t[:, n * 512:(n + 1) * 512], ps, ps.t[:, :], bias,
                   bias.t[:, n * 512:(n + 1) * 512], ALU.add)
        for mi, mod in enumerate((G.modx, G.modc)):
            TS(P, "dve", mod, mod.t[:, D:2 * D], mod, mod.t[:, D:2 * D], 1.0, None, ALU.add)
            P.dma(G.MOD[mi:mi + 1, :], mod.t[0:1, :], reads=[mod], writes=[G.MOD_r], q="pool")


def norm_block(P, G, W, blk, hT):
    t0, ntok = blk
    nt = ntok // 128

    def stage1(tl):
        tt = t0 // 128 + tl
        xt = W.xrot.next()
        sap, sres = G.xsrc(tt)
        P.dma(xt.ap(), sap, reads=[sres], writes=[xt])
        ss = W.ssrot.next()
        ACT(P, W.junk, W.junk.ap(), xt, xt.ap(), AF.Square, accum=ss.t[:, 0:1], extra_w=[ss])
        RSQRT(P, ss, ss.t[:, 0:1], 1.0 / D, EPS)
        return xt, ss

    def stage2(tl, xt, ss):
        tt = t0 // 128 + tl
        mod = G.modc if tt < 2 else G.modx
        tmp = W.tmprot.next()
        STT(P, "dve", tmp, tmp.ap(), xt, xt.ap(), ss.t[:, 0:1], mod, mod.t[:, D:2 * D], ALU.mult, ALU.mult,
            extra_r=[ss])
        h = W.hrot.next()
        TT(P, "dve", h, h.ap(), tmp, tmp.ap(), mod, mod.t[:, 0:D], ALU.add)
        tp = G.ps.next()
        tpb = tp.t[:].bitcast(BF16)
        for k in range(8):
            TR(P, tp, tpb[:, k * 128:(k + 1) * 128], h, h.t[:, k * 128:(k + 1) * 128], G.identb)
        CP(P, "act", hT, hT.t[:, :, tl * 128:(tl + 1) * 128], tp, tpb.rearrange("p (k t) -> p k t", k=8))

    pend = [stage1(0)]
    for tl in range(nt):
        if tl + 1 < nt:
            pend.append(stage1(tl + 1))
        xt, ss = pend.pop(0)
        stage2(tl, xt, ss)


def norm_work(P):
    W = G_()
    W.xrot = rots(P, "xt", [128, D], F32, 3)
    W.ssrot = rots(P, "ss", [128, 1], F32, 4)
    W.junk = P.sbuf("junk", [128, D], F32)
    W.tmprot = rots(P, "tmp", [128, D], F32, 2)
    W.hrot = rots(P, "h", [128, D], BF16, 2)
    W.hTrot = rots(P, "hT", [128, 8, 512], BF16, 3)
    return W


def qk_post(P, G, W, ps, psap, nh, gain, cs, ob):
    n = nh * 64
    sq = W.sq
    ACT(P, sq, sq.t[:, 0:n], ps, psap, AF.Square)
    ssh = W.sshrot.next()
    P.dve(lambda e: e.tensor_reduce(out=ssh.t[:, 0:nh], in_=sq.t[:, 0:n].rearrange("p (h d) -> p h d", h=nh),
                                    axis=AX.X, op=ALU.add), reads=[sq], writes=[ssh])
    RSQRT(P, ssh, ssh.t[:, 0:nh], 1.0 / 64, EPS)
    qg = W.qgrot.next()
    v3 = lambda t, ap: ap.rearrange("p (h d) -> p h d", h=nh)
    TT(P, "dve", qg, v3(qg, qg.t[:, 0:n]), ps, v3(ps, psap), gain,
       gain.t[:, 0:64].unsqueeze(1).to_broadcast([128, nh, 64]), ALU.mult)
    src = qg
    if cs is not None:
        ro = W.rorot.next()
        q4 = qg.t[:, 0:n].rearrange("p (h i two) -> p h i two", h=nh, two=2)
        r4 = ro.t[:, 0:n].rearrange("p (h i two) -> p h i two", h=nh, two=2)
        x0, x1 = q4[:, :, :, 0], q4[:, :, :, 1]
        cosb = cs.t[:, 0:32].unsqueeze(1).to_broadcast([128, nh, 32])
        sinb = cs.t[:, 32:64].unsqueeze(1).to_broadcast([128, nh, 32])
        ta, tb = W.ropet.next(), W.ropet.next()
        a3 = lambda t: t.t[:, 0:nh * 32].rearrange("p (h i) -> p h i", h=nh)
        TT(P, "dve", ta, a3(ta), qg, x0, cs, cosb, ALU.mult)
        TT(P, "dve", tb, a3(tb), qg, x1, cs, sinb, ALU.mult)
        TT(P, "dve", ro, r4[:, :, :, 0], ta, a3(ta), tb, a3(tb), ALU.subtract)
        tc_, td = W.ropet.next(), W.ropet.next()
        TT(P, "dve", tc_, a3(tc_), qg, x0, cs, sinb, ALU.mult)
        TT(P, "dve", td, a3(td), qg, x1, cs, cosb, ALU.mult)
        TT(P, "dve", ro, r4[:, :, :, 1], tc_, a3(tc_), td, a3(td), ALU.add)
        src = ro
    TT(P, "dve", ob, v3(ob, ob.t[:, 0:n]), src, v3(src, src.t[:, 0:n]), ssh,
       ssh.t[:, 0:nh].unsqueeze(2).to_broadcast([128, nh, 64]), ALU.mult)


def even_phase1(P, G, L, j):
    with phase(P):
        load_mod(P, G)
        W = norm_work(P)
        wbf = P.sbuf("winbf", [128, 8, 2304], BF16)
        stg = rots(P, "wstg", [128, 2304], F32, 2)
        for k in range(8):
            s = stg.next()
            P.dma(s.ap(), G.ev_w_in[j][k * 128:(k + 1) * 128, :], writes=[s])
            CP(P, "act", wbf, wbf.t[:, k, 0:512].rearrange("q (p r d) -> q p r d", p=4, r=2),
               s, s.t[:, 0:512].rearrange("q (r p d) -> q p r d", r=2, p=4))
            CP(P, "act", wbf, wbf.t[:, k, 512:2304], s, s.t[:, 512:2304])
        gq = P.sbuf("gq", [128, 64], F32)
        gk = P.sbuf("gk", [128, 64], F32)
        P.dma(gq.ap(), G.q_norm_w[j].partition_broadcast(128), writes=[gq])
        P.dma(gk.ap(), G.k_norm_w[j].partition_broadcast(128), writes=[gk])
        W.sq = P.sbuf("sq", [128, 512], F32)
        W.sshrot = rots(P, "ssh", [128, 8], F32, 4)
        W.qgrot = rots(P, "qg", [128, 512], F32, 2)
        W.rorot = rots(P, "ro", [128, 512], F32, 2)
        W.ropet = rots(P, "ropet", [128, 256], F32, 8)
        csrot = rots(P, "cs", [128, 64], F32, 2)
        gorot = rots(P, "go", [128, 512], BF16, 3)
        uorot = rots(P, "uo", [128, 512], F32, 2)
        varot = rots(P, "va", [128, 2, 192], BF16, 2)
        for va in varot.tiles:
            P.dve(lambda e, va=va: e.memset(va.ap(), 1.0), writes=[va])
        qbrot = rots(P, "qb", [128, 512], BF16, 2)
        kbrot = rots(P, "kb", [128, 128], BF16, 2)
        qsrot = rots(P, "qs", [128, 640], BF16, 2)
        qkrot = Rot(G.banks[0:4])
        ps_save = G.ps
        G.ps = Rot(G.banks[4:8])

        def fm(bi, blk, hT, f):
            t0, ntok = blk
            col = 768 + 128 * f
            ps = G.ps.next()
            for k in range(8):
                MM(P, ps, ps.t[:, 0:ntok], wbf, wbf.t[:, k, col:col + 128], hT, hT.t[:, k, 0:ntok], k == 0, k == 7)
            if f < 4 or f >= 8:
                gi = f if f < 4 else f - 4
                o = gorot.next()
                ACT(P, o, o.t[:, 0:ntok], ps, ps.t[:, 0:ntok], AF.Silu)
                P.dma(G.GT[gi, :, t0:t0 + ntok], o.t[:, 0:ntok], reads=[o], writes=[G.GT_r[gi][bi]], q="pool")
            else:
                ui = f - 4
                o = uorot.next()
                CP(P, "act", o, o.t[:, 0:ntok], ps, ps.t[:, 0:ntok])
                P.dma(G.UT[ui, :, t0:t0 + ntok], o.t[:, 0:ntok], reads=[o], writes=[G.UT_r[ui][bi]], q="pool")

        def stageM(blk, hT, tl):
            t0, ntok = blk
            tt = t0 // 128 + tl
            psq = qkrot.next()
            pskv = qkrot.next()
            for k in range(8):
                MM(P, psq, psq.t[:, 0:512], hT, hT.t[:, k, tl * 128:(tl + 1) * 128], wbf, wbf.t[:, k, 0:512],
                   k == 0, k == 7)
            for k in range(8):
                MM(P, pskv, pskv.t[:, 0:256], hT, hT.t[:, k, tl * 128:(tl + 1) * 128], wbf, wbf.t[:, k, 512:768],
                   k == 0, k == 7)
            return psq, pskv

        def stageQ(blk, tl, psq, pskv):
            t0, ntok = blk
            tt = t0 // 128 + tl
            va = varot.next()
            CP(P, "act", va, va.t[:, :, 64:128], pskv, pskv.t[:, 128:256].rearrange("p (r d) -> p r d", r=2))
            P.dma(G.VS[tt], va.ap(), reads=[va], writes=[G.VS_r[tt]], q="pool")
            cs = None
            if tt >= 2:
                cs = csrot.next()
                P.dma(cs.ap(), G.rope[(tt - 2) * 128:(tt - 1) * 128, :], writes=[cs])
            qb = qbrot.next()
            kb = kbrot.next()
            qk_post(P, G, W, psq, psq.t[:, 0:512], 8, gq, cs, qb)
            qk_post(P, G, W, pskv, pskv.t[:, 0:128], 2, gk, cs, kb)
            tp = G.ps.next()
            tpb = tp.t[:].bitcast(BF16)
            for p in range(4):
                TR(P, tp, tpb[:, p * 128:(p + 1) * 128], qb, qb.t[:, p * 128:(p + 1) * 128], G.identb)
            TR(P, tp, tpb[:, 512:640], kb, kb.t[:, 0:128], G.identb)
            qs = qsrot.next()
            CP(P, "dve", qs, qs.t[:, 0:640], tp, tpb[:, 0:640])
            P.dma(G.QT[:, :, tt * 128:(tt + 1) * 128], qs.t[:, 0:512].rearrange("p (a t) -> p a t", a=4),
                  reads=[qs], writes=[G.QT_r[tt]], q="pool")
            P.dma(G.KT[:, tt * 128:(tt + 1) * 128], qs.t[:, 512:640], reads=[qs], writes=[G.KT_r[tt]], q="pool")

        NB = len(G.blocks)
        hTs = {0: W.hTrot.next()}
        norm_block(P, G, W, G.blocks[0], hTs[0])
        for bi, blk in enumerate(G.blocks):
            hT = hTs.pop(bi)
            nt = blk[1] // 128
            fl = [list(range(0, 4)), list(range(4, 8)), list(range(8, 12))]
            live = {}
            for tl in range(min(2, nt)):
                live[tl] = stageM(blk, hT, tl)
            for step in range(max(nt, 3)):
                if step < 3:
                    for f in fl[step]:
                        fm(bi, blk, hT, f)
                if step < nt:
                    stageQ(blk, step, *live.pop(step))
                    if step + 2 < nt:
                        live[step + 2] = stageM(blk, hT, step + 2)
                if step == 0 and bi + 1 < NB:
                    hTs[bi + 1] = W.hTrot.next()
                    norm_block(P, G, W, G.blocks[bi + 1], hTs[bi + 1])
        G.ps = ps_save


def even_phase2(P, G, L, j):
    with phase(P):
        T, NT = G.T, G.NT
        KT = P.sbuf("KTs", [128, T], BF16)
        P.dma(KT.ap(), G.KT, reads=G.KT_r, writes=[KT])
        VA = P.sbuf("VAs", [128, NT, 2, 192], BF16)
        P.dma(VA.ap(), G.VS.rearrange("n p r c -> p n r c"), reads=G.VS_r, writes=[VA])
        SKEW = 2
        srot = Rot(G.banks[0:4])
        accrot = Rot(G.banks[4:8])
        qrot = rots(P, "qtb", [128, 8, 512], BF16, 2)
        for qt_ in qrot.tiles:
            P.pool(lambda e, qt_=qt_: e.memset(qt_.ap(), 0.0), writes=[qt_])
        ptrot = rots(P, "pt", [128, 512], BF16, 5)
        rrot = rots(P, "rr", [128, 512], F32, 2)
        mrot = rots(P, "mm", [128, 512], F32, 2)
        grot = rots(P, "gg", [128, 512], BF16, 2)
        orot = rots(P, "oo", [128, 512], BF16, 2)
        for bi, (t0, ntok) in enumerate(G.blocks):
            keys = [0, 1] if bi == 0 else list(range(NT))
            tts = list(range(t0 // 128, (t0 + ntok) // 128))
            QTb = qrot.next()
            for r_ in range(2):
                P.dma(QTb.t[r_ * 64:(r_ + 1) * 64, r_ * 4:(r_ + 1) * 4, 0:ntok],
                      G.QT[r_ * 64:(r_ + 1) * 64, :, t0:t0 + ntok], reads=[G.QT_r[tt] for tt in tts], writes=[QTb])
            for c in range(4):
                accs = []
                for hh in range(2):
                    h = 2 * c + hh
                    p, r = h % 4, h // 4
                    acc = accrot.next()
                    vsl = slice(64, 192) if hh == 0 else slice(0, 128)

                    def pv(pt, kt, acc=acc, r=r, vsl=vsl):
                        MM(P, acc, acc.t[:, 0:ntok], VA, VA.t[:, kt, r, vsl], pt, pt.t[:, 0:ntok],
                           kt == keys[0], kt == keys[-1])
                    pend = []
                    for kt in keys:
                        sps = srot.next()
                        MM(P, sps, sps.t[:, 0:ntok], KT, KT.t[:, kt * 128:(kt + 1) * 128],
                           QTb, QTb.t[:, h, 0:ntok], True, True)
                        if len(pend) >= SKEW:
                            pv(*pend.pop(0))
                        pt = ptrot.next()
                        ACT(P, pt, pt.t[:, 0:ntok], sps, sps.t[:, 0:ntok], AF.Exp, scale=0.125)
                        pend.append((pt, kt))
                    for pp_ in pend:
                        pv(*pp_)
                    accs.append(acc)
                A, B = accs
                Rr = rrot.next()
                P.dve(lambda e, Rr=Rr, A=A: e.reciprocal(out=Rr.t[0:64, 0:ntok], in_=A.t[64:128, 0:ntok]),
                      reads=[A], writes=[Rr])
                P.dve(lambda e, Rr=Rr, B=B: e.reciprocal(out=Rr.t[64:128, 0:ntok], in_=B.t[0:64, 0:ntok]),
                      reads=[B], writes=[Rr])
                M = mrot.next()
                TT(P, "dve", M, M.t[0:64, 0:ntok], A, A.t[0:64, 0:ntok], Rr, Rr.t[0:64, 0:ntok], ALU.mult)
                TT(P, "dve", M, M.t[64:128, 0:ntok], B, B.t[64:128, 0:ntok], Rr, Rr.t[64:128, 0:ntok], ALU.mult)
                g = grot.next()
                P.dma(g.t[:, 0:ntok], G.GT[c, :, t0:t0 + ntok], reads=[G.GT_r[c][bi]], writes=[g])
                o = orot.next()
                TT(P, "pool", o, o.t[:, 0:ntok], M, M.t[:, 0:ntok], g, g.t[:, 0:ntok], ALU.mult)
                P.dma(G.MIXT[c, :, t0:t0 + ntok], o.t[:, 0:ntok], reads=[o], writes=[G.MIXT_r[c][bi]], q="pool")


def even_phase3(P, G, L, j):
    T = G.T
    segs_f = [(a, a + n) for (a, n) in G.blocks]
    segs_r = [segs_f[0]] + segs_f[:0:-1]
    with phase(P):
        u = P.sbuf("ru", [128, T], F32)
        y = P.sbuf("ry", [128, T], F32)
        yb = P.sbuf("ryb", [128, T], BF16)
        H = P.sbuf("rH", [128, T], F32)
        HB = u
        wrot = rots(P, "rw", [128, 512], F32, 2)
        wrd = [rots(P, "rw%d_" % d_, [128, 512], F32, 10) for d_ in range(2)]
        cols = rots(P, "rcol", [128, 17], F32, 2)
        wstg = rots(P, "rwst", [128, 128], F32, 2)
        wbd = rots(P, "rwbd", [128, 128], BF16, 4)
        grot = rots(P, "rg", [128, 512], BF16, 2)
        orot = rots(P, "ro_", [128, 512], BF16, 2)
        for c in range(4):
            cs_ = slice(c * 128, (c + 1) * 128)
            P.dma(u.ap(), G.UT[c], reads=G.UT_r[c], writes=[u])
            col = cols.next()
            P.dma(col.t[:, 0:4], G.rg_conv_w[j][:, cs_].rearrange("k p -> p k"), writes=[col],
                  allow_slow_non_contiguous=True)
            P.dma(col.t[:, 4:5], G.rg_conv_b[j][cs_].rearrange("(p o) -> p o", o=1), writes=[col])
            for d in range(2):
                P.dma(col.t[:, 5 + d:6 + d], G.rg_ba[j, d].rearrange("h i -> (h i)")[cs_].rearrange("(p o) -> p o", o=1),
                      writes=[col])
                P.dma(col.t[:, 7 + d:8 + d], G.rg_bx[j, d].rearrange("h i -> (h i)")[cs_].rearrange("(p o) -> p o", o=1),
                      writes=[col])
                P.dma(col.t[:, 9 + d:10 + d], G.rg_lambda[j, d][cs_].rearrange("(p o) -> p o", o=1), writes=[col])
            ACT(P, col, col.t[:, 11:13], col, col.t[:, 9:11], AF.Exp, scale=-1.0)
            ACT(P, col, col.t[:, 11:13], col, col.t[:, 11:13], AF.Ln, bias=G.onecol.t[:, 0:1], extra_r=[G.onecol])
            TS(P, "dve", col, col.t[:, 11:13], col, col.t[:, 11:13], -8.0, None, ALU.mult)
            for (a, b) in (segs_f[0], (CTX, T)):
                TS(P, "dve", y, y.t[:, a:b], u, u.t[:, a:b], col.t[:, 2:3], col.t[:, 4:5], ALU.mult, ALU.add,
                   extra_r=[col])
                STT(P, "dve", y, y.t[:, a + 2:b], u, u.t[:, a:b - 2], col.t[:, 0:1], y, y.t[:, a + 2:b], ALU.mult,
                    ALU.add, extra_r=[col])
                STT(P, "dve", y, y.t[:, a + 1:b], u, u.t[:, a:b - 1], col.t[:, 1:2], y, y.t[:, a + 1:b], ALU.mult,
                    ALU.add, extra_r=[col])
                STT(P, "dve", y, y.t[:, a:b - 1], u, u.t[:, a + 1:b], col.t[:, 3:4], y, y.t[:, a:b - 1], ALU.mult,
                    ALU.add, extra_r=[col])
            CP(P, "pool", yb, yb.ap(), y, y.ap())
            TS(P, "dve", col, col.t[:, 13:17], col, col.t[:, 5:9], -1.0, None, ALU.mult)

            def dir_gen(d):
                wts = []
                for wsrc in (G.rg_wa, G.rg_wx):
                    st = wstg.next()
                    P.pool(lambda e, st=st: e.memset(st.ap(), 0.0), writes=[st])
                    for hb in range(2):
                        P.dma(st.t[hb * 64:(hb + 1) * 64, hb * 64:(hb + 1) * 64], wsrc[j, d, 2 * c + hb], writes=[st])
                    wb = wbd.next()
                    CP(P, "pool", wb, wb.ap(), st, st.ap())
                    wts.append(wb)
                yield
                wr = wrd[d]
                for (a, b) in (segs_f if d == 0 else segs_r):
                    n = b - a
                    psr, psi = G.ps.next(), G.ps.next()
                    MM(P, psr, psr.t[:, 0:n], wts[0], wts[0].ap(), yb, yb.t[:, a:b], True, True)
                    MM(P, psi, psi.t[:, 0:n], wts[1], wts[1].ap(), yb, yb.t[:, a:b], True, True)
                    rr, ii, aa, a2, bt = [wr.next() for _ in range(5)]
                    ACT(P, rr, rr.t[:, 0:n], psr, psr.t[:, 0:n], AF.Exp, scale=-1.0, bias=col.t[:, 13 + d:14 + d],
                        extra_r=[col])
                    ACT(P, ii, ii.t[:, 0:n], psi, psi.t[:, 0:n], AF.Exp, scale=-1.0, bias=col.t[:, 15 + d:16 + d],
                        extra_r=[col])
                    for t_ in (rr, ii):
                        ACT(P, t_, t_.t[:, 0:n], t_, t_.t[:, 0:n], AF.Ln, bias=G.onecol.t[:, 0:1], extra_r=[G.onecol])
                        ACT(P, t_, t_.t[:, 0:n], t_, t_.t[:, 0:n], AF.Exp, scale=-1.0)
                    ACT(P, aa, aa.t[:, 0:n], rr, rr.t[:, 0:n], AF.Exp, scale=col.t[:, 11 + d:12 + d], extra_r=[col])
                    TT(P, "pool", a2, a2.t[:, 0:n], aa, aa.t[:, 0:n], aa, aa.t[:, 0:n], ALU.mult)
                    TS(P, "pool", a2, a2.t[:, 0:n], a2, a2.t[:, 0:n], -1.0, 1.0, ALU.mult, ALU.add)
                    ACT(P, a2, a2.t[:, 0:n], a2, a2.t[:, 0:n], AF.Ln)
                    ACT(P, a2, a2.t[:, 0:n], a2, a2.t[:, 0:n], AF.Exp, scale=0.5)
                    TT(P, "dve", bt, bt.t[:, 0:n], ii, ii.t[:, 0:n], y, y.t[:, a:b], ALU.mult)
                    TT(P, "dve", bt, bt.t[:, 0:n], bt, bt.t[:, 0:n], a2, a2.t[:, 0:n], ALU.mult)
                    if d == 0:
                        init = 0.0 if a == 0 else H.t[:, a - 1:a]
                        P.dve(lambda e, a=a, b=b, n=n, aa=aa, bt=bt, init=init: e.tensor_tensor_scan(
                            out=H.t[:, a:b], data0=aa.t[:, 0:n], data1=bt.t[:, 0:n], initial=init,
                            op0=ALU.mult, op1=ALU.add), reads=[aa, bt, H], writes=[H])
                    else:
                        if a == 0:
                            init = 0.0
                        elif b == T:
                            init = HB.t[:, 0:1]
                        else:
                            init = HB.t[:, b:b + 1]
                        P.dve(lambda e, a=a, b=b, n=n, aa=aa, bt=bt, init=init: e.tensor_tensor_scan(
                            out=HB.t[:, a:b][:, ::-1], data0=aa.t[:, 0:n][:, ::-1], data1=bt.t[:, 0:n][:, ::-1],
                            initial=init, op0=ALU.mult, op1=ALU.add), reads=[aa, bt, HB], writes=[HB])
                    yield

            run_window([dir_gen(0), dir_gen(1)], 2)
            for bi, (t0, ntok) in enumerate(G.blocks):
                sm = wrot.next()
                TT(P, "dve", sm, sm.t[:, 0:ntok], H, H.t[:, t0:t0 + ntok], HB, HB.t[:, t0:t0 + ntok], ALU.add)
                g = grot.next()
                P.dma(g.t[:, 0:ntok], G.GT[4 + c, :, t0:t0 + ntok], reads=[G.GT_r[4 + c][bi]], writes=[g])
                o = orot.next()
                TT(P, "pool", o, o.t[:, 0:ntok], sm, sm.t[:, 0:ntok], g, g.t[:, 0:ntok], ALU.mult)
                P.dma(G.MIXT[4 + c, :, t0:t0 + ntok], o.t[:, 0:ntok], reads=[o], writes=[G.MIXT_r[4 + c][bi]], q="pool")


def odd_phase1(P, G, L, j):
    with phase(P):
        load_mod(P, G)
        W = norm_work(P)
        wbf = P.sbuf("owin", [128, 8, 2048], BF16)
        stg = rots(P, "owstg", [128, 2048], F32, 2)
        for k in range(8):
            s = stg.next()
            P.dma(s.ap(), G.od_w_in[j][k * 128:(k + 1) * 128, :], writes=[s])
            CP(P, "act", wbf, wbf.ap()[:, k, :], s, s.ap())
        gorot = rots(P, "ogo", [128, 512], BF16, 3)
        uorot = rots(P, "ouo", [128, 512], F32, 3)
        def blk_gen(bi, blk):
            t0, ntok = blk
            hT = W.hTrot.next()
            norm_block(P, G, W, blk, hT)
            yield
            for f in range(16):
                if f % 4 == 0 and f > 0:
                    yield
                ps = G.ps.next()
                for k in range(8):
                    MM(P, ps, ps.t[:, 0:ntok], wbf, wbf.t[:, k, f * 128:(f + 1) * 128], hT, hT.t[:, k, 0:ntok],
                       k == 0, k == 7)
                if f < 8:
                    o = uorot.next()
                    nch = ntok // 8
                    CP(P, "act", o, o.t[:, 0:ntok].rearrange("p (i c) -> p i c", i=8), ps,
                       ps.t[:, 0:ntok].rearrange("p (c i) -> p i c", i=8))
                    P.dma(G.UO[f, :, :, t0 // 8:t0 // 8 + nch], o.t[:, 0:ntok].rearrange("p (i c) -> p i c", i=8),
                          reads=[o], writes=[G.UO_r[f][bi]], q="pool")
                else:
                    o = gorot.next()
                    ACT(P, o, o.t[:, 0:ntok], ps, ps.t[:, 0:ntok], AF.Silu)
                    P.dma(G.GT[f - 8, :, t0:t0 + ntok], o.t[:, 0:ntok], reads=[o], writes=[G.GT_r[f - 8][bi]], q="pool")

        run_window([blk_gen(bi, blk) for bi, blk in enumerate(G.blocks)], 2)


def odd_phase2(P, G, L, j):
    S = G.S
    NX = S // 8
    NCH = 32 + NX
    NS = min(512, NX)
    XB = 34
    segs_x = [(a, min(a + NS, NX)) for a in range(0, NX, NS)]
    seg_c = (0, 32, 0)
    segs_f = [seg_c] + [(32 + a, 32 + b, XB + a) for (a, b) in segs_x]
    segs_r = [seg_c] + [(32 + a, 32 + b, XB + a) for (a, b) in segs_x[::-1]]
    NLV = 13
    with phase(P):
        lre = P.sbuf("s_lre", [128, 64], F32)
        lim = P.sbuf("s_lim", [128, 64], F32)
        dtt = P.sbuf("s_dt", [128, 64], F32)
        for d in range(2):
            for h in range(2):
                dst = (slice(h * 64, (h + 1) * 64), slice(d * 32, (d + 1) * 32))
                P.dma(lre.t[dst], G.s5_lambda_re[j, d].rearrange("(q h) p -> h p q", h=2)[h], writes=[lre],
                      allow_slow_non_contiguous=True)
                P.dma(lim.t[dst], G.s5_lambda_im[j, d].rearrange("(q h) p -> h p q", h=2)[h], writes=[lim],
                      allow_slow_non_contiguous=True)
                P.dma(dtt.t[dst], G.s5_log_step[j, d].rearrange("(q h) -> h q", h=2)[h].partition_broadcast(64),
                      writes=[dtt], allow_slow_non_contiguous=True)
        ACT(P, dtt, dtt.ap(), dtt, dtt.ap(), AF.Exp)
        mag = P.sbuf("s_mag", [128, 64], F32)
        th = P.sbuf("s_th", [128, 64], F32)
        xx = P.sbuf("s_xx", [128, 64], F32)
        TT(P, "dve", xx, xx.ap(), lre, lre.ap(), dtt, dtt.ap(), ALU.mult)
        TS(P, "dve", mag, mag.ap(), xx, xx.ap(), 1.0 / 720, None, ALU.mult)
        for cf in (1.0 / 120, 1.0 / 24, 1.0 / 6, 0.5, 1.0):
            STT(P, "dve", mag, mag.ap(), mag, mag.ap(), cf, xx, xx.ap(), ALU.add, ALU.mult)
        TS(P, "dve", mag, mag.ap(), mag, mag.ap(), 1.0, None, ALU.add)
        TT(P, "dve", th, th.ap(), lim, lim.ap(), dtt, dtt.ap(), ALU.mult)
        cc = P.sbuf("s_cc", [128, 64], F32)
        sn = P.sbuf("s_sn", [128, 64], F32)
        t1 = P.sbuf("s_t1", [128, 64], F32)
        t2 = P.sbuf("s_t2", [128, 64], F32)
        ACT(P, sn, sn.ap(), th, th.ap(), AF.Sin, scale=1.0 / 64)
        ACT(P, cc, cc.ap(), th, th.ap(), AF.Sin, scale=1.0 / 64, bias=G.hpicol.t[:, 0:1], extra_r=[G.hpicol])

        def renorm(c_t, c_ap, s_t, s_ap):
            TT(P, "dve", t1, t1.ap(), c_t, c_ap, c_t, c_ap, ALU.mult)
            TT(P, "dve", t2, t2.ap(), s_t, s_ap, s_t, s_ap, ALU.mult)
            TT(P, "dve", t1, t1.ap(), t1, t1.ap(), t2, t2.ap(), ALU.add)
            TS(P, "dve", t1, t1.ap(), t1, t1.ap(), -0.5, 1.5, ALU.mult, ALU.add)
            TT(P, "dve", c_t, c_ap, c_t, c_ap, t1, t1.ap(), ALU.mult)
            TT(P, "dve", s_t, s_ap, s_t, s_ap, t1, t1.ap(), ALU.mult)

        def square_cis(c_t, c_ap, s_t, s_ap, oc_t, oc_ap, os_t, os_ap):
            TT(P, "dve", t1, t1.ap(), c_t, c_ap, c_t, c_ap, ALU.mult)
            TT(P, "dve", t2, t2.ap(), s_t, s_ap, s_t, s_ap, ALU.mult)
            STT(P, "dve", os_t, os_ap, c_t, c_ap, 2.0, s_t, s_ap, ALU.mult, ALU.mult)
            TT(P, "dve", oc_t, oc_ap, t1, t1.ap(), t2, t2.ap(), ALU.subtract)
            renorm(oc_t, oc_ap, os_t, os_ap)
        c2 = P.sbuf("s_c2", [128, 64], F32)
        s2 = P.sbuf("s_s2", [128, 64], F32)
        cur = (cc, sn)
        nxt = (c2, s2)
        for _ in range(6):
            square_cis(cur[0], cur[0].ap(), cur[1], cur[1].ap(), nxt[0], nxt[0].ap(), nxt[1], nxt[1].ap())
            cur, nxt = nxt, cur
        cth, sth = cur
        pwc = P.sbuf("s_pwc", [128, NLV, 64], F32)
        pws = P.sbuf("s_pws", [128, NLV, 64], F32)
        npws = P.sbuf("s_npws", [128, NLV, 64], F32)
        CP(P, "dve", pwc, pwc.t[:, 0, :], cth, cth.ap())
        CP(P, "dve", pws, pws.t[:, 0, :], sth, sth.ap())
        for k in range(NLV - 1):
            square_cis(pwc, pwc.t[:, k, :], pws, pws.t[:, k, :], pwc, pwc.t[:, k + 1, :], pws, pws.t[:, k + 1, :])
        TS(P, "dve", npws, npws.ap(), pws, pws.ap(), -1.0, None, ALU.mult)
        PR = P.sbuf("s_PR", [128, 9, 64], F32)
        PI = P.sbuf("s_PI", [128, 9, 64], F32)
        NPR = P.sbuf("s_NPR", [128, 9, 64], F32)
        NPI = P.sbuf("s_NPI", [128, 9, 64], F32)
        are = P.sbuf("s_are", [128, 64], F32)
        aim = P.sbuf("s_aim", [128, 64], F32)
        TT(P, "dve", are, are.ap(), mag, mag.ap(), cth, cth.ap(), ALU.mult)
        TT(P, "dve", aim, aim.ap(), mag, mag.ap(), sth, sth.ap(), ALU.mult)
        P.dve(lambda e: e.memset(PR.t[:, 0, :], 1.0), writes=[PR])
        P.dve(lambda e: e.memset(PI.t[:, 0, :], 0.0), writes=[PI])
        for tau in range(8):
            TT(P, "dve", t1, t1.ap(), PR, PR.t[:, tau, :], are, are.ap(), ALU.mult)
            TT(P, "dve", t2, t2.ap(), PI, PI.t[:, tau, :], aim, aim.ap(), ALU.mult)
            TT(P, "dve", PR, PR.t[:, tau + 1, :], t1, t1.ap(), t2, t2.ap(), ALU.subtract)
            TT(P, "dve", t1, t1.ap(), PR, PR.t[:, tau, :], aim, aim.ap(), ALU.mult)
            TT(P, "dve", t2, t2.ap(), PI, PI.t[:, tau, :], are, are.ap(), ALU.mult)
            TT(P, "dve", PI, PI.t[:, tau + 1, :], t1, t1.ap(), t2, t2.ap(), ALU.add)
        TS(P, "dve", NPR, NPR.ap(), PR, PR.ap(), -1.0, None, ALU.mult)
        TS(P, "dve", NPI, NPI.ap(), PI, PI.ap(), -1.0, None, ALU.mult)
        mag8 = P.sbuf("s_mag8", [128, 64], F32)
        TT(P, "dve", mag8, mag8.ap(), mag, mag.ap(), mag, mag.ap(), ALU.mult)
        TT(P, "dve", mag8, mag8.ap(), mag8, mag8.ap(), mag8, mag8.ap(), ALU.mult)
        TT(P, "dve", mag8, mag8.ap(), mag8, mag8.ap(), mag8, mag8.ap(), ALU.mult)
        nr = P.sbuf("s_nr", [128, 64], F32)
        TS(P, "dve", nr, nr.ap(), are, are.ap(), -1.0, None, ALU.add)
        den = P.sbuf("s_den", [128, 64], F32)
        TT(P, "dve", den, den.ap(), lre, lre.ap(), lre, lre.ap(), ALU.mult)
        TT(P, "dve", t1, t1.ap(), lim, lim.ap(), lim, lim.ap(), ALU.mult)
        TT(P, "dve", den, den.ap(), den, den.ap(), t1, t1.ap(), ALU.add)
        P.dve(lambda e: e.reciprocal(out=den.ap(), in_=den.ap()), reads=[den], writes=[den])
        fre = P.sbuf("s_fre", [128, 64], F32)
        fim = P.sbuf("s_fim", [128, 64], F32)
        nfim = P.sbuf("s_nfim", [128, 64], F32)
        TT(P, "dve", t1, t1.ap(), nr, nr.ap(), lre, lre.ap(), ALU.mult)
        TT(P, "dve", t2, t2.ap(), aim, aim.ap(), lim, lim.ap(), ALU.mult)
        TT(P, "dve", fre, fre.ap(), t1, t1.ap(), t2, t2.ap(), ALU.add)
        TT(P, "dve", fre, fre.ap(), fre, fre.ap(), den, den.ap(), ALU.mult)
        TT(P, "dve", t1, t1.ap(), aim, aim.ap(), lre, lre.ap(), ALU.mult)
        TT(P, "dve", t2, t2.ap(), nr, nr.ap(), lim, lim.ap(), ALU.mult)
        TT(P, "dve", fim, fim.ap(), t1, t1.ap(), t2, t2.ap(), ALU.subtract)
        TT(P, "dve", fim, fim.ap(), fim, fim.ap(), den, den.ap(), ALU.mult)
        TS(P, "dve", nfim, nfim.ap(), fim, fim.ap(), -1.0, None, ALU.mult)

        ud = P.sbuf("s_ud", [128, 8, NCH], F32)
        ubd = P.sbuf("s_ubd", [128, 8, NCH], BF16)
        dcol = rots(P, "s_dcol", [128, 1], F32, 2)
        kacc = P.sbuf("s_kacc", [128, 15, 128], F32)
        kkb = P.sbuf("s_kkb", [128, 15, 128], BF16)
        ecos = rots(P, "s_ecos", [128, NS], F32, 2)
        esin = rots(P, "s_esin", [128, NS], F32, 2)
        rho = rots(P, "s_rho", [128, NS], F32, 2)
        braw = rots(P, "s_braw", [128, 2, 32], F32, 2)
        bbt = rots(P, "s_bbt", [128, 2, 32], F32, 2)
        craw = rots(P, "s_craw", [32, 2, 128], F32, 2)
        ctt = rots(P, "s_ct", [128, 2, 32], F32, 2)
        tmpA = rots(P, "s_tmpA", [128, 9, 32], F32, 8)
        baf = rots(P, "s_baf", [128, 8, 2, 32], BF16, 2)
        caf = rots(P, "s_caf", [128, 9, 2, 32], F32, 2)
        om = rots(P, "s_om", [128, 8, 2, 128], BF16, 2)
        ccp = rots(P, "s_ccp", [128, 9, 2, 128], BF16, 4)
        hsr = rots(P, "s_hs", [128, 2, NCH + 4], BF16, 4)
        wkd = [rots(P, "s_wk%d_" % d_, [128, NS], F32, 8) for d_ in range(2)]
        gl = rots(P, "s_gl", [128, 2], F32, 8)
        ygr = rots(P, "s_yg", [128, 512], BF16, 2)
        ygt = rots(P, "s_ygt", [128, 512], F32, 2)

        def lvl(n):
            k = n.bit_length() - 1
            assert (1 << k) == n
            return k + 3

        for c in range(8):
            P.dma(ud.ap(), G.UO[c], reads=G.UO_r[c], writes=[ud])
            CP(P, "act", ubd, ubd.ap(), ud, ud.ap())
            dc = dcol.next()
            P.dma(dc.ap(), G.s5_d[j][c * 128:(c + 1) * 128].rearrange("(p o) -> p o", o=1), writes=[dc])
            TS(P, "dve", ud, ud.ap(), ud, ud.ap(), dc.t[:, 0:1], None, ALU.mult, extra_r=[dc])
            P.pool(lambda e: e.memset(kacc.ap(), 0.0), writes=[kacc])
            def stream_gen(q, d, outl):
                gq_ = 8 * c + 2 * q
                pq = 4 * c + q
                rows = slice(32 * q, 32 * q + 32)
                colx = d * 32 + pq
                br = braw.next()
                P.pool(lambda e, br=br: e.memset(br.ap(), 0.0), writes=[br])
                for h in range(2):
                    P.dma(br.t[h * 64:(h + 1) * 64, 0, h * 16:(h + 1) * 16], G.s5_b_re[j, d, gq_ + h], writes=[br])
                    P.dma(br.t[h * 64:(h + 1) * 64, 1, h * 16:(h + 1) * 16], G.s5_b_im[j, d, gq_ + h], writes=[br])
                bt_ = bbt.next()
                frc, fic, nfic = fre.t[:, colx:colx + 1], fim.t[:, colx:colx + 1], nfim.t[:, colx:colx + 1]
                TS(P, "dve", bt_, bt_.t[:, 0, :], br, br.t[:, 0, :], frc, None, ALU.mult, extra_r=[fre])
                STT(P, "dve", bt_, bt_.t[:, 0, :], br, br.t[:, 1, :], nfic, bt_, bt_.t[:, 0, :], ALU.mult, ALU.add,
                    extra_r=[nfim])
                TS(P, "dve", bt_, bt_.t[:, 1, :], br, br.t[:, 1, :], frc, None, ALU.mult, extra_r=[fre])
                STT(P, "dve", bt_, bt_.t[:, 1, :], br, br.t[:, 0, :], fic, bt_, bt_.t[:, 1, :], ALU.mult, ALU.add,
                    extra_r=[fim])
                cr = craw.next()
                P.pool(lambda e, cr=cr: e.memset(cr.ap(), 0.0), writes=[cr])
                for h in range(2):
                    P.dma(cr.t[h * 16:(h + 1) * 16, 0, h * 64:(h + 1) * 64], G.s5_c_re[j, d, gq_ + h], writes=[cr])
                    P.dma(cr.t[h * 16:(h + 1) * 16, 1, h * 64:(h + 1) * 64], G.s5_c_im[j, d, gq_ + h], writes=[cr])
                tp2 = G.ps.next()
                TR(P, tp2, tp2.t[:, 0:32], cr, cr.t[:, 0, :], G.identf)
                TR(P, tp2, tp2.t[:, 32:64], cr, cr.t[:, 1, :], G.identf)
                ct = ctt.next()
                CP(P, "act", ct, ct.ap(), tp2, tp2.t[:, 0:64].rearrange("p (a m) -> p a m", a=2))
                def bc_p(tab, n):
                    return tab.t[:, 0:n, colx:colx + 1].to_broadcast([128, n, 32])

                def bc_x(t, a, n):
                    return t.t[:, a, :].unsqueeze(1).to_broadcast([128, n, 32])
                ba = baf.next()
                ca = caf.next()
                x1, x2 = tmpA.next(), tmpA.next()
                TT(P, "dve", x1, x1.t[:, 0:8, :], bt_, bc_x(bt_, 0, 8), PR, bc_p(PR, 8), ALU.mult)
                TT(P, "dve", x2, x2.t[:, 0:8, :], bt_, bc_x(bt_, 1, 8), PI, bc_p(PI, 8), ALU.mult)
                TT(P, "dve", ba, ba.t[:, :, 0, :], x1, x1.t[:, 0:8, :], x2, x2.t[:, 0:8, :], ALU.subtract)
                x3, x4 = tmpA.next(), tmpA.next()
                TT(P, "pool", x3, x3.t[:, 0:8, :], bt_, bc_x(bt_, 0, 8), PI, bc_p(PI, 8), ALU.mult)
                TT(P, "pool", x4, x4.t[:, 0:8, :], bt_, bc_x(bt_, 1, 8), PR, bc_p(PR, 8), ALU.mult)
                TT(P, "pool", ba, ba.t[:, :, 1, :], x3, x3.t[:, 0:8, :], x4, x4.t[:, 0:8, :], ALU.add)
                y1, y2 = tmpA.next(), tmpA.next()
                TT(P, "dve", y1, y1.ap(), ct, bc_x(ct, 0, 9), PR, bc_p(PR, 9), ALU.mult)
                TT(P, "dve", y2, y2.ap(), ct, bc_x(ct, 1, 9), PI, bc_p(PI, 9), ALU.mult)
                TT(P, "dve", ca, ca.t[:, :, 0, :], y1, y1.ap(), y2, y2.ap(), ALU.subtract)
                y3, y4 = tmpA.next(), tmpA.next()
                TT(P, "pool", y3, y3.ap(), ct, bc_x(ct, 0, 9), NPI, bc_p(NPI, 9), ALU.mult)
                TT(P, "pool", y4, y4.ap(), ct, bc_x(ct, 1, 9), NPR, bc_p(NPR, 9), ALU.mult)
                TT(P, "pool", ca, ca.t[:, :, 1, :], y3, y3.ap(), y4, y4.ap(), ALU.add)
                omt = om.next()
                P.act(lambda e, omt=omt: e.memzero(omt.ap()), writes=[omt])
                for half in range(2):
                    tpo = G.ps.next()
                    tpb = tpo.t[:].bitcast(BF16)
                    for tt_ in range(4):
                        tau = half * 4 + tt_
                        for a_ in range(2):
                            TR(P, tpo, tpb[0:32, (tt_ * 2 + a_) * 128:(tt_ * 2 + a_ + 1) * 128], ba, ba.t[:, tau, a_, :],
                               G.identb)
                    CP(P, "act", omt, omt.t[rows, half * 4:(half + 1) * 4, :, :], tpo,
                       tpb[0:32, 0:1024].rearrange("p (t a m) -> p t a m", t=4, a=2))
                cpt = ccp.next()
                P.act(lambda e, cpt=cpt: e.memzero(cpt.ap()), writes=[cpt])
                CP(P, "pool", cpt, cpt.t[:, :, :, rows], ca, ca.ap())
                psk = G.ps.next()
                for tau in range(8):
                    sl = slice(tau * 32, (tau + 1) * 32)
                    MM(P, psk, psk.t[0:32, sl], bt_, bt_.t[:, 0, :], ca, ca.t[:, tau, 0, :], tau == 0, False)
                for tau in range(8):
                    sl = slice(tau * 32, (tau + 1) * 32)
                    MM(P, psk, psk.t[0:32, sl], bt_, bt_.t[:, 1, :], ca, ca.t[:, tau, 1, :], False, tau == 7)
                if d == 0:
                    CP(P, "act", kacc, kacc.t[rows, 0, rows], psk, psk.t[0:32, 0:32])
                else:
                    TT(P, "dve", kacc, kacc.t[rows, 0, rows], psk, psk.t[0:32, 0:32], kacc, kacc.t[rows, 0, rows],
                       ALU.add)
                kb = 1 + 7 * d
                CP(P, "act", kacc, kacc.t[rows, kb:kb + 7, rows], psk,
                   psk.t[0:32, 32:256].rearrange("p (t m) -> p t m", t=7))
                yield
                ec, es = ecos.next(), esin.next()
                P.dve(lambda e, ec=ec: e.memset(ec.t[:, 0:1], 1.0), writes=[ec])
                P.dve(lambda e, es=es: e.memset(es.t[:, 0:1], 0.0), writes=[es])
                m = 1
                k = 3
                while m < NS:
                    ck, sk = pwc.t[:, k, colx:colx + 1], pws.t[:, k, colx:colx + 1]
                    nsk = npws.t[:, k, colx:colx + 1]
                    w1, w2 = wkd[d].next(), wkd[d].next()
                    w3 = wkd[d].next()
                    if m < 32:
                        TS(P, "dve", w1, w1.t[:, 0:m], ec, ec.t[:, 0:m], ck, None, ALU.mult, extra_r=[pwc])
                        TS(P, "dve", w2, w2.t[:, 0:m], es, es.t[:, 0:m], ck, None, ALU.mult, extra_r=[pwc])
                        TS(P, "dve", w3, w3.t[:, 0:m], ec, ec.t[:, 0:m], sk, None, ALU.mult, extra_r=[pws])
                    else:
                        ACT(P, w1, w1.t[:, 0:m], ec, ec.t[:, 0:m], AF.Copy, scale=ck, extra_r=[pwc])
                        ACT(P, w2, w2.t[:, 0:m], es, es.t[:, 0:m], AF.Copy, scale=ck, extra_r=[pwc])
                        ACT(P, w3, w3.t[:, 0:m], ec, ec.t[:, 0:m], AF.Copy, scale=sk, extra_r=[pws])
                    STT(P, "dve", ec, ec.t[:, m:2 * m], es, es.t[:, 0:m], nsk, w1, w1.t[:, 0:m], ALU.mult, ALU.add,
                        extra_r=[npws])
                    TT(P, "dve", es, es.t[:, m:2 * m], w2, w2.t[:, 0:m], w3, w3.t[:, 0:m], ALU.add)
                    m *= 2
                    k += 1
                rh = rho.next()
                ACT(P, rh, rh.ap(), G.ones512, G.ones512.t[:, 0:NS], AF.Copy, scale=mag8.t[:, colx:colx + 1],
                    extra_r=[mag8])
                yield
                hs = hsr.next()
                P.act(lambda e, hs=hs: e.memzero(hs.ap()), writes=[hs])
                prev = None
                for si, (a, b, hcol) in enumerate(segs_f if d == 0 else segs_r):
                    if si > 0:
                        yield
                    n = b - a
                    rv = (lambda ap: ap) if d == 0 else (lambda ap: ap[:, ::-1])
                    pvr, pvi = G.ps.next(), G.ps.next()
                    for ip in range(8):
                        tau = 7 - ip if d == 0 else ip
                        MM(P, pvr, pvr.t[:, 0:n], omt, omt.t[:, tau, 0, :], ubd, ubd.t[:, ip, a:b], ip == 0, ip == 7)
                    for ip in range(8):
                        tau = 7 - ip if d == 0 else ip
                        MM(P, pvi, pvi.t[:, 0:n], omt, omt.t[:, tau, 1, :], ubd, ubd.t[:, ip, a:b], ip == 0, ip == 7)
                    m1, m2, m3, m4, gre, gim = [wkd[d].next() for _ in range(6)]
                    wre, wim = m1, m3
                    TT(P, "dve", m1, m1.t[:, 0:n], pvr, rv(pvr.t[:, 0:n]), ec, ec.t[:, 0:n], ALU.mult)
                    TT(P, "dve", m2, m2.t[:, 0:n], pvi, rv(pvi.t[:, 0:n]), es, es.t[:, 0:n], ALU.mult)
                    TT(P, "pool", wre, wre.t[:, 0:n], m1, m1.t[:, 0:n], m2, m2.t[:, 0:n], ALU.add)
                    TT(P, "dve", m3, m3.t[:, 0:n], pvi, rv(pvi.t[:, 0:n]), ec, ec.t[:, 0:n], ALU.mult)
                    TT(P, "dve", m4, m4.t[:, 0:n], pvr, rv(pvr.t[:, 0:n]), es, es.t[:, 0:n], ALU.mult)
                    TT(P, "pool", wim, wim.t[:, 0:n], m3, m3.t[:, 0:n], m4, m4.t[:, 0:n], ALU.subtract)
                    yield
                    if prev is None:
                        ire, iim = 0.0, 0.0
                        init_r = []
                    else:
                        pre, pim, pn = prev
                        kk = lvl(pn)
                        ck, sk = pwc.t[:, kk, colx:colx + 1], pws.t[:, kk, colx:colx + 1]
                        g_ = gl.next()
                        t_ = gl.next()
                        lr, li = pre.t[:, pn - 1:pn], pim.t[:, pn - 1:pn]
                        TS(P, "dve", t_, t_.t[:, 0:1], pre, lr, ck, None, ALU.mult, extra_r=[pwc])
                        TS(P, "dve", t_, t_.t[:, 1:2], pim, li, ck, None, ALU.mult, extra_r=[pwc])
                        TS(P, "dve", g_, g_.t[:, 0:1], pim, li, sk, None, ALU.mult, extra_r=[pws])
                        TT(P, "dve", g_, g_.t[:, 0:1], t_, t_.t[:, 0:1], g_, g_.t[:, 0:1], ALU.subtract)
                        STT(P, "dve", g_, g_.t[:, 1:2], pre, lr, sk, t_, t_.t[:, 1:2], ALU.mult, ALU.add,
                            extra_r=[pws])
                        ire, iim = g_.t[:, 0:1], g_.t[:, 1:2]
                        init_r = [g_]
                    P.dve(lambda e, gre=gre, wre=wre, rh=rh, ire=ire, n=n: e.tensor_tensor_scan(
                        out=gre.t[:, 0:n], data0=rh.t[:, 0:n], data1=wre.t[:, 0:n], initial=ire,
                        op0=ALU.mult, op1=ALU.add), reads=[rh, wre] + init_r, writes=[gre])
                    P.dve(lambda e, gim=gim, wim=wim, rh=rh, iim=iim, n=n: e.tensor_tensor_scan(
                        out=gim.t[:, 0:n], data0=rh.t[:, 0:n], data1=wim.t[:, 0:n], initial=iim,
                        op0=ALU.mult, op1=ALU.add), reads=[rh, wim] + init_r, writes=[gim])
                    prev = (gre, gim, n)
                    yield
                    o1, o2, o3, o4 = m2, m4, m1, m3
                    off = hcol + (1 if d == 0 else 0)
                    TT(P, "pool", o1, o1.t[:, 0:n], gre, gre.t[:, 0:n], ec, ec.t[:, 0:n], ALU.mult)
                    TT(P, "pool", o2, o2.t[:, 0:n], gim, gim.t[:, 0:n], es, es.t[:, 0:n], ALU.mult)
                    TT(P, "dve", hs, rv(hs.t[:, 0, off:off + n]), o1, o1.t[:, 0:n], o2, o2.t[:, 0:n], ALU.subtract)
                    TT(P, "pool", o3, o3.t[:, 0:n], gre, gre.t[:, 0:n], es, es.t[:, 0:n], ALU.mult)
                    TT(P, "pool", o4, o4.t[:, 0:n], gim, gim.t[:, 0:n], ec, ec.t[:, 0:n], ALU.mult)
                    TT(P, "dve", hs, rv(hs.t[:, 1, off:off + n]), o3, o3.t[:, 0:n], o4, o4.t[:, 0:n], ALU.add)
                    if si == 0:
                        if d == 0:
                            CP(P, "dve", hs, hs.t[:, :, XB:XB + 1], hs, hs.t[:, :, 32:33])
                        else:
                            CP(P, "dve", hs, hs.t[:, :, XB + NX:XB + NX + 1], hs, hs.t[:, :, 0:1])
                outl.append((hs, cpt))

            def gelu_blocks(bis):
                for bi in bis:
                    t0, ntok = G.blocks[bi]
                    nch = ntok // 8
                    ysl = ud.t[:, :, t0 // 8:t0 // 8 + nch]
                    g1, g2 = ygt.next(), ygt.next()
                    v3 = lambda t: t.t[:, 0:ntok].rearrange("p (i c) -> p i c", i=8)
                    ACT(P, g1, v3(g1), ud, ysl, AF.Square)
                    ACT(P, g1, g1.t[:, 0:ntok], g1, g1.t[:, 0:ntok], AF.Identity, scale=0.044715,
                        bias=G.onecol.t[:, 0:1], extra_r=[G.onecol])
                    TT(P, "pool", g1, v3(g1), g1, v3(g1), ud, ysl, ALU.mult)
                    ACT(P, g2, g2.t[:, 0:ntok], g1, g1.t[:, 0:ntok], AF.Sigmoid, scale=1.5957691216057308)
                    o = ygr.next()
                    TT(P, "dve", o, o.t[:, 0:ntok].rearrange("p (c i) -> p i c", i=8), g2, v3(g2), ud, ysl, ALU.mult)
                    P.dma(G.YG[c, :, t0:t0 + ntok], o.t[:, 0:ntok], reads=[o], writes=[G.YG_r[c][bi]], q="pool")

            def stageB_gen(q, streams):
                if q == 3:
                    CP(P, "pool", kkb, kkb.ap(), kacc, kacc.ap())
                order = segs_f if q < 3 else segs_f[1:] + segs_f[:1]
                for (a, b, hcol) in order:
                    n = b - a
                    for i in range(8):
                        py = G.ps.next()
                        mms = []
                        for d in range(2):
                            hs, cpt = streams[d]
                            tau = i + 1 if d == 0 else 8 - i
                            ro = hcol + (0 if d == 0 else 1)
                            mms.append((cpt, cpt.t[:, tau, 0, :], hs, hs.t[:, 0, ro:ro + n]))
                            mms.append((cpt, cpt.t[:, tau, 1, :], hs, hs.t[:, 1, ro:ro + n]))
                        if q == 3:
                            for ip in range(8):
                                if ip == i:
                                    ki = 0
                                elif ip < i:
                                    ki = i - ip
                                else:
                                    ki = 7 + (ip - i)
                                mms.append((kkb, kkb.t[:, ki, :], ubd, ubd.t[:, ip, a:b]))
                        for mi, (lt, lap, rt, rap) in enumerate(mms):
                            MM(P, py, py.t[:, 0:n], lt, lap, rt, rap, mi == 0, mi == len(mms) - 1)
                        TT(P, "dve", ud, ud.t[:, i, a:b], py, py.t[:, 0:n], ud, ud.t[:, i, a:b], ALU.add)
                        yield
                    if q == 3:
                        gelu_blocks([bi for bi, (t0, ntok) in enumerate(G.blocks) if a * 8 <= t0 < b * 8])

            def run_rr(gens):
                gens = list(gens)
                while gens:
                    for g_ in list(gens):
                        try:
                            next(g_)
                        except StopIteration:
                            gens.remove(g_)

            pendB = None
            for q in range(4):
                o0, o1 = [], []
                gl_ = [stream_gen(q, 0, o0), stream_gen(q, 1, o1)]
                if pendB is not None:
                    gl_.append(pendB)
                run_rr(gl_)
                pendB = stageB_gen(q, [o0[0], o1[0]])
            run_rr([pendB])
            if "YD" in G.debug:
                if c == 0:
                    G.YD = G.nc.dram_tensor("YD", [8, 128, 8, NCH], F32, kind="ExternalOutput").ap()
                P.dma(G.YD[c], ud.ap(), reads=[ud])
def odd_phase3(P, G, L, j):
    with phase(P):
        wbf = P.sbuf("gluw", [128, 8, 2048], BF16)
        stg = rots(P, "gwstg", [128, 2048], F32, 2)
        for k in range(8):
            s = stg.next()
            P.dma(s.ap(), G.glu_w[j][k * 128:(k + 1) * 128, :], writes=[s])
            CP(P, "pool", wbf, wbf.t[:, k, :], s, s.ap())
        gb = P.sbuf("glub", [128, 16], F32)
        P.dma(gb.ap(), G.glu_b[j].rearrange("(f p) -> p f", p=128), writes=[gb], allow_slow_non_contiguous=True)
        yrot = rots(P, "gyg", [128, 8, 512], BF16, 2)
        srot = rots(P, "gsg", [128, 8, 512], BF16, 2)
        arot = rots(P, "ga", [128, 512], F32, 3)
        brot = rots(P, "gb", [128, 512], F32, 3)
        orot = rots(P, "gout", [128, 512], BF16, 3)
        for bi, (t0, ntok) in enumerate(G.blocks):
            yg = yrot.next()
            P.dma(yg.t[:, :, 0:ntok], G.YG[:, :, t0:t0 + ntok].rearrange("c p t -> p c t"),
                  reads=[G.YG_r[c][bi] for c in range(8)], writes=[yg])
            sg = srot.next()
            P.dma(sg.t[:, :, 0:ntok], G.GT[:, :, t0:t0 + ntok].rearrange("c p t -> p c t"),
                  reads=[G.GT_r[c][bi] for c in range(8)], writes=[sg])
            for f in range(8):
                pa, pb = G.ps.next(), G.ps.next()
                for k in range(8):
                    MM(P, pa, pa.t[:, 0:ntok], wbf, wbf.t[:, k, f * 128:(f + 1) * 128], yg, yg.t[:, k, 0:ntok],
                       k == 0, k == 7)
                for k in range(8):
                    MM(P, pb, pb.t[:, 0:ntok], wbf, wbf.t[:, k, 1024 + f * 128:1024 + (f + 1) * 128], yg,
                       yg.t[:, k, 0:ntok], k == 0, k == 7)
                a_, b_ = arot.next(), brot.next()
                ACT(P, a_, a_.t[:, 0:ntok], pa, pa.t[:, 0:ntok], AF.Identity, bias=gb.t[:, f:f + 1], extra_r=[gb])
                ACT(P, b_, b_.t[:, 0:ntok], pb, pb.t[:, 0:ntok], AF.Sigmoid, bias=gb.t[:, 8 + f:9 + f], extra_r=[gb])
                TT(P, "dve", a_, a_.t[:, 0:ntok], a_, a_.t[:, 0:ntok], b_, b_.t[:, 0:ntok], ALU.mult)
                o = orot.next()
                TT(P, "pool", o, o.t[:, 0:ntok], a_, a_.t[:, 0:ntok], sg, sg.t[:, f, 0:ntok], ALU.mult)
                P.dma(G.MIXT[f, :, t0:t0 + ntok], o.t[:, 0:ntok], reads=[o], writes=[G.MIXT_r[f][bi]], q="pool")


def out_phase(P, G, L, w_dram, last):
    with phase(P):
        load_mod(P, G)
        wo = P.sbuf("wo", [128, 8, D], BF16)
        stg = rots(P, "wostg", [128, D], F32, 2)
        for k in range(8):
            s = stg.next()
            P.dma(s.ap(), w_dram[k * 128:(k + 1) * 128, :], writes=[s])
            CP(P, "pool", wo, wo.t[:, k, :], s, s.ap())
        mrot = rots(P, "mixT", [128, 8, 512], BF16, 2)
        xrot = rots(P, "oxt", [128, D], F32, 2)
        trot = rots(P, "otmp", [128, 512], F32, 2)
        orot = rots(P, "oxo", [128, D], F32, 2)
        for bi, (t0, ntok) in enumerate(G.blocks):
            if last and bi == 0:
                continue
            mixT = mrot.next()
            P.dma(mixT.t[:, :, 0:ntok], G.MIXT[:, :, t0:t0 + ntok].rearrange("c p t -> p c t"),
                  reads=[G.MIXT_r[c][bi] for c in range(8)], writes=[mixT])
            for tl in range(ntok // 128):
                tt = t0 // 128 + tl
                mod = G.modc if tt < 2 else G.modx
                xt = xrot.next()
                sap, sres = G.xsrc(tt)
                P.dma(xt.ap(), sap, reads=[sres], writes=[xt])
                xo = orot.next()
                for nh in range(2):
                    ps = G.ps.next()
                    for k in range(8):
                        MM(P, ps, ps.t[:, :], mixT, mixT.t[:, k, tl * 128:(tl + 1) * 128], wo,
                           wo.t[:, k, nh * 512:(nh + 1) * 512], k == 0, k == 7)
                    tmp = trot.next()
                    TT(P, "dve", tmp, tmp.ap(), ps, ps.t[:, :], mod,
                       mod.t[:, 2 * D + nh * 512:2 * D + (nh + 1) * 512], ALU.mult)
                    TT(P, "pool", xo, xo.t[:, nh * 512:(nh + 1) * 512], tmp, tmp.ap(), xt,
                       xt.t[:, nh * 512:(nh + 1) * 512], ALU.add)
                P.dma(G.XS[tt * 128:(tt + 1) * 128, :], xo.ap(), reads=[xo], writes=[G.XS_r[tt]], q="pool")


def final_phase(P, G):
    with phase(P):
        fw = P.sbuf("fw", [128, D], F32)
        P.dma(fw.ap(), G.final_norm_w.partition_broadcast(128), writes=[fw])
        xrot = rots(P, "fxt", [128, D], F32, 3)
        orot = rots(P, "fxo", [128, D], F32, 3)
        junk = P.sbuf("fjunk", [128, D], F32)
        ssrot = rots(P, "fss", [128, 1], F32, 4)
        for tt in range(2, G.NT):
            xt = xrot.next()
            P.dma(xt.ap(), G.XS[tt * 128:(tt + 1) * 128, :], reads=[G.XS_r[tt]], writes=[xt])
            ss = ssrot.next()
            ACT(P, junk, junk.ap(), xt, xt.ap(), AF.Square, accum=ss.t[:, 0:1], extra_w=[ss])
            RSQRT(P, ss, ss.t[:, 0:1], 1.0 / D, EPS)
            xo = orot.next()
            STT(P, "dve", xo, xo.ap(), xt, xt.ap(), ss.t[:, 0:1], fw, fw.ap(), ALU.mult, ALU.mult, extra_r=[ss])
            P.dma(G.out[(tt - 2) * 128:(tt - 1) * 128, :], xo.ap(), reads=[xo], q="pool")


USED_INPUTS = []


def build(S, depth=4, debug=(), stop_after=None):
    nc = bass.Bass("TRN2", target_bir_lowering=False)
    T = CTX + S
    NT = T // 128
    G = G_()
    G.debug = debug
    G.nc = nc
    G.S, G.T, G.NT = S, T, NT
    G.blocks = [(0, CTX)] + [(CTX + 512 * i, 512) for i in range(S // 512)]
    NB = len(G.blocks)

    def din(name, shape, dt=F32):
        if name not in USED_INPUTS:
            USED_INPUTS.append(name)
        return nc.dram_tensor(name, list(shape), dt, kind="ExternalInput").ap()

    def dscr(name, shape, dt):
        kind = "ExternalOutput" if name in debug else "Internal"
        return nc.dram_tensor(name, list(shape), dt, kind=kind).ap()

    G.x = din("x", [S, D])
    G.c = din("c", [D])
    G.ctx = din("ctx", [CTX, D])
    G.c_ctx = din("c_ctx", [D])
    G.ada_w = din("ada_w", [4, D, 3 * D])
    G.ada_b = din("ada_b", [4, 3 * D])
    G.ev_w_in = din("ev_w_in", [2, D, 2304])
    G.ev_w_out = din("ev_w_out", [2, D, D])
    G.q_norm_w = din("q_norm_w", [2, 64])
    G.k_norm_w = din("k_norm_w", [2, 64])
    G.rg_conv_w = din("rg_conv_w", [2, 4, 512])
    G.rg_conv_b = din("rg_conv_b", [2, 512])
    G.rg_wa = din("rg_wa", [2, 2, 8, 64, 64])
    G.rg_ba = din("rg_ba", [2, 2, 8, 64])
    G.rg_wx = din("rg_wx", [2, 2, 8, 64, 64])
    G.rg_bx = din("rg_bx", [2, 2, 8, 64])
    G.rg_lambda = din("rg_lambda", [2, 2, 512])
    G.final_norm_w = din("final_norm_w", [D])
    G.od_w_in = din("od_w_in", [2, D, 2048])
    G.s5_lambda_re = din("s5_lambda_re", [2, 2, 64, 64])
    G.s5_lambda_im = din("s5_lambda_im", [2, 2, 64, 64])
    G.s5_log_step = din("s5_log_step", [2, 2, 64])
    G.s5_b_re = din("s5_b_re", [2, 2, 64, 64, 16])
    G.s5_b_im = din("s5_b_im", [2, 2, 64, 64, 16])
    G.s5_c_re = din("s5_c_re", [2, 2, 64, 16, 64])
    G.s5_c_im = din("s5_c_im", [2, 2, 64, 16, 64])
    G.s5_d = din("s5_d", [2, D])
    G.glu_w = din("glu_w", [2, D, 2048])
    G.glu_b = din("glu_b", [2, 2048])
    G.od_w_out = din("od_w_out", [2, D, D])
    G.rope = din("rope", [S, 64])
    G.ident_in = din("ident", [128, 128])
    G.out = nc.dram_tensor("out", [S, D], F32, kind="ExternalOutput").ap()

    G.XS = dscr("XS", [T, D], F32)
    G.XS_r = [Res("XS%d" % i) for i in range(NT)]
    G.QT = dscr("QT", [128, 4, T], BF16)
    G.QT_r = [Res("QT%d" % i) for i in range(NT)]
    G.KT = dscr("KT", [128, T], BF16)
    G.KT_r = [Res("KT%d" % i) for i in range(NT)]
    G.VS = dscr("VS", [NT, 128, 2, 192], BF16)
    G.VS_r = [Res("VS%d" % i) for i in range(NT)]
    G.GT = dscr("GT", [8, 128, T], BF16)
    G.GT_r = [[Res("GT%d_%d" % (g, b)) for b in range(NB)] for g in range(8)]
    G.UT = dscr("UT", [4, 128, T], F32)
    G.UT_r = [[Res("UT%d_%d" % (g, b)) for b in range(NB)] for g in range(4)]
    G.UO = dscr("UO", [8, 128, 8, T // 8], F32)
    G.UO_r = [[Res("UO%d_%d" % (g, b)) for b in range(NB)] for g in range(8)]
    G.YG = dscr("YG", [8, 128, T], BF16)
    G.YG_r = [[Res("YG%d_%d" % (g, b)) for b in range(NB)] for g in range(8)]
    G.MIXT = dscr("MIXT", [8, 128, T], BF16)
    G.MIXT_r = [[Res("MX%d_%d" % (g, b)) for b in range(NB)] for g in range(8)]
    in_res = Res("inputs")

    P = Prog(nc)
    G.banks = [P.psum("ps%d" % i, [128, 512], F32) for i in range(8)]
    G.ps = Rot(G.banks)
    G.onecol = P.sbuf("onecol", [128, 1], F32)
    P.dve(lambda e: e.memset(G.onecol.ap(), 1.0), writes=[G.onecol])
    G.hpicol = P.sbuf("hpicol", [128, 1], F32)
    P.dve(lambda e: e.memset(G.hpicol.ap(), float(np.pi / 2)), writes=[G.hpicol])
    G.ones512 = P.sbuf("ones512", [128, 512], F32)
    P.dve(lambda e: e.memset(G.ones512.ap(), 1.0), writes=[G.ones512])
    G.MOD = dscr("MOD", [2, 3 * D], F32)
    G.MOD_r = Res("MOD")
    identf = P.sbuf("identf", [128, 128], F32)
    G.identb = P.sbuf("identb", [128, 128], BF16)
    P.dma(identf.ap(), G.ident_in, writes=[identf])
    CP(P, "dve", G.identb, G.identb.ap(), identf, identf.ap())
    G.identf = identf

    layer0 = [True]

    def xsrc(tt):
        if layer0[0]:
            if tt < 2:
                return G.ctx[tt * 128:(tt + 1) * 128, :], in_res
            return G.x[(tt - 2) * 128:(tt - 1) * 128, :], in_res
        return G.XS[tt * 128:(tt + 1) * 128, :], G.XS_r[tt]
    G.xsrc = xsrc

    def run_layers():
        for L in range(depth):
            layer0[0] = (L == 0)
            j = L // 2
            last = (L == depth - 1)
            if L % 2 == 0:
                seq = [("ada", lambda: ada_phase(P, G, L)), ("e1", lambda: even_phase1(P, G, L, j)),
                       ("e2", lambda: even_phase2(P, G, L, j)), ("e3", lambda: even_phase3(P, G, L, j)),
                       ("out", lambda: out_phase(P, G, L, G.ev_w_out[j], last))]
            else:
                seq = [("ada", lambda: ada_phase(P, G, L)), ("o1", lambda: odd_phase1(P, G, L, j)),
                       ("o2", lambda: odd_phase2(P, G, L, j)), ("o3", lambda: odd_phase3(P, G, L, j)),
                       ("out", lambda: out_phase(P, G, L, G.od_w_out[j], last))]
            for name, fn in seq:
                fn()
                if stop_after == (name, L):
                    return
    run_layers()
    if stop_after is None:
        final_phase(P, G)
    P.finish()
    return nc


def rope_table(S):
    t = np.arange(S)
    row = (t // 64).astype(np.float32)
    col = (t % 64).astype(np.float32)
    inv = (10000.0 ** (-np.arange(16, dtype=np.float32) / 16)).astype(np.float32)
    ang = np.concatenate([row[:, None] * inv, col[:, None] * inv], axis=-1).astype(np.float32)
    return np.concatenate([np.cos(ang), np.sin(ang)], axis=-1).astype(np.float32)


PER_BATCH = ("x", "c", "ctx")
_NC_CACHE = {}


def make_in_map(inp, b, S):
    m = {}
    for k, v in inp.items():
        v = np.asarray(v)
        if k == "x":
            m[k] = np.ascontiguousarray(v[b, :S])
        elif k in ("c", "ctx"):
            m[k] = np.ascontiguousarray(v[b])
        else:
            m[k] = np.ascontiguousarray(v)
    m["rope"] = rope_table(S)
    m["ident"] = np.eye(128, dtype=np.float32)
    return m


def kernel(**inputs):
    S = inputs["x"].shape[1]
    B = inputs["x"].shape[0]
    if S not in _NC_CACHE:
        _NC_CACHE[S] = build(S)
    nc = _NC_CACHE[S]
    names = set(USED_INPUTS)
    in_maps = []
    for b in range(B):
        m = make_in_map(inputs, b, S)
        in_maps.append({k: v for k, v in m.items() if k in names})
    res = run_bass_kernel_spmd(nc, in_maps, core_ids=list(range(B)))
    return np.stack([np.asarray(r["out"]) for r in res.results], axis=0).astype(np.float32)
```

```python
from contextlib import ExitStack
import numpy as np
import concourse.bass as bass
import concourse.mybir as mybir
from concourse.bass_utils import run_bass_kernel_spmd

F32 = mybir.dt.float32
BF16 = mybir.dt.bfloat16
ALU = mybir.AluOpType
AF = mybir.ActivationFunctionType
AX = mybir.AxisListType


class Res:
    def __init__(self, name):
        self.name = name
        self.lw = None
        self.rd = {}


class Tile(Res):
    def __init__(self, name, t):
        super().__init__(name)
        self.t = t

    def ap(self):
        return self.t[:]


class Prog:
    COMPUTE = ("pe", "dve", "act", "pool")
    RING = {"sp": 12, "pool": 6, "act": 6}

    def __init__(self, nc):
        self.nc = nc
        self.stack = ExitStack()
        self.ops = {e: [] for e in ("pe", "dve", "act", "pool", "sp")}
        self.sem = {}
        self.cnt = {}
        for e in self.COMPUTE:
            self.sem[e] = self.stack.enter_context(nc.semaphore("s_" + e))
            self.cnt[e] = 0
        self.ring = {}
        self.ring_pos = {}
        for q, n in self.RING.items():
            names = []
            for i in range(n):
                nm = "d_%s%d" % (q, i)
                self.sem[nm] = self.stack.enter_context(nc.semaphore(nm))
                self.cnt[nm] = 0
                names.append(nm)
            self.ring[q] = names
            self.ring_pos[q] = 0
        self.seen = {e: {} for e in self.ops}
        self.n_ops = 0

    def sbuf(self, name, shape, dtype):
        self.n_alloc = getattr(self, "n_alloc", 0) + 1
        name = "%s_%d" % (name, self.n_alloc)
        t = self.stack.enter_context(self.nc.sbuf_tensor(name, list(shape), dtype))
        return Tile(name, t)

    def psum(self, name, shape, dtype):
        t = self.stack.enter_context(self.nc.psum_tensor(name, list(shape), dtype))
        return Tile(name, t)

    def _deps(self, eng, tok_sem, reads, writes, same_eng_all=False):
        need = {}

        def add(tok):
            if tok is None:
                return
            s, v = tok
            if need.get(s, 0) < v:
                need[s] = v

        for r in reads:
            add(r.lw)
        for w in writes:
            add(w.lw)
            for s, v in w.rd.items():
                add((s, v))
        waits = []
        for s, v in need.items():
            if s == tok_sem and not same_eng_all and eng == "pe":
                continue
            if self.seen[eng].get(s, 0) >= v:
                continue
            self.seen[eng][s] = v
            waits.append((s, v))
        return waits

    def _commit(self, tok, reads, writes):
        for w in writes:
            w.lw = tok
            w.rd = {}
        for r in reads:
            if r in writes:
                continue
            if r.rd.get(tok[0], 0) < tok[1]:
                r.rd[tok[0]] = tok[1]

    def op(self, eng, fn, reads=(), writes=()):
        reads = list(reads)
        writes = list(writes)
        waits = self._deps(eng, eng, reads, writes)
        self.cnt[eng] += 1
        tok = (eng, self.cnt[eng])
        self._commit(tok, reads, writes)
        self.ops[eng].append((fn, waits, (eng, 1)))
        self.n_ops += 1
        return tok

    def pe(self, fn, reads=(), writes=()):
        return self.op("pe", fn, reads, writes)

    def dve(self, fn, reads=(), writes=()):
        return self.op("dve", fn, reads, writes)

    def act(self, fn, reads=(), writes=()):
        return self.op("act", fn, reads, writes)

    def pool(self, fn, reads=(), writes=()):
        return self.op("pool", fn, reads, writes)

    def dma(self, out, in_, reads=(), writes=(), q="sp", **kw):
        reads = list(reads)
        writes = list(writes)
        ring = self.ring[q]
        slot = ring[self.ring_pos[q] % len(ring)]
        self.ring_pos[q] += 1
        waits = self._deps(q, slot, reads, writes, same_eng_all=True)
        prev = self.cnt[slot]
        if prev > 0 and self.seen[q].get(slot, 0) < prev:
            self.seen[q][slot] = prev
            waits.append((slot, prev))
        self.cnt[slot] += 16
        tok = (slot, self.cnt[slot])
        self._commit(tok, reads, writes)
        self.ops[q].append((lambda e: e.dma_start(out=out, in_=in_, **kw), waits, (slot, 16)))
        self.n_ops += 1
        return tok

    def all_tokens(self):
        return [(s, v) for s, v in self.cnt.items() if v > 0]

    def flush(self):
        nc = self.nc
        final_waits = self.all_tokens()
        ops, sem = self.ops, self.sem

        def replay(name, eng):
            for fn, waits, inc in ops[name]:
                for s, v in waits:
                    eng.wait_ge(sem[s], v)
                ins = fn(eng)
                ins.then_inc(sem[inc[0]], inc[1])
            for s, v in final_waits:
                eng.wait_ge(sem[s], v)
                self.seen[name][s] = v

        with nc.Block() as block:
            @block.tensor
            def _(e):
                replay("pe", e)

            @block.vector
            def _(e):
                replay("dve", e)

            @block.scalar
            def _(e):
                replay("act", e)

            @block.gpsimd
            def _(e):
                replay("pool", e)

            @block.sync
            def _(e):
                replay("sp", e)
        self.ops = {e: [] for e in self.ops}

    def finish(self):
        self.flush()
        self.stack.close()


class Rot:
    def __init__(self, tiles):
        self.tiles = tiles
        self.i = 0

    def next(self):
        t = self.tiles[self.i % len(self.tiles)]
        self.i += 1
        return t


D = 1024
CTX = 256
EPS = 1e-6


class G_:
    pass


def MM(P, ot, oap, lt, lap, rt, rap, start, stop):
    P.pe(lambda e: e.matmul(oap, lhsT=lap, rhs=rap, start=start, stop=stop), reads=[lt, rt], writes=[ot])


def TR(P, ot, oap, it, iap, ident):
    P.pe(lambda e: e.transpose(oap, iap, ident.ap()[0:iap.shape[0], 0:iap.shape[0]]), reads=[it, ident], writes=[ot])


def ACT(P, ot, oap, it, iap, func, bias=None, scale=None, accum=None, extra_r=(), extra_w=()):
    kw = {}
    if bias is not None:
        kw["bias"] = bias
    if scale is not None:
        kw["scale"] = scale
    if accum is not None:
        kw["accum_out"] = accum
    P.act(lambda e: e.activation(out=oap, in_=iap, func=func, **kw), reads=[it] + list(extra_r),
          writes=[ot] + list(extra_w))


def TT(P, eng, ot, oap, at, aap, bt, bap, op):
    P.op(eng, lambda e: e.tensor_tensor(out=oap, in0=aap, in1=bap, op=op), reads=[at, bt], writes=[ot])


def TS(P, eng, ot, oap, it, iap, s1, s2, op0, op1=None, extra_r=()):
    if op1 is None:
        P.op(eng, lambda e: e.tensor_scalar(out=oap, in0=iap, scalar1=s1, scalar2=None, op0=op0),
             reads=[it] + list(extra_r), writes=[ot])
    else:
        P.op(eng, lambda e: e.tensor_scalar(out=oap, in0=iap, scalar1=s1, scalar2=s2, op0=op0, op1=op1),
             reads=[it] + list(extra_r), writes=[ot])


def STT(P, eng, ot, oap, at, aap, scal, bt, bap, op0, op1, extra_r=()):
    P.op(eng, lambda e: e.scalar_tensor_tensor(out=oap, in0=aap, scalar=scal, in1=bap, op0=op0, op1=op1),
         reads=[at, bt] + list(extra_r), writes=[ot])


def CP(P, eng, ot, oap, it, iap):
    if eng == "act":
        P.act(lambda e: e.copy(out=oap, in_=iap), reads=[it], writes=[ot])
    else:
        P.op(eng, lambda e: e.tensor_copy(out=oap, in_=iap), reads=[it], writes=[ot])


def RSQRT(P, t, ap, mul, add):
    TS(P, "dve", t, ap, t, ap, mul, add, ALU.mult, ALU.add)
    ACT(P, t, ap, t, ap, AF.Sqrt)
    P.dve(lambda e: e.reciprocal(out=ap, in_=ap), reads=[t], writes=[t])


def phase(P):
    from contextlib import contextmanager

    @contextmanager
    def cm():
        outer = P.stack
        P.stack = ExitStack()
        try:
            yield
            P.flush()
        finally:
            P.stack.close()
            P.stack = outer
    return cm()


def run_window(gens, width):
    gens = list(gens)
    active = []
    while gens or active:
        while gens and len(active) < width:
            active.append(gens.pop(0))
        for g_ in list(active):
            try:
                next(g_)
            except StopIteration:
                active.remove(g_)


def rots(P, name, shape, dtype, n):
    return Rot([P.sbuf("%s%d" % (name, i), shape, dtype) for i in range(n)])


def load_mod(P, G):
    G.modx = P.sbuf("modx", [128, 3 * D], F32)
    G.modc = P.sbuf("modc", [128, 3 * D], F32)
    P.dma(G.modx.ap(), G.MOD[0].partition_broadcast(128), reads=[G.MOD_r], writes=[G.modx])
    P.dma(G.modc.ap(), G.MOD[1].partition_broadcast(128), reads=[G.MOD_r], writes=[G.modc])


def ada_phase(P, G, L):
    with phase(P):
        G.modx = P.sbuf("modx", [128, 3 * D], F32)
        G.modc = P.sbuf("modc", [128, 3 * D], F32)
        ccol = P.sbuf("ccol", [128, 16], F32)
        P.dma(ccol.t[:, 0:8], G.c.rearrange("(k p) -> p k", p=128), writes=[ccol], allow_slow_non_contiguous=True)
        P.dma(ccol.t[:, 8:16], G.c_ctx.rearrange("(k p) -> p k", p=128), writes=[ccol],
              allow_slow_non_contiguous=True)
        sil = P.sbuf("sil", [128, 16], F32)
        ACT(P, sil, sil.ap(), ccol, ccol.ap(), AF.Silu)
        ones = P.sbuf("ones", [128, 128], F32)
        P.dve(lambda e: e.memset(ones.ap(), 1.0), writes=[ones])
        sbc = P.sbuf("sbc", [128, 16, 128], F32)
        for k in range(16):
            TS(P, "dve", sbc, sbc.t[:, k, :], ones, ones.ap(), sil.t[:, k:k + 1], None, ALU.mult, extra_r=[sil])
        bias = P.sbuf("adab", [128, 3 * D], F32)
        P.dma(bias.ap(), G.ada_b[L].partition_broadcast(128), writes=[bias])
        wrot = rots(P, "adaw", [128, 8, 512], F32, 2)
        for n in range(6):
            w = wrot.next()
            P.dma(w.ap(), G.ada_w[L][:, n * 512:(n + 1) * 512].rearrange("(k p) n -> p k n", p=128), writes=[w])
            for which, mod in ((0, G.modx), (1, G.modc)):
                ps = G.ps.next()
                for k in range(8):
                    MM(P, ps, ps.t[:, :], sbc, sbc.t[:, which * 8 + k, :], w, w.t[:, k, :], k == 0, k == 7)
                TT(P, "dve", mod, mod.t[:, n * 512:(n + 1) * 512], ps, ps.t[:, :], bias,
                   bias.t[:, n * 512:(n + 1) * 512], ALU.add)
        for mi, mod in enumerate((G.modx, G.modc)):
            TS(P, "dve", mod, mod.t[:, D:2 * D], mod, mod.t[:, D:2 * D], 1.0, None, ALU.add)
            P.dma(G.MOD[mi:mi + 1, :], mod.t[0:1, :], reads=[mod], writes=[G.MOD_r], q="pool")


def norm_block(P, G, W, blk, hT):
    t0, ntok = blk
    nt = ntok // 128

    def stage1(tl):
        tt = t0 // 128 + tl
        xt = W.xrot.next()
        sap, sres = G.xsrc(tt)
        P.dma(xt.ap(), sap, reads=[sres], writes=[xt])
        ss = W.ssrot.next()
        ACT(P, W.junk, W.junk.ap(), xt, xt.ap(), AF.Square, accum=ss.t[:, 0:1], extra_w=[ss])
        RSQRT(P, ss, ss.t[:, 0:1], 1.0 / D, EPS)
        return xt, ss

    def stage2(tl, xt, ss):
        tt = t0 // 128 + tl
        mod = G.modc if tt < 2 else G.modx
        tmp = W.tmprot.next()
        STT(P, "dve", tmp, tmp.ap(), xt, xt.ap(), ss.t[:, 0:1], mod, mod.t[:, D:2 * D], ALU.mult, ALU.mult,
            extra_r=[ss])
        h = W.hrot.next()
        TT(P, "dve", h, h.ap(), tmp, tmp.ap(), mod, mod.t[:, 0:D], ALU.add)
        tp = G.ps.next()
        tpb = tp.t[:].bitcast(BF16)
        for k in range(8):
            TR(P, tp, tpb[:, k * 128:(k + 1) * 128], h, h.t[:, k * 128:(k + 1) * 128], G.identb)
        CP(P, "act", hT, hT.t[:, :, tl * 128:(tl + 1) * 128], tp, tpb.rearrange("p (k t) -> p k t", k=8))

    pend = [stage1(0)]
    for tl in range(nt):
        if tl + 1 < nt:
            pend.append(stage1(tl + 1))
        xt, ss = pend.pop(0)
        stage2(tl, xt, ss)


def norm_work(P):
    W = G_()
    W.xrot = rots(P, "xt", [128, D], F32, 3)
    W.ssrot = rots(P, "ss", [128, 1], F32, 4)
    W.junk = P.sbuf("junk", [128, D], F32)
    W.tmprot = rots(P, "tmp", [128, D], F32, 2)
    W.hrot = rots(P, "h", [128, D], BF16, 2)
    W.hTrot = rots(P, "hT", [128, 8, 512], BF16, 3)
    return W


def qk_post(P, G, W, ps, psap, nh, gain, cs, ob):
    n = nh * 64
    sq = W.sq
    ACT(P, sq, sq.t[:, 0:n], ps, psap, AF.Square)
    ssh = W.sshrot.next()
    P.dve(lambda e: e.tensor_reduce(out=ssh.t[:, 0:nh], in_=sq.t[:, 0:n].rearrange("p (h d) -> p h d", h=nh),
                                    axis=AX.X, op=ALU.add), reads=[sq], writes=[ssh])
    RSQRT(P, ssh, ssh.t[:, 0:nh], 1.0 / 64, EPS)
    qg = W.qgrot.next()
    v3 = lambda t, ap: ap.rearrange("p (h d) -> p h d", h=nh)
    TT(P, "dve", qg, v3(qg, qg.t[:, 0:n]), ps, v3(ps, psap), gain,
       gain.t[:, 0:64].unsqueeze(1).to_broadcast([128, nh, 64]), ALU.mult)
    src = qg
    if cs is not None:
        ro = W.rorot.next()
        q4 = qg.t[:, 0:n].rearrange("p (h i two) -> p h i two", h=nh, two=2)
        r4 = ro.t[:, 0:n].rearrange("p (h i two) -> p h i two", h=nh, two=2)
        x0, x1 = q4[:, :, :, 0], q4[:, :, :, 1]
        cosb = cs.t[:, 0:32].unsqueeze(1).to_broadcast([128, nh, 32])
        sinb = cs.t[:, 32:64].unsqueeze(1).to_broadcast([128, nh, 32])
        ta, tb = W.ropet.next(), W.ropet.next()
        a3 = lambda t: t.t[:, 0:nh * 32].rearrange("p (h i) -> p h i", h=nh)
        TT(P, "dve", ta, a3(ta), qg, x0, cs, cosb, ALU.mult)
        TT(P, "dve", tb, a3(tb), qg, x1, cs, sinb, ALU.mult)
        TT(P, "dve", ro, r4[:, :, :, 0], ta, a3(ta), tb, a3(tb), ALU.subtract)
        tc_, td = W.ropet.next(), W.ropet.next()
        TT(P, "dve", tc_, a3(tc_), qg, x0, cs, sinb, ALU.mult)
        TT(P, "dve", td, a3(td), qg, x1, cs, cosb, ALU.mult)
        TT(P, "dve", ro, r4[:, :, :, 1], tc_, a3(tc_), td, a3(td), ALU.add)
        src = ro
    TT(P, "dve", ob, v3(ob, ob.t[:, 0:n]), src, v3(src, src.t[:, 0:n]), ssh,
       ssh.t[:, 0:nh].unsqueeze(2).to_broadcast([128, nh, 64]), ALU.mult)


def even_phase1(P, G, L, j):
    with phase(P):
        load_mod(P, G)
        W = norm_work(P)
        wbf = P.sbuf("winbf", [128, 8, 2304], BF16)
        stg = rots(P, "wstg", [128, 2304], F32, 2)
        for k in range(8):
            s = stg.next()
            P.dma(s.ap(), G.ev_w_in[j][k * 128:(k + 1) * 128, :], writes=[s])
            CP(P, "act", wbf, wbf.t[:, k, 0:512].rearrange("q (p r d) -> q p r d", p=4, r=2),
               s, s.t[:, 0:512].rearrange("q (r p d) -> q p r d", r=2, p=4))
            CP(P, "act", wbf, wbf.t[:, k, 512:2304], s, s.t[:, 512:2304])
        gq = P.sbuf("gq", [128, 64], F32)
        gk = P.sbuf("gk", [128, 64], F32)
        P.dma(gq.ap(), G.q_norm_w[j].partition_broadcast(128), writes=[gq])
        P.dma(gk.ap(), G.k_norm_w[j].partition_broadcast(128), writes=[gk])
        W.sq = P.sbuf("sq", [128, 512], F32)
        W.sshrot = rots(P, "ssh", [128, 8], F32, 4)
        W.qgrot = rots(P, "qg", [128, 512], F32, 2)
        W.rorot = rots(P, "ro", [128, 512], F32, 2)
        W.ropet = rots(P, "ropet", [128, 256], F32, 8)
        csrot = rots(P, "cs", [128, 64], F32, 2)
        gorot = rots(P, "go", [128, 512], BF16, 3)
        uorot = rots(P, "uo", [128, 512], F32, 2)
        varot = rots(P, "va", [128, 2, 192], BF16, 2)
        for va in varot.tiles:
            P.dve(lambda e, va=va: e.memset(va.ap(), 1.0), writes=[va])
        qbrot = rots(P, "qb", [128, 512], BF16, 2)
        kbrot = rots(P, "kb", [128, 128], BF16, 2)
        qsrot = rots(P, "qs", [128, 640], BF16, 2)
        qkrot = Rot(G.banks[0:4])
        ps_save = G.ps
        G.ps = Rot(G.banks[4:8])

        def fm(bi, blk, hT, f):
            t0, ntok = blk
            col = 768 + 128 * f
            ps = G.ps.next()
            for k in range(8):
                MM(P, ps, ps.t[:, 0:ntok], wbf, wbf.t[:, k, col:col + 128], hT, hT.t[:, k, 0:ntok], k == 0, k == 7)
            if f < 4 or f >= 8:
                gi = f if f < 4 else f - 4
                o = gorot.next()
                ACT(P, o, o.t[:, 0:ntok], ps, ps.t[:, 0:ntok], AF.Silu)
                P.dma(G.GT[gi, :, t0:t0 + ntok], o.t[:, 0:ntok], reads=[o], writes=[G.GT_r[gi][bi]], q="pool")
            else:
                ui = f - 4
                o = uorot.next()
                CP(P, "act", o, o.t[:, 0:ntok], ps, ps.t[:, 0:ntok])
                P.dma(G.UT[ui, :, t0:t0 + ntok], o.t[:, 0:ntok], reads=[o], writes=[G.UT_r[ui][bi]], q="pool")

        def stageM(blk, hT, tl):
            t0, ntok = blk
            tt = t0 // 128 + tl
            psq = qkrot.next()
            pskv = qkrot.next()
            for k in range(8):
                MM(P, psq, psq.t[:, 0:512], hT, hT.t[:, k, tl * 128:(tl + 1) * 128], wbf, wbf.t[:, k, 0:512],
                   k == 0, k == 7)
            for k in range(8):
                MM(P, pskv, pskv.t[:, 0:256], hT, hT.t[:, k, tl * 128:(tl + 1) * 128], wbf, wbf.t[:, k, 512:768],
                   k == 0, k == 7)
            return psq, pskv

        def stageQ(blk, tl, psq, pskv):
            t0, ntok = blk
            tt = t0 // 128 + tl
            va = varot.next()
            CP(P, "act", va, va.t[:, :, 64:128], pskv, pskv.t[:, 128:256].rearrange("p (r d) -> p r d", r=2))
            P.dma(G.VS[tt], va.ap(), reads=[va], writes=[G.VS_r[tt]], q="pool")
            cs = None
            if tt >= 2:
                cs = csrot.next()
                P.dma(cs.ap(), G.rope[(tt - 2) * 128:(tt - 1) * 128, :], writes=[cs])
            qb = qbrot.next()
            kb = kbrot.next()
            qk_post(P, G, W, psq, psq.t[:, 0:512], 8, gq, cs, qb)
            qk_post(P, G, W, pskv, pskv.t[:, 0:128], 2, gk, cs, kb)
            tp = G.ps.next()
            tpb = tp.t[:].bitcast(BF16)
            for p in range(4):
                TR(P, tp, tpb[:, p * 128:(p + 1) * 128], qb, qb.t[:, p * 128:(p + 1) * 128], G.identb)
            TR(P, tp, tpb[:, 512:640], kb, kb.t[:, 0:128], G.identb)
            qs = qsrot.next()
            CP(P, "dve", qs, qs.t[:, 0:640], tp, tpb[:, 0:640])
            P.dma(G.QT[:, :, tt * 128:(tt + 1) * 128], qs.t[:, 0:512].rearrange("p (a t) -> p a t", a=4),
                  reads=[qs], writes=[G.QT_r[tt]], q="pool")
            P.dma(G.KT[:, tt * 128:(tt + 1) * 128], qs.t[:, 512:640], reads=[qs], writes=[G.KT_r[tt]], q="pool")

        NB = len(G.blocks)
        hTs = {0: W.hTrot.next()}
        norm_block(P, G, W, G.blocks[0], hTs[0])
        for bi, blk in enumerate(G.blocks):
            hT = hTs.pop(bi)
            nt = blk[1] // 128
            fl = [list(range(0, 4)), list(range(4, 8)), list(range(8, 12))]
            live = {}
            for tl in range(min(2, nt)):
                live[tl] = stageM(blk, hT, tl)
            for step in range(max(nt, 3)):
                if step < 3:
                    for f in fl[step]:
                        fm(bi, blk, hT, f)
                if step < nt:
                    stageQ(blk, step, *live.pop(step))
                    if step + 2 < nt:
                        live[step + 2] = stageM(blk, hT, step + 2)
                if step == 0 and bi + 1 < NB:
                    hTs[bi + 1] = W.hTrot.next()
                    norm_block(P, G, W, G.blocks[bi + 1], hTs[bi + 1])
        G.ps = ps_save


def even_phase2(P, G, L, j):
    with phase(P):
        T, NT = G.T, G.NT
        KT = P.sbuf("KTs", [128, T], BF16)
        P.dma(KT.ap(), G.KT, reads=G.KT_r, writes=[KT])
        VA = P.sbuf("VAs", [128, NT, 2, 192], BF16)
        P.dma(VA.ap(), G.VS.rearrange("n p r c -> p n r c"), reads=G.VS_r, writes=[VA])
        SKEW = 2
        srot = Rot(G.banks[0:4])
        accrot = Rot(G.banks[4:8])
        qrot = rots(P, "qtb", [128, 8, 512], BF16, 2)
        for qt_ in qrot.tiles:
            P.pool(lambda e, qt_=qt_: e.memset(qt_.ap(), 0.0), writes=[qt_])
        ptrot = rots(P, "pt", [128, 512], BF16, 5)
        rrot = rots(P, "rr", [128, 512], F32, 2)
        mrot = rots(P, "mm", [128, 512], F32, 2)
        grot = rots(P, "gg", [128, 512], BF16, 2)
        orot = rots(P, "oo", [128, 512], BF16, 2)
        for bi, (t0, ntok) in enumerate(G.blocks):
            keys = [0, 1] if bi == 0 else list(range(NT))
            tts = list(range(t0 // 128, (t0 + ntok) // 128))
            QTb = qrot.next()
            for r_ in range(2):
                P.dma(QTb.t[r_ * 64:(r_ + 1) * 64, r_ * 4:(r_ + 1) * 4, 0:ntok],
                      G.QT[r_ * 64:(r_ + 1) * 64, :, t0:t0 + ntok], reads=[G.QT_r[tt] for tt in tts], writes=[QTb])
            for c in range(4):
                accs = []
                for hh in range(2):
                    h = 2 * c + hh
                    p, r = h % 4, h // 4
                    acc = accrot.next()
                    vsl = slice(64, 192) if hh == 0 else slice(0, 128)

                    def pv(pt, kt, acc=acc, r=r, vsl=vsl):
                        MM(P, acc, acc.t[:, 0:ntok], VA, VA.t[:, kt, r, vsl], pt, pt.t[:, 0:ntok],
                           kt == keys[0], kt == keys[-1])
                    pend = []
                    for kt in keys:
                        sps = srot.next()
                        MM(P, sps, sps.t[:, 0:ntok], KT, KT.t[:, kt * 128:(kt + 1) * 128],
                           QTb, QTb.t[:, h, 0:ntok], True, True)
                        if len(pend) >= SKEW:
                            pv(*pend.pop(0))
                        pt = ptrot.next()
                        ACT(P, pt, pt.t[:, 0:ntok], sps, sps.t[:, 0:ntok], AF.Exp, scale=0.125)
                        pend.append((pt, kt))
                    for pp_ in pend:
                        pv(*pp_)
                    accs.append(acc)
                A, B = accs
                Rr = rrot.next()
                P.dve(lambda e, Rr=Rr, A=A: e.reciprocal(out=Rr.t[0:64, 0:ntok], in_=A.t[64:128, 0:ntok]),
                      reads=[A], writes=[Rr])
                P.dve(lambda e, Rr=Rr, B=B: e.reciprocal(out=Rr.t[64:128, 0:ntok], in_=B.t[0:64, 0:ntok]),
                      reads=[B], writes=[Rr])
                M = mrot.next()
                TT(P, "dve", M, M.t[0:64, 0:ntok], A, A.t[0:64, 0:ntok], Rr, Rr.t[0:64, 0:ntok], ALU.mult)
                TT(P, "dve", M, M.t[64:128, 0:ntok], B, B.t[64:128, 0:ntok], Rr, Rr.t[64:128, 0:ntok], ALU.mult)
                g = grot.next()
                P.dma(g.t[:, 0:ntok], G.GT[c, :, t0:t0 + ntok], reads=[G.GT_r[c][bi]], writes=[g])
                o = orot.next()
                TT(P, "pool", o, o.t[:, 0:ntok], M, M.t[:, 0:ntok], g, g.t[:, 0:ntok], ALU.mult)
                P.dma(G.MIXT[c, :, t0:t0 + ntok], o.t[:, 0:ntok], reads=[o], writes=[G.MIXT_r[c][bi]], q="pool")


def even_phase3(P, G, L, j):
    T = G.T
    segs_f = [(a, a + n) for (a, n) in G.blocks]
    segs_r = [segs_f[0]] + segs_f[:0:-1]
    with phase(P):
        u = P.sbuf("ru", [128, T], F32)
        y = P.sbuf("ry", [128, T], F32)
        yb = P.sbuf("ryb", [128, T], BF16)
        H = P.sbuf("rH", [128, T], F32)
        HB = u
        wrot = rots(P, "rw", [128, 512], F32, 2)
        wrd = [rots(P, "rw%d_" % d_, [128, 512], F32, 10) for d_ in range(2)]
        cols = rots(P, "rcol", [128, 17], F32, 2)
        wstg = rots(P, "rwst", [128, 128], F32, 2)
        wbd = rots(P, "rwbd", [128, 128], BF16, 4)
        grot = rots(P, "rg", [128, 512], BF16, 2)
        orot = rots(P, "ro_", [128, 512], BF16, 2)
        for c in range(4):
            cs_ = slice(c * 128, (c + 1) * 128)
            P.dma(u.ap(), G.UT[c], reads=G.UT_r[c], writes=[u])
            col = cols.next()
            P.dma(col.t[:, 0:4], G.rg_conv_w[j][:, cs_].rearrange("k p -> p k"), writes=[col],
                  allow_slow_non_contiguous=True)
            P.dma(col.t[:, 4:5], G.rg_conv_b[j][cs_].rearrange("(p o) -> p o", o=1), writes=[col])
            for d in range(2):
                P.dma(col.t[:, 5 + d:6 + d], G.rg_ba[j, d].rearrange("h i -> (h i)")[cs_].rearrange("(p o) -> p o", o=1),
                      writes=[col])
                P.dma(col.t[:, 7 + d:8 + d], G.rg_bx[j, d].rearrange("h i -> (h i)")[cs_].rearrange("(p o) -> p o", o=1),
                      writes=[col])
                P.dma(col.t[:, 9 + d:10 + d], G.rg_lambda[j, d][cs_].rearrange("(p o) -> p o", o=1), writes=[col])
            ACT(P, col, col.t[:, 11:13], col, col.t[:, 9:11], AF.Exp, scale=-1.0)
            ACT(P, col, col.t[:, 11:13], col, col.t[:, 11:13], AF.Ln, bias=G.onecol.t[:, 0:1], extra_r=[G.onecol])
            TS(P, "dve", col, col.t[:, 11:13], col, col.t[:, 11:13], -8.0, None, ALU.mult)
            for (a, b) in (segs_f[0], (CTX, T)):
                TS(P, "dve", y, y.t[:, a:b], u, u.t[:, a:b], col.t[:, 2:3], col.t[:, 4:5], ALU.mult, ALU.add,
                   extra_r=[col])
                STT(P, "dve", y, y.t[:, a + 2:b], u, u.t[:, a:b - 2], col.t[:, 0:1], y, y.t[:, a + 2:b], ALU.mult,
                    ALU.add, extra_r=[col])
                STT(P, "dve", y, y.t[:, a + 1:b], u, u.t[:, a:b - 1], col.t[:, 1:2], y, y.t[:, a + 1:b], ALU.mult,
                    ALU.add, extra_r=[col])
                STT(P, "dve", y, y.t[:, a:b - 1], u, u.t[:, a + 1:b], col.t[:, 3:4], y, y.t[:, a:b - 1], ALU.mult,
                    ALU.add, extra_r=[col])
            CP(P, "pool", yb, yb.ap(), y, y.ap())
            TS(P, "dve", col, col.t[:, 13:17], col, col.t[:, 5:9], -1.0, None, ALU.mult)

            def dir_gen(d):
                wts = []
                for wsrc in (G.rg_wa, G.rg_wx):
                    st = wstg.next()
                    P.pool(lambda e, st=st: e.memset(st.ap(), 0.0), writes=[st])
                    for hb in range(2):
                        P.dma(st.t[hb * 64:(hb + 1) * 64, hb * 64:(hb + 1) * 64], wsrc[j, d, 2 * c + hb], writes=[st])
                    wb = wbd.next()
                    CP(P, "pool", wb, wb.ap(), st, st.ap())
                    wts.append(wb)
                yield
                wr = wrd[d]
                for (a, b) in (segs_f if d == 0 else segs_r):
                    n = b - a
                    psr, psi = G.ps.next(), G.ps.next()
                    MM(P, psr, psr.t[:, 0:n], wts[0], wts[0].ap(), yb, yb.t[:, a:b], True, True)
                    MM(P, psi, psi.t[:, 0:n], wts[1], wts[1].ap(), yb, yb.t[:, a:b], True, True)
                    rr, ii, aa, a2, bt = [wr.next() for _ in range(5)]
                    ACT(P, rr, rr.t[:, 0:n], psr, psr.t[:, 0:n], AF.Exp, scale=-1.0, bias=col.t[:, 13 + d:14 + d],
                        extra_r=[col])
                    ACT(P, ii, ii.t[:, 0:n], psi, psi.t[:, 0:n], AF.Exp, scale=-1.0, bias=col.t[:, 15 + d:16 + d],
                        extra_r=[col])
                    for t_ in (rr, ii):
                        ACT(P, t_, t_.t[:, 0:n], t_, t_.t[:, 0:n], AF.Ln, bias=G.onecol.t[:, 0:1], extra_r=[G.onecol])
                        ACT(P, t_, t_.t[:, 0:n], t_, t_.t[:, 0:n], AF.Exp, scale=-1.0)
                    ACT(P, aa, aa.t[:, 0:n], rr, rr.t[:, 0:n], AF.Exp, scale=col.t[:, 11 + d:12 + d], extra_r=[col])
                    TT(P, "pool", a2, a2.t[:, 0:n], aa, aa.t[:, 0:n], aa, aa.t[:, 0:n], ALU.mult)
                    TS(P, "pool", a2, a2.t[:, 0:n], a2, a2.t[:, 0:n], -1.0, 1.0, ALU.mult, ALU.add)
                    yield
                    ACT(P, a2, a2.t[:, 0:n], a2, a2.t[:, 0:n], AF.Ln)
                    ACT(P, a2, a2.t[:, 0:n], a2, a2.t[:, 0:n], AF.Exp, scale=0.5)
                    TT(P, "dve", bt, bt.t[:, 0:n], ii, ii.t[:, 0:n], y, y.t[:, a:b], ALU.mult)
                    TT(P, "dve", bt, bt.t[:, 0:n], bt, bt.t[:, 0:n], a2, a2.t[:, 0:n], ALU.mult)
                    if d == 0:
                        init = 0.0 if a == 0 else H.t[:, a - 1:a]
                        P.dve(lambda e, a=a, b=b, n=n, aa=aa, bt=bt, init=init: e.tensor_tensor_scan(
                            out=H.t[:, a:b], data0=aa.t[:, 0:n], data1=bt.t[:, 0:n], initial=init,
                            op0=ALU.mult, op1=ALU.add), reads=[aa, bt, H], writes=[H])
                    else:
                        if a == 0:
                            init = 0.0
                        elif b == T:
                            init = HB.t[:, 0:1]
                        else:
                            init = HB.t[:, b:b + 1]
                        P.dve(lambda e, a=a, b=b, n=n, aa=aa, bt=bt, init=init: e.tensor_tensor_scan(
                            out=HB.t[:, a:b][:, ::-1], data0=aa.t[:, 0:n][:, ::-1], data1=bt.t[:, 0:n][:, ::-1],
                            initial=init, op0=ALU.mult, op1=ALU.add), reads=[aa, bt, HB], writes=[HB])
                    yield

            run_window([dir_gen(0), dir_gen(1)], 2)
            for bi, (t0, ntok) in enumerate(G.blocks):
                sm = wrot.next()
                TT(P, "dve", sm, sm.t[:, 0:ntok], H, H.t[:, t0:t0 + ntok], HB, HB.t[:, t0:t0 + ntok], ALU.add)
                g = grot.next()
                P.dma(g.t[:, 0:ntok], G.GT[4 + c, :, t0:t0 + ntok], reads=[G.GT_r[4 + c][bi]], writes=[g])
                o = orot.next()
                TT(P, "pool", o, o.t[:, 0:ntok], sm, sm.t[:, 0:ntok], g, g.t[:, 0:ntok], ALU.mult)
                P.dma(G.MIXT[4 + c, :, t0:t0 + ntok], o.t[:, 0:ntok], reads=[o], writes=[G.MIXT_r[4 + c][bi]], q="pool")


def odd_phase1(P, G, L, j):
    with phase(P):
        load_mod(P, G)
        W = norm_work(P)
        wbf = P.sbuf("owin", [128, 8, 2048], BF16)
        stg = rots(P, "owstg", [128, 2048], F32, 2)
        for k in range(8):
            s = stg.next()
            P.dma(s.ap(), G.od_w_in[j][k * 128:(k + 1) * 128, :], writes=[s])
            CP(P, "act", wbf, wbf.ap()[:, k, :], s, s.ap())
        gorot = rots(P, "ogo", [128, 512], BF16, 3)
        uorot = rots(P, "ouo", [128, 512], F32, 3)
        def blk_gen(bi, blk):
            t0, ntok = blk
            hT = W.hTrot.next()
            norm_block(P, G, W, blk, hT)
            yield
            for f in range(16):
                if f % 4 == 0 and f > 0:
                    yield
                ps = G.ps.next()
                for k in range(8):
                    MM(P, ps, ps.t[:, 0:ntok], wbf, wbf.t[:, k, f * 128:(f + 1) * 128], hT, hT.t[:, k, 0:ntok],
                       k == 0, k == 7)
                if f < 8:
                    o = uorot.next()
                    nch = ntok // 8
                    CP(P, "act", o, o.t[:, 0:ntok].rearrange("p (i c) -> p i c", i=8), ps,
                       ps.t[:, 0:ntok].rearrange("p (c i) -> p i c", i=8))
                    P.dma(G.UO[f, :, :, t0 // 8:t0 // 8 + nch], o.t[:, 0:ntok].rearrange("p (i c) -> p i c", i=8),
                          reads=[o], writes=[G.UO_r[f][bi]], q="pool")
                else:
                    o = gorot.next()
                    ACT(P, o, o.t[:, 0:ntok], ps, ps.t[:, 0:ntok], AF.Silu)
                    P.dma(G.GT[f - 8, :, t0:t0 + ntok], o.t[:, 0:ntok], reads=[o], writes=[G.GT_r[f - 8][bi]], q="pool")

        run_window([blk_gen(bi, blk) for bi, blk in enumerate(G.blocks)], 2)


def odd_phase2(P, G, L, j):
    S = G.S
    NX = S // 8
    NCH = 32 + NX
    NS = min(512, NX)
    XB = 34
    segs_x = [(a, min(a + NS, NX)) for a in range(0, NX, NS)]
    seg_c = (0, 32, 0)
    segs_f = [seg_c] + [(32 + a, 32 + b, XB + a) for (a, b) in segs_x]
    segs_r = [seg_c] + [(32 + a, 32 + b, XB + a) for (a, b) in segs_x[::-1]]
    NLV = 13
    with phase(P):
        lre = P.sbuf("s_lre", [128, 64], F32)
        lim = P.sbuf("s_lim", [128, 64], F32)
        dtt = P.sbuf("s_dt", [128, 64], F32)
        for d in range(2):
            for h in range(2):
                dst = (slice(h * 64, (h + 1) * 64), slice(d * 32, (d + 1) * 32))
                P.dma(lre.t[dst], G.s5_lambda_re[j, d].rearrange("(q h) p -> h p q", h=2)[h], writes=[lre],
                      allow_slow_non_contiguous=True)
                P.dma(lim.t[dst], G.s5_lambda_im[j, d].rearrange("(q h) p -> h p q", h=2)[h], writes=[lim],
                      allow_slow_non_contiguous=True)
                P.dma(dtt.t[dst], G.s5_log_step[j, d].rearrange("(q h) -> h q", h=2)[h].partition_broadcast(64),
                      writes=[dtt], allow_slow_non_contiguous=True)
        ACT(P, dtt, dtt.ap(), dtt, dtt.ap(), AF.Exp)
        mag = P.sbuf("s_mag", [128, 64], F32)
        th = P.sbuf("s_th", [128, 64], F32)
        xx = P.sbuf("s_xx", [128, 64], F32)
        TT(P, "dve", xx, xx.ap(), lre, lre.ap(), dtt, dtt.ap(), ALU.mult)
        TS(P, "dve", mag, mag.ap(), xx, xx.ap(), 1.0 / 720, None, ALU.mult)
        for cf in (1.0 / 120, 1.0 / 24, 1.0 / 6, 0.5, 1.0):
            STT(P, "dve", mag, mag.ap(), mag, mag.ap(), cf, xx, xx.ap(), ALU.add, ALU.mult)
        TS(P, "dve", mag, mag.ap(), mag, mag.ap(), 1.0, None, ALU.add)
        TT(P, "dve", th, th.ap(), lim, lim.ap(), dtt, dtt.ap(), ALU.mult)
        cc = P.sbuf("s_cc", [128, 64], F32)
        sn = P.sbuf("s_sn", [128, 64], F32)
        t1 = P.sbuf("s_t1", [128, 64], F32)
        t2 = P.sbuf("s_t2", [128, 64], F32)
        ACT(P, sn, sn.ap(), th, th.ap(), AF.Sin, scale=1.0 / 64)
        ACT(P, cc, cc.ap(), th, th.ap(), AF.Sin, scale=1.0 / 64, bias=G.hpicol.t[:, 0:1], extra_r=[G.hpicol])

        def renorm(c_t, c_ap, s_t, s_ap):
            TT(P, "dve", t1, t1.ap(), c_t, c_ap, c_t, c_ap, ALU.mult)
            TT(P, "dve", t2, t2.ap(), s_t, s_ap, s_t, s_ap, ALU.mult)
            TT(P, "dve", t1, t1.ap(), t1, t1.ap(), t2, t2.ap(), ALU.add)
            TS(P, "dve", t1, t1.ap(), t1, t1.ap(), -0.5, 1.5, ALU.mult, ALU.add)
            TT(P, "dve", c_t, c_ap, c_t, c_ap, t1, t1.ap(), ALU.mult)
            TT(P, "dve", s_t, s_ap, s_t, s_ap, t1, t1.ap(), ALU.mult)

        def square_cis(c_t, c_ap, s_t, s_ap, oc_t, oc_ap, os_t, os_ap):
            TT(P, "dve", t1, t1.ap(), c_t, c_ap, c_t, c_ap, ALU.mult)
            TT(P, "dve", t2, t2.ap(), s_t, s_ap, s_t, s_ap, ALU.mult)
            STT(P, "dve", os_t, os_ap, c_t, c_ap, 2.0, s_t, s_ap, ALU.mult, ALU.mult)
            TT(P, "dve", oc_t, oc_ap, t1, t1.ap(), t2, t2.ap(), ALU.subtract)
            renorm(oc_t, oc_ap, os_t, os_ap)
        c2 = P.sbuf("s_c2", [128, 64], F32)
        s2 = P.sbuf("s_s2", [128, 64], F32)
        cur = (cc, sn)
        nxt = (c2, s2)
        for _ in range(6):
            square_cis(cur[0], cur[0].ap(), cur[1], cur[1].ap(), nxt[0], nxt[0].ap(), nxt[1], nxt[1].ap())
            cur, nxt = nxt, cur
        cth, sth = cur
        pwc = P.sbuf("s_pwc", [128, NLV, 64], F32)
        pws = P.sbuf("s_pws", [128, NLV, 64], F32)
        npws = P.sbuf("s_npws", [128, NLV, 64], F32)
        CP(P, "dve", pwc, pwc.t[:, 0, :], cth, cth.ap())
        CP(P, "dve", pws, pws.t[:, 0, :], sth, sth.ap())
        for k in range(NLV - 1):
            square_cis(pwc, pwc.t[:, k, :], pws, pws.t[:, k, :], pwc, pwc.t[:, k + 1, :], pws, pws.t[:, k + 1, :])
        TS(P, "dve", npws, npws.ap(), pws, pws.ap(), -1.0, None, ALU.mult)
        PR = P.sbuf("s_PR", [128, 9, 64], F32)
        PI = P.sbuf("s_PI", [128, 9, 64], F32)
        NPR = P.sbuf("s_NPR", [128, 9, 64], F32)
        NPI = P.sbuf("s_NPI", [128, 9, 64], F32)
        are = P.sbuf("s_are", [128, 64], F32)
        aim = P.sbuf("s_aim", [128, 64], F32)
        TT(P, "dve", are, are.ap(), mag, mag.ap(), cth, cth.ap(), ALU.mult)
        TT(P, "dve", aim, aim.ap(), mag, mag.ap(), sth, sth.ap(), ALU.mult)
        P.dve(lambda e: e.memset(PR.t[:, 0, :], 1.0), writes=[PR])
        P.dve(lambda e: e.memset(PI.t[:, 0, :], 0.0), writes=[PI])
        for tau in range(8):
            TT(P, "dve", t1, t1.ap(), PR, PR.t[:, tau, :], are, are.ap(), ALU.mult)
            TT(P, "dve", t2, t2.ap(), PI, PI.t[:, tau, :], aim, aim.ap(), ALU.mult)
            TT(P, "dve", PR, PR.t[:, tau + 1, :], t1, t1.ap(), t2, t2.ap(), ALU.subtract)
            TT(P, "dve", t1, t1.ap(), PR, PR.t[:, tau, :], aim, aim.ap(), ALU.mult)
            TT(P, "dve", t2, t2.ap(), PI, PI.t[:, tau, :], are, are.ap(), ALU.mult)
            TT(P, "dve", PI, PI.t[:, tau + 1, :], t1, t1.ap(), t2, t2.ap(), ALU.add)
        TS(P, "dve", NPR, NPR.ap(), PR, PR.ap(), -1.0, None, ALU.mult)
        TS(P, "dve", NPI, NPI.ap(), PI, PI.ap(), -1.0, None, ALU.mult)
        mag8 = P.sbuf("s_mag8", [128, 64], F32)
        TT(P, "dve", mag8, mag8.ap(), mag, mag.ap(), mag, mag.ap(), ALU.mult)
        TT(P, "dve", mag8, mag8.ap(), mag8, mag8.ap(), mag8, mag8.ap(), ALU.mult)
        TT(P, "dve", mag8, mag8.ap(), mag8, mag8.ap(), mag8, mag8.ap(), ALU.mult)
        nr = P.sbuf("s_nr", [128, 64], F32)
        TS(P, "dve", nr, nr.ap(), are, are.ap(), -1.0, None, ALU.add)
        den = P.sbuf("s_den", [128, 64], F32)
        TT(P, "dve", den, den.ap(), lre, lre.ap(), lre, lre.ap(), ALU.mult)
        TT(P, "dve", t1, t1.ap(), lim, lim.ap(), lim, lim.ap(), ALU.mult)
        TT(P, "dve", den, den.ap(), den, den.ap(), t1, t1.ap(), ALU.add)
        P.dve(lambda e: e.reciprocal(out=den.ap(), in_=den.ap()), reads=[den], writes=[den])
        fre = P.sbuf("s_fre", [128, 64], F32)
        fim = P.sbuf("s_fim", [128, 64], F32)
        nfim = P.sbuf("s_nfim", [128, 64], F32)
        TT(P, "dve", t1, t1.ap(), nr, nr.ap(), lre, lre.ap(), ALU.mult)
        TT(P, "dve", t2, t2.ap(), aim, aim.ap(), lim, lim.ap(), ALU.mult)
        TT(P, "dve", fre, fre.ap(), t1, t1.ap(), t2, t2.ap(), ALU.add)
        TT(P, "dve", fre, fre.ap(), fre, fre.ap(), den, den.ap(), ALU.mult)
        TT(P, "dve", t1, t1.ap(), aim, aim.ap(), lre, lre.ap(), ALU.mult)
        TT(P, "dve", t2, t2.ap(), nr, nr.ap(), lim, lim.ap(), ALU.mult)
        TT(P, "dve", fim, fim.ap(), t1, t1.ap(), t2, t2.ap(), ALU.subtract)
        TT(P, "dve", fim, fim.ap(), fim, fim.ap(), den, den.ap(), ALU.mult)
        TS(P, "dve", nfim, nfim.ap(), fim, fim.ap(), -1.0, None, ALU.mult)

        ud = P.sbuf("s_ud", [128, 8, NCH], F32)
        ubd = P.sbuf("s_ubd", [128, 8, NCH], BF16)
        dcol = rots(P, "s_dcol", [128, 1], F32, 2)
        kacc = P.sbuf("s_kacc", [128, 15, 128], F32)
        kkb = P.sbuf("s_kkb", [128, 15, 128], BF16)
        ecos = rots(P, "s_ecos", [128, NS], F32, 2)
        esin = rots(P, "s_esin", [128, NS], F32, 2)
        rho = rots(P, "s_rho", [128, NS], F32, 2)
        braw = rots(P, "s_braw", [128, 2, 32], F32, 2)
        bbt = rots(P, "s_bbt", [128, 2, 32], F32, 2)
        craw = rots(P, "s_craw", [32, 2, 128], F32, 2)
        ctt = rots(P, "s_ct", [128, 2, 32], F32, 2)
        tmpA = rots(P, "s_tmpA", [128, 9, 32], F32, 8)
        baf = rots(P, "s_baf", [128, 8, 2, 32], BF16, 2)
        caf = rots(P, "s_caf", [128, 9, 2, 32], F32, 2)
        om = rots(P, "s_om", [128, 8, 2, 128], BF16, 2)
        ccp = rots(P, "s_ccp", [128, 9, 2, 128], BF16, 4)
        hsr = rots(P, "s_hs", [128, 2, NCH + 4], BF16, 4)
        wkd = [rots(P, "s_wk%d_" % d_, [128, NS], F32, 8) for d_ in range(2)]
        gl = rots(P, "s_gl", [128, 2], F32, 8)
        ygr = rots(P, "s_yg", [128, 512], BF16, 2)
        ygt = rots(P, "s_ygt", [128, 512], F32, 2)

        def lvl(n):
            k = n.bit_length() - 1
            assert (1 << k) == n
            return k + 3

        for c in range(8):
            P.dma(ud.ap(), G.UO[c], reads=G.UO_r[c], writes=[ud])
            CP(P, "act", ubd, ubd.ap(), ud, ud.ap())
            dc = dcol.next()
            P.dma(dc.ap(), G.s5_d[j][c * 128:(c + 1) * 128].rearrange("(p o) -> p o", o=1), writes=[dc])
            TS(P, "dve", ud, ud.ap(), ud, ud.ap(), dc.t[:, 0:1], None, ALU.mult, extra_r=[dc])
            P.pool(lambda e: e.memset(kacc.ap(), 0.0), writes=[kacc])
            def stream_gen(q, d, outl):
                gq_ = 8 * c + 2 * q
                pq = 4 * c + q
                rows = slice(32 * q, 32 * q + 32)
                colx = d * 32 + pq
                br = braw.next()
                P.pool(lambda e, br=br: e.memset(br.ap(), 0.0), writes=[br])
                for h in range(2):
                    P.dma(br.t[h * 64:(h + 1) * 64, 0, h * 16:(h + 1) * 16], G.s5_b_re[j, d, gq_ + h], writes=[br])
                    P.dma(br.t[h * 64:(h + 1) * 64, 1, h * 16:(h + 1) * 16], G.s5_b_im[j, d, gq_ + h], writes=[br])
                bt_ = bbt.next()
                frc, fic, nfic = fre.t[:, colx:colx + 1], fim.t[:, colx:colx + 1], nfim.t[:, colx:colx + 1]
                TS(P, "dve", bt_, bt_.t[:, 0, :], br, br.t[:, 0, :], frc, None, ALU.mult, extra_r=[fre])
                STT(P, "dve", bt_, bt_.t[:, 0, :], br, br.t[:, 1, :], nfic, bt_, bt_.t[:, 0, :], ALU.mult, ALU.add,
                    extra_r=[nfim])
                TS(P, "dve", bt_, bt_.t[:, 1, :], br, br.t[:, 1, :], frc, None, ALU.mult, extra_r=[fre])
                STT(P, "dve", bt_, bt_.t[:, 1, :], br, br.t[:, 0, :], fic, bt_, bt_.t[:, 1, :], ALU.mult, ALU.add,
                    extra_r=[fim])
                cr = craw.next()
                P.pool(lambda e, cr=cr: e.memset(cr.ap(), 0.0), writes=[cr])
                for h in range(2):
                    P.dma(cr.t[h * 16:(h + 1) * 16, 0, h * 64:(h + 1) * 64], G.s5_c_re[j, d, gq_ + h], writes=[cr])
                    P.dma(cr.t[h * 16:(h + 1) * 16, 1, h * 64:(h + 1) * 64], G.s5_c_im[j, d, gq_ + h], writes=[cr])
                tp2 = G.ps.next()
                TR(P, tp2, tp2.t[:, 0:32], cr, cr.t[:, 0, :], G.identf)
                TR(P, tp2, tp2.t[:, 32:64], cr, cr.t[:, 1, :], G.identf)
                ct = ctt.next()
                CP(P, "act", ct, ct.ap(), tp2, tp2.t[:, 0:64].rearrange("p (a m) -> p a m", a=2))
                def bc_p(tab, n):
                    return tab.t[:, 0:n, colx:colx + 1].to_broadcast([128, n, 32])

                def bc_x(t, a, n):
                    return t.t[:, a, :].unsqueeze(1).to_broadcast([128, n, 32])
                ba = baf.next()
                ca = caf.next()
                x1, x2 = tmpA.next(), tmpA.next()
                TT(P, "dve", x1, x1.t[:, 0:8, :], bt_, bc_x(bt_, 0, 8), PR, bc_p(PR, 8), ALU.mult)
                TT(P, "dve", x2, x2.t[:, 0:8, :], bt_, bc_x(bt_, 1, 8), PI, bc_p(PI, 8), ALU.mult)
                TT(P, "dve", ba, ba.t[:, :, 0, :], x1, x1.t[:, 0:8, :], x2, x2.t[:, 0:8, :], ALU.subtract)
                x3, x4 = tmpA.next(), tmpA.next()
                TT(P, "pool", x3, x3.t[:, 0:8, :], bt_, bc_x(bt_, 0, 8), PI, bc_p(PI, 8), ALU.mult)
                TT(P, "pool", x4, x4.t[:, 0:8, :], bt_, bc_x(bt_, 1, 8), PR, bc_p(PR, 8), ALU.mult)
                TT(P, "pool", ba, ba.t[:, :, 1, :], x3, x3.t[:, 0:8, :], x4, x4.t[:, 0:8, :], ALU.add)
                y1, y2 = tmpA.next(), tmpA.next()
                TT(P, "dve", y1, y1.ap(), ct, bc_x(ct, 0, 9), PR, bc_p(PR, 9), ALU.mult)
                TT(P, "dve", y2, y2.ap(), ct, bc_x(ct, 1, 9), PI, bc_p(PI, 9), ALU.mult)
                TT(P, "dve", ca, ca.t[:, :, 0, :], y1, y1.ap(), y2, y2.ap(), ALU.subtract)
                y3, y4 = tmpA.next(), tmpA.next()
                TT(P, "pool", y3, y3.ap(), ct, bc_x(ct, 0, 9), NPI, bc_p(NPI, 9), ALU.mult)
                TT(P, "pool", y4, y4.ap(), ct, bc_x(ct, 1, 9), NPR, bc_p(NPR, 9), ALU.mult)
                TT(P, "pool", ca, ca.t[:, :, 1, :], y3, y3.ap(), y4, y4.ap(), ALU.add)
                omt = om.next()
                P.act(lambda e, omt=omt: e.memzero(omt.ap()), writes=[omt])
                for half in range(2):
                    tpo = G.ps.next()
                    tpb = tpo.t[:].bitcast(BF16)
                    for tt_ in range(4):
                        tau = half * 4 + tt_
                        for a_ in range(2):
                            TR(P, tpo, tpb[0:32, (tt_ * 2 + a_) * 128:(tt_ * 2 + a_ + 1) * 128], ba, ba.t[:, tau, a_, :],
                               G.identb)
                    CP(P, "act", omt, omt.t[rows, half * 4:(half + 1) * 4, :, :], tpo,
                       tpb[0:32, 0:1024].rearrange("p (t a m) -> p t a m", t=4, a=2))
                cpt = ccp.next()
                P.act(lambda e, cpt=cpt: e.memzero(cpt.ap()), writes=[cpt])
                CP(P, "pool", cpt, cpt.t[:, :, :, rows], ca, ca.ap())
                psk = G.ps.next()
                for tau in range(8):
                    sl = slice(tau * 32, (tau + 1) * 32)
                    MM(P, psk, psk.t[0:32, sl], bt_, bt_.t[:, 0, :], ca, ca.t[:, tau, 0, :], tau == 0, False)
                for tau in range(8):
                    sl = slice(tau * 32, (tau + 1) * 32)
                    MM(P, psk, psk.t[0:32, sl], bt_, bt_.t[:, 1, :], ca, ca.t[:, tau, 1, :], False, tau == 7)
                if d == 0:
                    CP(P, "act", kacc, kacc.t[rows, 0, rows], psk, psk.t[0:32, 0:32])
                else:
                    TT(P, "dve", kacc, kacc.t[rows, 0, rows], psk, psk.t[0:32, 0:32], kacc, kacc.t[rows, 0, rows],
                       ALU.add)
                kb = 1 + 7 * d
                CP(P, "act", kacc, kacc.t[rows, kb:kb + 7, rows], psk,
                   psk.t[0:32, 32:256].rearrange("p (t m) -> p t m", t=7))
                yield
                ec, es = ecos.next(), esin.next()
                P.dve(lambda e, ec=ec: e.memset(ec.t[:, 0:1], 1.0), writes=[ec])
                P.dve(lambda e, es=es: e.memset(es.t[:, 0:1], 0.0), writes=[es])
                m = 1
                k = 3
                while m < NS:
                    ck, sk = pwc.t[:, k, colx:colx + 1], pws.t[:, k, colx:colx + 1]
                    nsk = npws.t[:, k, colx:colx + 1]
                    w1, w2 = wkd[d].next(), wkd[d].next()
                    w3 = wkd[d].next()
                    if m < 32:
                        TS(P, "dve", w1, w1.t[:, 0:m], ec, ec.t[:, 0:m], ck, None, ALU.mult, extra_r=[pwc])
                        TS(P, "dve", w2, w2.t[:, 0:m], es, es.t[:, 0:m], ck, None, ALU.mult, extra_r=[pwc])
                        TS(P, "dve", w3, w3.t[:, 0:m], ec, ec.t[:, 0:m], sk, None, ALU.mult, extra_r=[pws])
                    else:
                        ACT(P, w1, w1.t[:, 0:m], ec, ec.t[:, 0:m], AF.Copy, scale=ck, extra_r=[pwc])
                        ACT(P, w2, w2.t[:, 0:m], es, es.t[:, 0:m], AF.Copy, scale=ck, extra_r=[pwc])
                        ACT(P, w3, w3.t[:, 0:m], ec, ec.t[:, 0:m], AF.Copy, scale=sk, extra_r=[pws])
                    STT(P, "dve", ec, ec.t[:, m:2 * m], es, es.t[:, 0:m], nsk, w1, w1.t[:, 0:m], ALU.mult, ALU.add,
                        extra_r=[npws])
                    TT(P, "dve", es, es.t[:, m:2 * m], w2, w2.t[:, 0:m], w3, w3.t[:, 0:m], ALU.add)
                    m *= 2
                    k += 1
                rh = rho.next()
                ACT(P, rh, rh.ap(), G.ones512, G.ones512.t[:, 0:NS], AF.Copy, scale=mag8.t[:, colx:colx + 1],
                    extra_r=[mag8])
                yield
                hs = hsr.next()
                P.act(lambda e, hs=hs: e.memzero(hs.ap()), writes=[hs])
                prev = None
                for si, (a, b, hcol) in enumerate(segs_f if d == 0 else segs_r):
                    if si > 0:
                        yield
                    n = b - a
                    rv = (lambda ap: ap) if d == 0 else (lambda ap: ap[:, ::-1])
                    pvr, pvi = G.ps.next(), G.ps.next()
                    for ip in range(8):
                        tau = 7 - ip if d == 0 else ip
                        MM(P, pvr, pvr.t[:, 0:n], omt, omt.t[:, tau, 0, :], ubd, ubd.t[:, ip, a:b], ip == 0, ip == 7)
                    for ip in range(8):
                        tau = 7 - ip if d == 0 else ip
                        MM(P, pvi, pvi.t[:, 0:n], omt, omt.t[:, tau, 1, :], ubd, ubd.t[:, ip, a:b], ip == 0, ip == 7)
                    m1, m2, m3, m4, gre, gim = [wkd[d].next() for _ in range(6)]
                    wre, wim = m1, m3
                    TT(P, "dve", m1, m1.t[:, 0:n], pvr, rv(pvr.t[:, 0:n]), ec, ec.t[:, 0:n], ALU.mult)
                    TT(P, "dve", m2, m2.t[:, 0:n], pvi, rv(pvi.t[:, 0:n]), es, es.t[:, 0:n], ALU.mult)
                    TT(P, "pool", wre, wre.t[:, 0:n], m1, m1.t[:, 0:n], m2, m2.t[:, 0:n], ALU.add)
                    TT(P, "dve", m3, m3.t[:, 0:n], pvi, rv(pvi.t[:, 0:n]), ec, ec.t[:, 0:n], ALU.mult)
                    TT(P, "dve", m4, m4.t[:, 0:n], pvr, rv(pvr.t[:, 0:n]), es, es.t[:, 0:n], ALU.mult)
                    TT(P, "pool", wim, wim.t[:, 0:n], m3, m3.t[:, 0:n], m4, m4.t[:, 0:n], ALU.subtract)
                    yield
                    if prev is None:
                        ire, iim = 0.0, 0.0
                        init_r = []
                    else:
                        pre, pim, pn = prev
                        kk = lvl(pn)
                        ck, sk = pwc.t[:, kk, colx:colx + 1], pws.t[:, kk, colx:colx + 1]
                        g_ = gl.next()
                        t_ = gl.next()
                        lr, li = pre.t[:, pn - 1:pn], pim.t[:, pn - 1:pn]
                        TS(P, "dve", t_, t_.t[:, 0:1], pre, lr, ck, None, ALU.mult, extra_r=[pwc])
                        TS(P, "dve", t_, t_.t[:, 1:2], pim, li, ck, None, ALU.mult, extra_r=[pwc])
                        TS(P, "dve", g_, g_.t[:, 0:1], pim, li, sk, None, ALU.mult, extra_r=[pws])
                        TT(P, "dve", g_, g_.t[:, 0:1], t_, t_.t[:, 0:1], g_, g_.t[:, 0:1], ALU.subtract)
                        STT(P, "dve", g_, g_.t[:, 1:2], pre, lr, sk, t_, t_.t[:, 1:2], ALU.mult, ALU.add,
                            extra_r=[pws])
                        ire, iim = g_.t[:, 0:1], g_.t[:, 1:2]
                        init_r = [g_]
                    P.dve(lambda e, gre=gre, wre=wre, rh=rh, ire=ire, n=n: e.tensor_tensor_scan(
                        out=gre.t[:, 0:n], data0=rh.t[:, 0:n], data1=wre.t[:, 0:n], initial=ire,
                        op0=ALU.mult, op1=ALU.add), reads=[rh, wre] + init_r, writes=[gre])
                    P.dve(lambda e, gim=gim, wim=wim, rh=rh, iim=iim, n=n: e.tensor_tensor_scan(
                        out=gim.t[:, 0:n], data0=rh.t[:, 0:n], data1=wim.t[:, 0:n], initial=iim,
                        op0=ALU.mult, op1=ALU.add), reads=[rh, wim] + init_r, writes=[gim])
                    prev = (gre, gim, n)
                    yield
                    o1, o2, o3, o4 = m2, m4, m1, m3
                    off = hcol + (1 if d == 0 else 0)
                    TT(P, "pool", o1, o1.t[:, 0:n], gre, gre.t[:, 0:n], ec, ec.t[:, 0:n], ALU.mult)
                    TT(P, "pool", o2, o2.t[:, 0:n], gim, gim.t[:, 0:n], es, es.t[:, 0:n], ALU.mult)
                    TT(P, "dve", hs, rv(hs.t[:, 0, off:off + n]), o1, o1.t[:, 0:n], o2, o2.t[:, 0:n], ALU.subtract)
                    TT(P, "pool", o3, o3.t[:, 0:n], gre, gre.t[:, 0:n], es, es.t[:, 0:n], ALU.mult)
                    TT(P, "pool", o4, o4.t[:, 0:n], gim, gim.t[:, 0:n], ec, ec.t[:, 0:n], ALU.mult)
                    TT(P, "dve", hs, rv(hs.t[:, 1, off:off + n]), o3, o3.t[:, 0:n], o4, o4.t[:, 0:n], ALU.add)
                    if si == 0:
                        if d == 0:
                            CP(P, "dve", hs, hs.t[:, :, XB:XB + 1], hs, hs.t[:, :, 32:33])
                        else:
                            CP(P, "dve", hs, hs.t[:, :, XB + NX:XB + NX + 1], hs, hs.t[:, :, 0:1])
                outl.append((hs, cpt))

            def gelu_blocks(bis):
                for bi in bis:
                    t0, ntok = G.blocks[bi]
                    nch = ntok // 8
                    ysl = ud.t[:, :, t0 // 8:t0 // 8 + nch]
                    g1, g2 = ygt.next(), ygt.next()
                    v3 = lambda t: t.t[:, 0:ntok].rearrange("p (i c) -> p i c", i=8)
                    ACT(P, g1, v3(g1), ud, ysl, AF.Square)
                    ACT(P, g1, g1.t[:, 0:ntok], g1, g1.t[:, 0:ntok], AF.Identity, scale=0.044715,
                        bias=G.onecol.t[:, 0:1], extra_r=[G.onecol])
                    TT(P, "pool", g1, v3(g1), g1, v3(g1), ud, ysl, ALU.mult)
                    ACT(P, g2, g2.t[:, 0:ntok], g1, g1.t[:, 0:ntok], AF.Sigmoid, scale=1.5957691216057308)
                    o = ygr.next()
                    TT(P, "dve", o, o.t[:, 0:ntok].rearrange("p (c i) -> p i c", i=8), g2, v3(g2), ud, ysl, ALU.mult)
                    P.dma(G.YG[c, :, t0:t0 + ntok], o.t[:, 0:ntok], reads=[o], writes=[G.YG_r[c][bi]], q="pool")

            def stageB_gen(q, streams):
                if q == 3:
                    CP(P, "pool", kkb, kkb.ap(), kacc, kacc.ap())
                order = segs_f if q < 3 else segs_f[1:] + segs_f[:1]
                for (a, b, hcol) in order:
                    n = b - a
                    for i in range(8):
                        py = G.ps.next()
                        mms = []
                        for d in range(2):
                            hs, cpt = streams[d]
                            tau = i + 1 if d == 0 else 8 - i
                            ro = hcol + (0 if d == 0 else 1)
                            mms.append((cpt, cpt.t[:, tau, 0, :], hs, hs.t[:, 0, ro:ro + n]))
                            mms.append((cpt, cpt.t[:, tau, 1, :], hs, hs.t[:, 1, ro:ro + n]))
                        if q == 3:
                            for ip in range(8):
                                if ip == i:
                                    ki = 0
                                elif ip < i:
                                    ki = i - ip
                                else:
                                    ki = 7 + (ip - i)
                                mms.append((kkb, kkb.t[:, ki, :], ubd, ubd.t[:, ip, a:b]))
                        for mi, (lt, lap, rt, rap) in enumerate(mms):
                            MM(P, py, py.t[:, 0:n], lt, lap, rt, rap, mi == 0, mi == len(mms) - 1)
                        TT(P, "dve", ud, ud.t[:, i, a:b], py, py.t[:, 0:n], ud, ud.t[:, i, a:b], ALU.add)
                        yield
                    if q == 3:
                        gelu_blocks([bi for bi, (t0, ntok) in enumerate(G.blocks) if a * 8 <= t0 < b * 8])

            def run_rr(gens):
                gens = list(gens)
                while gens:
                    for g_ in list(gens):
                        try:
                            next(g_)
                        except StopIteration:
                            gens.remove(g_)

            pendB = None
            for q in range(4):
                o0, o1 = [], []
                gl_ = [stream_gen(q, 0, o0), stream_gen(q, 1, o1)]
                if pendB is not None:
                    gl_.append(pendB)
                run_rr(gl_)
                pendB = stageB_gen(q, [o0[0], o1[0]])
            run_rr([pendB])
            if "YD" in G.debug:
                if c == 0:
                    G.YD = G.nc.dram_tensor("YD", [8, 128, 8, NCH], F32, kind="ExternalOutput").ap()
                P.dma(G.YD[c], ud.ap(), reads=[ud])
def odd_phase3(P, G, L, j):
    with phase(P):
        wbf = P.sbuf("gluw", [128, 8, 2048], BF16)
        stg = rots(P, "gwstg", [128, 2048], F32, 2)
        for k in range(8):
            s = stg.next()
            P.dma(s.ap(), G.glu_w[j][k * 128:(k + 1) * 128, :], writes=[s])
            CP(P, "pool", wbf, wbf.t[:, k, :], s, s.ap())
        gb = P.sbuf("glub", [128, 16], F32)
        P.dma(gb.ap(), G.glu_b[j].rearrange("(f p) -> p f", p=128), writes=[gb], allow_slow_non_contiguous=True)
        yrot = rots(P, "gyg", [128, 8, 512], BF16, 2)
        srot = rots(P, "gsg", [128, 8, 512], BF16, 2)
        arot = rots(P, "ga", [128, 512], F32, 3)
        brot = rots(P, "gb", [128, 512], F32, 3)
        orot = rots(P, "gout", [128, 512], BF16, 3)
        for bi, (t0, ntok) in enumerate(G.blocks):
            yg = yrot.next()
            P.dma(yg.t[:, :, 0:ntok], G.YG[:, :, t0:t0 + ntok].rearrange("c p t -> p c t"),
                  reads=[G.YG_r[c][bi] for c in range(8)], writes=[yg])
            sg = srot.next()
            P.dma(sg.t[:, :, 0:ntok], G.GT[:, :, t0:t0 + ntok].rearrange("c p t -> p c t"),
                  reads=[G.GT_r[c][bi] for c in range(8)], writes=[sg])
            for f in range(8):
                pa, pb = G.ps.next(), G.ps.next()
                for k in range(8):
                    MM(P, pa, pa.t[:, 0:ntok], wbf, wbf.t[:, k, f * 128:(f + 1) * 128], yg, yg.t[:, k, 0:ntok],
                       k == 0, k == 7)
                for k in range(8):
                    MM(P, pb, pb.t[:, 0:ntok], wbf, wbf.t[:, k, 1024 + f * 128:1024 + (f + 1) * 128], yg,
                       yg.t[:, k, 0:ntok], k == 0, k == 7)
                a_, b_ = arot.next(), brot.next()
                ACT(P, a_, a_.t[:, 0:ntok], pa, pa.t[:, 0:ntok], AF.Identity, bias=gb.t[:, f:f + 1], extra_r=[gb])
                ACT(P, b_, b_.t[:, 0:ntok], pb, pb.t[:, 0:ntok], AF.Sigmoid, bias=gb.t[:, 8 + f:9 + f], extra_r=[gb])
                TT(P, "dve", a_, a_.t[:, 0:ntok], a_, a_.t[:, 0:ntok], b_, b_.t[:, 0:ntok], ALU.mult)
                o = orot.next()
                TT(P, "pool", o, o.t[:, 0:ntok], a_, a_.t[:, 0:ntok], sg, sg.t[:, f, 0:ntok], ALU.mult)
                P.dma(G.MIXT[f, :, t0:t0 + ntok], o.t[:, 0:ntok], reads=[o], writes=[G.MIXT_r[f][bi]], q="pool")


def out_phase(P, G, L, w_dram, last):
    with phase(P):
        load_mod(P, G)
        wo = P.sbuf("wo", [128, 8, D], BF16)
        stg = rots(P, "wostg", [128, D], F32, 2)
        for k in range(8):
            s = stg.next()
            P.dma(s.ap(), w_dram[k * 128:(k + 1) * 128, :], writes=[s])
            CP(P, "pool", wo, wo.t[:, k, :], s, s.ap())
        mrot = rots(P, "mixT", [128, 8, 512], BF16, 2)
        xrot = rots(P, "oxt", [128, D], F32, 2)
        trot = rots(P, "otmp", [128, 512], F32, 2)
        orot = rots(P, "oxo", [128, D], F32, 2)
        for bi, (t0, ntok) in enumerate(G.blocks):
            if last and bi == 0:
                continue
            mixT = mrot.next()
            P.dma(mixT.t[:, :, 0:ntok], G.MIXT[:, :, t0:t0 + ntok].rearrange("c p t -> p c t"),
                  reads=[G.MIXT_r[c][bi] for c in range(8)], writes=[mixT])
            for tl in range(ntok // 128):
                tt = t0 // 128 + tl
                mod = G.modc if tt < 2 else G.modx
                xt = xrot.next()
                sap, sres = G.xsrc(tt)
                P.dma(xt.ap(), sap, reads=[sres], writes=[xt])
                xo = orot.next()
                for nh in range(2):
                    ps = G.ps.next()
                    for k in range(8):
                        MM(P, ps, ps.t[:, :], mixT, mixT.t[:, k, tl * 128:(tl + 1) * 128], wo,
                           wo.t[:, k, nh * 512:(nh + 1) * 512], k == 0, k == 7)
                    tmp = trot.next()
                    TT(P, "dve", tmp, tmp.ap(), ps, ps.t[:, :], mod,
                       mod.t[:, 2 * D + nh * 512:2 * D + (nh + 1) * 512], ALU.mult)
                    TT(P, "pool", xo, xo.t[:, nh * 512:(nh + 1) * 512], tmp, tmp.ap(), xt,
                       xt.t[:, nh * 512:(nh + 1) * 512], ALU.add)
                P.dma(G.XS[tt * 128:(tt + 1) * 128, :], xo.ap(), reads=[xo], writes=[G.XS_r[tt]], q="pool")


def final_phase(P, G):
    with phase(P):
        fw = P.sbuf("fw", [128, D], F32)
        P.dma(fw.ap(), G.final_norm_w.partition_broadcast(128), writes=[fw])
        xrot = rots(P, "fxt", [128, D], F32, 3)
        orot = rots(P, "fxo", [128, D], F32, 3)
        junk = P.sbuf("fjunk", [128, D], F32)
        ssrot = rots(P, "fss", [128, 1], F32, 4)
        for tt in range(2, G.NT):
            xt = xrot.next()
            P.dma(xt.ap(), G.XS[tt * 128:(tt + 1) * 128, :], reads=[G.XS_r[tt]], writes=[xt])
            ss = ssrot.next()
            ACT(P, junk, junk.ap(), xt, xt.ap(), AF.Square, accum=ss.t[:, 0:1], extra_w=[ss])
            RSQRT(P, ss, ss.t[:, 0:1], 1.0 / D, EPS)
            xo = orot.next()
            STT(P, "dve", xo, xo.ap(), xt, xt.ap(), ss.t[:, 0:1], fw, fw.ap(), ALU.mult, ALU.mult, extra_r=[ss])
            P.dma(G.out[(tt - 2) * 128:(tt - 1) * 128, :], xo.ap(), reads=[xo], q="pool")


USED_INPUTS = []


def build(S, depth=4, debug=(), stop_after=None):
    nc = bass.Bass("TRN2", target_bir_lowering=False)
    T = CTX + S
    NT = T // 128
    G = G_()
    G.debug = debug
    G.nc = nc
    G.S, G.T, G.NT = S, T, NT
    G.blocks = [(0, CTX)] + [(CTX + 512 * i, 512) for i in range(S // 512)]
    NB = len(G.blocks)

    def din(name, shape, dt=F32):
        if name not in USED_INPUTS:
            USED_INPUTS.append(name)
        return nc.dram_tensor(name, list(shape), dt, kind="ExternalInput").ap()

    def dscr(name, shape, dt):
        kind = "ExternalOutput" if name in debug else "Internal"
        return nc.dram_tensor(name, list(shape), dt, kind=kind).ap()

    G.x = din("x", [S, D])
    G.c = din("c", [D])
    G.ctx = din("ctx", [CTX, D])
    G.c_ctx = din("c_ctx", [D])
    G.ada_w = din("ada_w", [4, D, 3 * D])
    G.ada_b = din("ada_b", [4, 3 * D])
    G.ev_w_in = din("ev_w_in", [2, D, 2304])
    G.ev_w_out = din("ev_w_out", [2, D, D])
    G.q_norm_w = din("q_norm_w", [2, 64])
    G.k_norm_w = din("k_norm_w", [2, 64])
    G.rg_conv_w = din("rg_conv_w", [2, 4, 512])
    G.rg_conv_b = din("rg_conv_b", [2, 512])
    G.rg_wa = din("rg_wa", [2, 2, 8, 64, 64])
    G.rg_ba = din("rg_ba", [2, 2, 8, 64])
    G.rg_wx = din("rg_wx", [2, 2, 8, 64, 64])
    G.rg_bx = din("rg_bx", [2, 2, 8, 64])
    G.rg_lambda = din("rg_lambda", [2, 2, 512])
    G.final_norm_w = din("final_norm_w", [D])
    G.od_w_in = din("od_w_in", [2, D, 2048])
    G.s5_lambda_re = din("s5_lambda_re", [2, 2, 64, 64])
    G.s5_lambda_im = din("s5_lambda_im", [2, 2, 64, 64])
    G.s5_log_step = din("s5_log_step", [2, 2, 64])
    G.s5_b_re = din("s5_b_re", [2, 2, 64, 64, 16])
    G.s5_b_im = din("s5_b_im", [2, 2, 64, 64, 16])
    G.s5_c_re = din("s5_c_re", [2, 2, 64, 16, 64])
    G.s5_c_im = din("s5_c_im", [2, 2, 64, 16, 64])
    G.s5_d = din("s5_d", [2, D])
    G.glu_w = din("glu_w", [2, D, 2048])
    G.glu_b = din("glu_b", [2, 2048])
    G.od_w_out = din("od_w_out", [2, D, D])
    G.rope = din("rope", [S, 64])
    G.ident_in = din("ident", [128, 128])
    G.out = nc.dram_tensor("out", [S, D], F32, kind="ExternalOutput").ap()

    G.XS = dscr("XS", [T, D], F32)
    G.XS_r = [Res("XS%d" % i) for i in range(NT)]
    G.QT = dscr("QT", [128, 4, T], BF16)
    G.QT_r = [Res("QT%d" % i) for i in range(NT)]
    G.KT = dscr("KT", [128, T], BF16)
    G.KT_r = [Res("KT%d" % i) for i in range(NT)]
    G.VS = dscr("VS", [NT, 128, 2, 192], BF16)
    G.VS_r = [Res("VS%d" % i) for i in range(NT)]
    G.GT = dscr("GT", [8, 128, T], BF16)
    G.GT_r = [[Res("GT%d_%d" % (g, b)) for b in range(NB)] for g in range(8)]
    G.UT = dscr("UT", [4, 128, T], F32)
    G.UT_r = [[Res("UT%d_%d" % (g, b)) for b in range(NB)] for g in range(4)]
    G.UO = dscr("UO", [8, 128, 8, T // 8], F32)
    G.UO_r = [[Res("UO%d_%d" % (g, b)) for b in range(NB)] for g in range(8)]
    G.YG = dscr("YG", [8, 128, T], BF16)
    G.YG_r = [[Res("YG%d_%d" % (g, b)) for b in range(NB)] for g in range(8)]
    G.MIXT = dscr("MIXT", [8, 128, T], BF16)
    G.MIXT_r = [[Res("MX%d_%d" % (g, b)) for b in range(NB)] for g in range(8)]
    in_res = Res("inputs")

    P = Prog(nc)
    G.banks = [P.psum("ps%d" % i, [128, 512], F32) for i in range(8)]
    G.ps = Rot(G.banks)
    G.onecol = P.sbuf("onecol", [128, 1], F32)
    P.dve(lambda e: e.memset(G.onecol.ap(), 1.0), writes=[G.onecol])
    G.hpicol = P.sbuf("hpicol", [128, 1], F32)
    P.dve(lambda e: e.memset(G.hpicol.ap(), float(np.pi / 2)), writes=[G.hpicol])
    G.ones512 = P.sbuf("ones512", [128, 512], F32)
    P.dve(lambda e: e.memset(G.ones512.ap(), 1.0), writes=[G.ones512])
    G.MOD = dscr("MOD", [2, 3 * D], F32)
    G.MOD_r = Res("MOD")
    identf = P.sbuf("identf", [128, 128], F32)
    G.identb = P.sbuf("identb", [128, 128], BF16)
    P.dma(identf.ap(), G.ident_in, writes=[identf])
    CP(P, "dve", G.identb, G.identb.ap(), identf, identf.ap())
    G.identf = identf

    layer0 = [True]

    def xsrc(tt):
        if layer0[0]:
            if tt < 2:
                return G.ctx[tt * 128:(tt + 1) * 128, :], in_res
            return G.x[(tt - 2) * 128:(tt - 1) * 128, :], in_res
        return G.XS[tt * 128:(tt + 1) * 128, :], G.XS_r[tt]
    G.xsrc = xsrc

    def run_layers():
        for L in range(depth):
            layer0[0] = (L == 0)
            j = L // 2
            last = (L == depth - 1)
            if L % 2 == 0:
                seq = [("ada", lambda: ada_phase(P, G, L)), ("e1", lambda: even_phase1(P, G, L, j)),
                       ("e2", lambda: even_phase2(P, G, L, j)), ("e3", lambda: even_phase3(P, G, L, j)),
                       ("out", lambda: out_phase(P, G, L, G.ev_w_out[j], last))]
            else:
                seq = [("ada", lambda: ada_phase(P, G, L)), ("o1", lambda: odd_phase1(P, G, L, j)),
                       ("o2", lambda: odd_phase2(P, G, L, j)), ("o3", lambda: odd_phase3(P, G, L, j)),
                       ("out", lambda: out_phase(P, G, L, G.od_w_out[j], last))]
            for name, fn in seq:
                fn()
                if stop_after == (name, L):
                    return
    run_layers()
    if stop_after is None:
        final_phase(P, G)
    P.finish()
    return nc


def rope_table(S):
    t = np.arange(S)
    row = (t // 64).astype(np.float32)
    col = (t % 64).astype(np.float32)
    inv = (10000.0 ** (-np.arange(16, dtype=np.float32) / 16)).astype(np.float32)
    ang = np.concatenate([row[:, None] * inv, col[:, None] * inv], axis=-1).astype(np.float32)
    return np.concatenate([np.cos(ang), np.sin(ang)], axis=-1).astype(np.float32)


PER_BATCH = ("x", "c", "ctx")
_NC_CACHE = {}


def make_in_map(inp, b, S):
    m = {}
    for k, v in inp.items():
        v = np.asarray(v)
        if k == "x":
            m[k] = np.ascontiguousarray(v[b, :S])
        elif k in ("c", "ctx"):
            m[k] = np.ascontiguousarray(v[b])
        else:
            m[k] = np.ascontiguousarray(v)
    m["rope"] = rope_table(S)
    m["ident"] = np.eye(128, dtype=np.float32)
    return m


def kernel(**inputs):
    S = inputs["x"].shape[1]
    B = inputs["x"].shape[0]
    if S not in _NC_CACHE:
        _NC_CACHE[S] = build(S)
    nc = _NC_CACHE[S]
    names = set(USED_INPUTS)
    in_maps = []
    for b in range(B):
        m = make_in_map(inputs, b, S)
        in_maps.append({k: v for k, v in m.items() if k in names})
    res = run_bass_kernel_spmd(nc, in_maps, core_ids=list(range(B)))
    return np.stack([np.asarray(r["out"]) for r in res.results], axis=0).astype(np.float32)
```

```python
from contextlib import ExitStack
import numpy as np
import concourse.bass as bass
import concourse.mybir as mybir
from concourse.bass_utils import run_bass_kernel_spmd

F32 = mybir.dt.float32
BF16 = mybir.dt.bfloat16
ALU = mybir.AluOpType
AF = mybir.ActivationFunctionType
AX = mybir.AxisListType


class Res:
    def __init__(self, name):
        self.name = name
        self.lw = None
        self.rd = {}


class Tile(Res):
    def __init__(self, name, t):
        super().__init__(name)
        self.t = t

    def ap(self):
        return self.t[:]


class Prog:
    COMPUTE = ("pe", "dve", "act", "pool")
    RING = {"sp": 12, "pool": 6, "act": 6}

    def __init__(self, nc):
        self.nc = nc
        self.stack = ExitStack()
        self.ops = {e: [] for e in ("pe", "dve", "act", "pool", "sp")}
        self.sem = {}
        self.cnt = {}
        for e in self.COMPUTE:
            self.sem[e] = self.stack.enter_context(nc.semaphore("s_" + e))
            self.cnt[e] = 0
        self.ring = {}
        self.ring_pos = {}
        for q, n in self.RING.items():
            names = []
            for i in range(n):
                nm = "d_%s%d" % (q, i)
                self.sem[nm] = self.stack.enter_context(nc.semaphore(nm))
                self.cnt[nm] = 0
                names.append(nm)
            self.ring[q] = names
            self.ring_pos[q] = 0
        self.seen = {e: {} for e in self.ops}
        self.n_ops = 0

    def sbuf(self, name, shape, dtype):
        self.n_alloc = getattr(self, "n_alloc", 0) + 1
        name = "%s_%d" % (name, self.n_alloc)
        t = self.stack.enter_context(self.nc.sbuf_tensor(name, list(shape), dtype))
        return Tile(name, t)

    def psum(self, name, shape, dtype):
        t = self.stack.enter_context(self.nc.psum_tensor(name, list(shape), dtype))
        return Tile(name, t)

    def _deps(self, eng, tok_sem, reads, writes, same_eng_all=False):
        need = {}

        def add(tok):
            if tok is None:
                return
            s, v = tok
            if need.get(s, 0) < v:
                need[s] = v

        for r in reads:
            add(r.lw)
        for w in writes:
            add(w.lw)
            for s, v in w.rd.items():
                add((s, v))
        waits = []
        for s, v in need.items():
            if s == tok_sem and not same_eng_all and eng == "pe":
                continue
            if self.seen[eng].get(s, 0) >= v:
                continue
            self.seen[eng][s] = v
            waits.append((s, v))
        return waits

    def _commit(self, tok, reads, writes):
        for w in writes:
            w.lw = tok
            w.rd = {}
        for r in reads:
            if r in writes:
                continue
            if r.rd.get(tok[0], 0) < tok[1]:
                r.rd[tok[0]] = tok[1]

    def op(self, eng, fn, reads=(), writes=()):
        reads = list(reads)
        writes = list(writes)
        waits = self._deps(eng, eng, reads, writes)
        self.cnt[eng] += 1
        tok = (eng, self.cnt[eng])
        self._commit(tok, reads, writes)
        self.ops[eng].append((fn, waits, (eng, 1)))
        self.n_ops += 1
        return tok

    def pe(self, fn, reads=(), writes=()):
        return self.op("pe", fn, reads, writes)

    def dve(self, fn, reads=(), writes=()):
        return self.op("dve", fn, reads, writes)

    def act(self, fn, reads=(), writes=()):
        return self.op("act", fn, reads, writes)

    def pool(self, fn, reads=(), writes=()):
        return self.op("pool", fn, reads, writes)

    def dma(self, out, in_, reads=(), writes=(), q="sp", **kw):
        reads = list(reads)
        writes = list(writes)
        ring = self.ring[q]
        slot = ring[self.ring_pos[q] % len(ring)]
        self.ring_pos[q] += 1
        waits = self._deps(q, slot, reads, writes, same_eng_all=True)
        prev = self.cnt[slot]
        if prev > 0 and self.seen[q].get(slot, 0) < prev:
            self.seen[q][slot] = prev
            waits.append((slot, prev))
        self.cnt[slot] += 16
        tok = (slot, self.cnt[slot])
        self._commit(tok, reads, writes)
        self.ops[q].append((lambda e: e.dma_start(out=out, in_=in_, **kw), waits, (slot, 16)))
        self.n_ops += 1
        return tok

    def all_tokens(self):
        return [(s, v) for s, v in self.cnt.items() if v > 0]

    def flush(self):
        nc = self.nc
        final_waits = self.all_tokens()
        ops, sem = self.ops, self.sem

        def replay(name, eng):
            for fn, waits, inc in ops[name]:
                for s, v in waits:
                    eng.wait_ge(sem[s], v)
                ins = fn(eng)
                ins.then_inc(sem[inc[0]], inc[1])
            for s, v in final_waits:
                eng.wait_ge(sem[s], v)
                self.seen[name][s] = v

        with nc.Block() as block:
            @block.tensor
            def _(e):
                replay("pe", e)

            @block.vector
            def _(e):
                replay("dve", e)

            @block.scalar
            def _(e):
                replay("act", e)

            @block.gpsimd
            def _(e):
                replay("pool", e)

            @block.sync
            def _(e):
                replay("sp", e)
        self.ops = {e: [] for e in self.ops}

    def finish(self):
        self.flush()
        self.stack.close()


class Rot:
    def __init__(self, tiles):
        self.tiles = tiles
        self.i = 0

    def next(self):
        t = self.tiles[self.i % len(self.tiles)]
        self.i += 1
        return t


D = 1024
CTX = 256
EPS = 1e-6


class G_:
    pass


def MM(P, ot, oap, lt, lap, rt, rap, start, stop):
    P.pe(lambda e: e.matmul(oap, lhsT=lap, rhs=rap, start=start, stop=stop), reads=[lt, rt], writes=[ot])


def TR(P, ot, oap, it, iap, ident):
    P.pe(lambda e: e.transpose(oap, iap, ident.ap()[0:iap.shape[0], 0:iap.shape[0]]), reads=[it, ident], writes=[ot])


def ACT(P, ot, oap, it, iap, func, bias=None, scale=None, accum=None, extra_r=(), extra_w=()):
    kw = {}
    if bias is not None:
        kw["bias"] = bias
    if scale is not None:
        kw["scale"] = scale
    if accum is not None:
        kw["accum_out"] = accum
    P.act(lambda e: e.activation(out=oap, in_=iap, func=func, **kw), reads=[it] + list(extra_r),
          writes=[ot] + list(extra_w))


def TT(P, eng, ot, oap, at, aap, bt, bap, op):
    P.op(eng, lambda e: e.tensor_tensor(out=oap, in0=aap, in1=bap, op=op), reads=[at, bt], writes=[ot])


def TS(P, eng, ot, oap, it, iap, s1, s2, op0, op1=None, extra_r=()):
    if op1 is None:
        P.op(eng, lambda e: e.tensor_scalar(out=oap, in0=iap, scalar1=s1, scalar2=None, op0=op0),
             reads=[it] + list(extra_r), writes=[ot])
    else:
        P.op(eng, lambda e: e.tensor_scalar(out=oap, in0=iap, scalar1=s1, scalar2=s2, op0=op0, op1=op1),
             reads=[it] + list(extra_r), writes=[ot])


def STT(P, eng, ot, oap, at, aap, scal, bt, bap, op0, op1, extra_r=()):
    P.op(eng, lambda e: e.scalar_tensor_tensor(out=oap, in0=aap, scalar=scal, in1=bap, op0=op0, op1=op1),
         reads=[at, bt] + list(extra_r), writes=[ot])


def CP(P, eng, ot, oap, it, iap):
    if eng == "act":
        P.act(lambda e: e.copy(out=oap, in_=iap), reads=[it], writes=[ot])
    else:
        P.op(eng, lambda e: e.tensor_copy(out=oap, in_=iap), reads=[it], writes=[ot])


def RSQRT(P, t, ap, mul, add):
    TS(P, "dve", t, ap, t, ap, mul, add, ALU.mult, ALU.add)
    ACT(P, t, ap, t, ap, AF.Sqrt)
    P.dve(lambda e: e.reciprocal(out=ap, in_=ap), reads=[t], writes=[t])


def phase(P):
    from contextlib import contextmanager

    @contextmanager
    def cm():
        outer = P.stack
        P.stack = ExitStack()
        try:
            yield
            P.flush()
        finally:
            P.stack.close()
            P.stack = outer
    return cm()


def run_window(gens, width):
    gens = list(gens)
    active = []
    while gens or active:
        while gens and len(active) < width:
            active.append(gens.pop(0))
        for g_ in list(active):
            try:
                next(g_)
            except StopIteration:
                active.remove(g_)


def rots(P, name, shape, dtype, n):
    return Rot([P.sbuf("%s%d" % (name, i), shape, dtype) for i in range(n)])


def load_mod(P, G):
    G.modx = P.sbuf("modx", [128, 3 * D], F32)
    G.modc = P.sbuf("modc", [128, 3 * D], F32)
    P.dma(G.modx.ap(), G.MOD[0].partition_broadcast(128), reads=[G.MOD_r], writes=[G.modx])
    P.dma(G.modc.ap(), G.MOD[1].partition_broadcast(128), reads=[G.MOD_r], writes=[G.modc])


def ada_phase(P, G, L):
    with phase(P):
        G.modx = P.sbuf("modx", [128, 3 * D], F32)
        G.modc = P.sbuf("modc", [128, 3 * D], F32)
        ccol = P.sbuf("ccol", [128, 16], F32)
        P.dma(ccol.t[:, 0:8], G.c.rearrange("(k p) -> p k", p=128), writes=[ccol], allow_slow_non_contiguous=True)
        P.dma(ccol.t[:, 8:16], G.c_ctx.rearrange("(k p) -> p k", p=128), writes=[ccol],
              allow_slow_non_contiguous=True)
        sil = P.sbuf("sil", [128, 16], F32)
        ACT(P, sil, sil.ap(), ccol, ccol.ap(), AF.Silu)
        ones = P.sbuf("ones", [128, 128], F32)
        P.dve(lambda e: e.memset(ones.ap(), 1.0), writes=[ones])
        sbc = P.sbuf("sbc", [128, 16, 128], F32)
        for k in range(16):
            TS(P, "dve", sbc, sbc.t[:, k, :], ones, ones.ap(), sil.t[:, k:k + 1], None, ALU.mult, extra_r=[sil])
        bias = P.sbuf("adab", [128, 3 * D], F32)
        P.dma(bias.ap(), G.ada_b[L].partition_broadcast(128), writes=[bias])
        wrot = rots(P, "adaw", [128, 8, 512], F32, 2)
        for n in range(6):
            w = wrot.next()
            P.dma(w.ap(), G.ada_w[L][:, n * 512:(n + 1) * 512].rearrange("(k p) n -> p k n", p=128), writes=[w])
            for which, mod in ((0, G.modx), (1, G.modc)):
                ps = G.ps.next()
                for k in range(8):
                    MM(P, ps, ps.t[:, :], sbc, sbc.t[:, which * 8 + k, :], w, w.t[:, k, :], k == 0, k == 7)
                TT(P, "dve", mod, mod.t[:, n * 512:(n + 1) * 512], ps, ps.t[:, :], bias,
                   bias.t[:, n * 512:(n + 1) * 512], ALU.add)
        for mi, mod in enumerate((G.modx, G.modc)):
            TS(P, "dve", mod, mod.t[:, D:2 * D], mod, mod.t[:, D:2 * D], 1.0, None, ALU.add)
            P.dma(G.MOD[mi:mi + 1, :], mod.t[0:1, :], reads=[mod], writes=[G.MOD_r], q="pool")


def norm_block(P, G, W, blk, hT):
    t0, ntok = blk
    nt = ntok // 128

    def stage1(tl):
        tt = t0 // 128 + tl
        xt = W.xrot.next()
        sap, sres = G.xsrc(tt)
        P.dma(xt.ap(), sap, reads=[sres], writes=[xt])
        ss = W.ssrot.next()
        ACT(P, W.junk, W.junk.ap(), xt, xt.ap(), AF.Square, accum=ss.t[:, 0:1], extra_w=[ss])
        RSQRT(P, ss, ss.t[:, 0:1], 1.0 / D, EPS)
        return xt, ss

    def stage2(tl, xt, ss):
        tt = t0 // 128 + tl
        mod = G.modc if tt < 2 else G.modx
        tmp = W.tmprot.next()
        STT(P, "dve", tmp, tmp.ap(), xt, xt.ap(), ss.t[:, 0:1], mod, mod.t[:, D:2 * D], ALU.mult, ALU.mult,
            extra_r=[ss])
        h = W.hrot.next()
        TT(P, "dve", h, h.ap(), tmp, tmp.ap(), mod, mod.t[:, 0:D], ALU.add)
        tp = G.ps.next()
        tpb = tp.t[:].bitcast(BF16)
        for k in range(8):
            TR(P, tp, tpb[:, k * 128:(k + 1) * 128], h, h.t[:, k * 128:(k + 1) * 128], G.identb)
        CP(P, "act", hT, hT.t[:, :, tl * 128:(tl + 1) * 128], tp, tpb.rearrange("p (k t) -> p k t", k=8))

    pend = [stage1(0)]
    for tl in range(nt):
        if tl + 1 < nt:
            pend.append(stage1(tl + 1))
        xt, ss = pend.pop(0)
        stage2(tl, xt, ss)


def norm_work(P):
    W = G_()
    W.xrot = rots(P, "xt", [128, D], F32, 3)
    W.ssrot = rots(P, "ss", [128, 1], F32, 4)
    W.junk = P.sbuf("junk", [128, D], F32)
    W.tmprot = rots(P, "tmp", [128, D], F32, 2)
    W.hrot = rots(P, "h", [128, D], BF16, 2)
    W.hTrot = rots(P, "hT", [128, 8, 512], BF16, 3)
    return W


def qk_post(P, G, W, ps, psap, nh, gain, cs, ob):
    n = nh * 64
    sq = W.sq
    ACT(P, sq, sq.t[:, 0:n], ps, psap, AF.Square)
    ssh = W.sshrot.next()
    P.dve(lambda e: e.tensor_reduce(out=ssh.t[:, 0:nh], in_=sq.t[:, 0:n].rearrange("p (h d) -> p h d", h=nh),
                                    axis=AX.X, op=ALU.add), reads=[sq], writes=[ssh])
    RSQRT(P, ssh, ssh.t[:, 0:nh], 1.0 / 64, EPS)
    qg = W.qgrot.next()
    v3 = lambda t, ap: ap.rearrange("p (h d) -> p h d", h=nh)
    TT(P, "dve", qg, v3(qg, qg.t[:, 0:n]), ps, v3(ps, psap), gain,
       gain.t[:, 0:64].unsqueeze(1).to_broadcast([128, nh, 64]), ALU.mult)
    src = qg
    if cs is not None:
        ro = W.rorot.next()
        q4 = qg.t[:, 0:n].rearrange("p (h i two) -> p h i two", h=nh, two=2)
        r4 = ro.t[:, 0:n].rearrange("p (h i two) -> p h i two", h=nh, two=2)
        x0, x1 = q4[:, :, :, 0], q4[:, :, :, 1]
        cosb = cs.t[:, 0:32].unsqueeze(1).to_broadcast([128, nh, 32])
        sinb = cs.t[:, 32:64].unsqueeze(1).to_broadcast([128, nh, 32])
        ta, tb = W.ropet.next(), W.ropet.next()
        a3 = lambda t: t.t[:, 0:nh * 32].rearrange("p (h i) -> p h i", h=nh)
        TT(P, "dve", ta, a3(ta), qg, x0, cs, cosb, ALU.mult)
        TT(P, "dve", tb, a3(tb), qg, x1, cs, sinb, ALU.mult)
        TT(P, "dve", ro, r4[:, :, :, 0], ta, a3(ta), tb, a3(tb), ALU.subtract)
        tc_, td = W.ropet.next(), W.ropet.next()
        TT(P, "dve", tc_, a3(tc_), qg, x0, cs, sinb, ALU.mult)
        TT(P, "dve", td, a3(td), qg, x1, cs, cosb, ALU.mult)
        TT(P, "dve", ro, r4[:, :, :, 1], tc_, a3(tc_), td, a3(td), ALU.add)
        src = ro
    TT(P, "dve", ob, v3(ob, ob.t[:, 0:n]), src, v3(src, src.t[:, 0:n]), ssh,
       ssh.t[:, 0:nh].unsqueeze(2).to_broadcast([128, nh, 64]), ALU.mult)


def even_phase1(P, G, L, j):
    with phase(P):
        load_mod(P, G)
        W = norm_work(P)
        wbf = P.sbuf("winbf", [128, 8, 2304], BF16)
        stg = rots(P, "wstg", [128, 2304], F32, 2)
        for k in range(8):
            s = stg.next()
            P.dma(s.ap(), G.ev_w_in[j][k * 128:(k + 1) * 128, :], writes=[s])
            CP(P, "act", wbf, wbf.t[:, k, 0:512].rearrange("q (p r d) -> q p r d", p=4, r=2),
               s, s.t[:, 0:512].rearrange("q (r p d) -> q p r d", r=2, p=4))
            CP(P, "act", wbf, wbf.t[:, k, 512:2304], s, s.t[:, 512:2304])
        gq = P.sbuf("gq", [128, 64], F32)
        gk = P.sbuf("gk", [128, 64], F32)
        P.dma(gq.ap(), G.q_norm_w[j].partition_broadcast(128), writes=[gq])
        P.dma(gk.ap(), G.k_norm_w[j].partition_broadcast(128), writes=[gk])
        W.sq = P.sbuf("sq", [128, 512], F32)
        W.sshrot = rots(P, "ssh", [128, 8], F32, 4)
        W.qgrot = rots(P, "qg", [128, 512], F32, 2)
        W.rorot = rots(P, "ro", [128, 512], F32, 2)
        W.ropet = rots(P, "ropet", [128, 256], F32, 8)
        csrot = rots(P, "cs", [128, 64], F32, 2)
        gorot = rots(P, "go", [128, 512], BF16, 3)
        uorot = rots(P, "uo", [128, 512], F32, 2)
        varot = rots(P, "va", [128, 2, 192], BF16, 2)
        for va in varot.tiles:
            P.dve(lambda e, va=va: e.memset(va.ap(), 1.0), writes=[va])
        qbrot = rots(P, "qb", [128, 512], BF16, 2)
        kbrot = rots(P, "kb", [128, 128], BF16, 2)
        qsrot = rots(P, "qs", [128, 640], BF16, 2)
        qkrot = Rot(G.banks[0:4])
        ps_save = G.ps
        G.ps = Rot(G.banks[4:8])

        def fm(bi, blk, hT, f):
            t0, ntok = blk
            col = 768 + 128 * f
            ps = G.ps.next()
            for k in range(8):
                MM(P, ps, ps.t[:, 0:ntok], wbf, wbf.t[:, k, col:col + 128], hT, hT.t[:, k, 0:ntok], k == 0, k == 7)
            if f < 4 or f >= 8:
                gi = f if f < 4 else f - 4
                o = gorot.next()
                ACT(P, o, o.t[:, 0:ntok], ps, ps.t[:, 0:ntok], AF.Silu)
                P.dma(G.GT[gi, :, t0:t0 + ntok], o.t[:, 0:ntok], reads=[o], writes=[G.GT_r[gi][bi]], q="pool")
            else:
                ui = f - 4
                o = uorot.next()
                CP(P, "act", o, o.t[:, 0:ntok], ps, ps.t[:, 0:ntok])
                P.dma(G.UT[ui, :, t0:t0 + ntok], o.t[:, 0:ntok], reads=[o], writes=[G.UT_r[ui][bi]], q="pool")

        def stageM(blk, hT, tl):
            t0, ntok = blk
            tt = t0 // 128 + tl
            psq = qkrot.next()
            pskv = qkrot.next()
            for k in range(8):
                MM(P, psq, psq.t[:, 0:512], hT, hT.t[:, k, tl * 128:(tl + 1) * 128], wbf, wbf.t[:, k, 0:512],
                   k == 0, k == 7)
            for k in range(8):
                MM(P, pskv, pskv.t[:, 0:256], hT, hT.t[:, k, tl * 128:(tl + 1) * 128], wbf, wbf.t[:, k, 512:768],
                   k == 0, k == 7)
            return psq, pskv

        def stageQ(blk, tl, psq, pskv):
            t0, ntok = blk
            tt = t0 // 128 + tl
            va = varot.next()
            CP(P, "act", va, va.t[:, :, 64:128], pskv, pskv.t[:, 128:256].rearrange("p (r d) -> p r d", r=2))
            P.dma(G.VS[tt], va.ap(), reads=[va], writes=[G.VS_r[tt]], q="pool")
            cs = None
            if tt >= 2:
                cs = csrot.next()
                P.dma(cs.ap(), G.rope[(tt - 2) * 128:(tt - 1) * 128, :], writes=[cs])
            qb = qbrot.next()
            kb = kbrot.next()
            qk_post(P, G, W, psq, psq.t[:, 0:512], 8, gq, cs, qb)
            qk_post(P, G, W, pskv, pskv.t[:, 0:128], 2, gk, cs, kb)
            tp = G.ps.next()
            tpb = tp.t[:].bitcast(BF16)
            for p in range(4):
                TR(P, tp, tpb[:, p * 128:(p + 1) * 128], qb, qb.t[:, p * 128:(p + 1) * 128], G.identb)
            TR(P, tp, tpb[:, 512:640], kb, kb.t[:, 0:128], G.identb)
            qs = qsrot.next()
            CP(P, "dve", qs, qs.t[:, 0:640], tp, tpb[:, 0:640])
            P.dma(G.QT[:, :, tt * 128:(tt + 1) * 128], qs.t[:, 0:512].rearrange("p (a t) -> p a t", a=4),
                  reads=[qs], writes=[G.QT_r[tt]], q="pool")
            P.dma(G.KT[:, tt * 128:(tt + 1) * 128], qs.t[:, 512:640], reads=[qs], writes=[G.KT_r[tt]], q="pool")

        NB = len(G.blocks)
        hTs = {0: W.hTrot.next()}
        norm_block(P, G, W, G.blocks[0], hTs[0])
        for bi, blk in enumerate(G.blocks):
            hT = hTs.pop(bi)
            nt = blk[1] // 128
            fl = [list(range(0, 4)), list(range(4, 8)), list(range(8, 12))]
            live = {}
            for tl in range(min(2, nt)):
                live[tl] = stageM(blk, hT, tl)
            for step in range(max(nt, 3)):
                if step < 3:
                    for f in fl[step]:
                        fm(bi, blk, hT, f)
                if step < nt:
                    stageQ(blk, step, *live.pop(step))
                    if step + 2 < nt:
                        live[step + 2] = stageM(blk, hT, step + 2)
                if step == 0 and bi + 1 < NB:
                    hTs[bi + 1] = W.hTrot.next()
                    norm_block(P, G, W, G.blocks[bi + 1], hTs[bi + 1])
        G.ps = ps_save


def even_phase2(P, G, L, j):
    with phase(P):
        T, NT = G.T, G.NT
        KT = P.sbuf("KTs", [128, T], BF16)
        P.dma(KT.ap(), G.KT, reads=G.KT_r, writes=[KT])
        VA = P.sbuf("VAs", [128, NT, 2, 192], BF16)
        P.dma(VA.ap(), G.VS.rearrange("n p r c -> p n r c"), reads=G.VS_r, writes=[VA])
        SKEW = 2
        srot = Rot(G.banks[0:4])
        accrot = Rot(G.banks[4:8])
        qrot = rots(P, "qtb", [128, 8, 512], BF16, 2)
        for qt_ in qrot.tiles:
            P.pool(lambda e, qt_=qt_: e.memset(qt_.ap(), 0.0), writes=[qt_])
        ptrot = rots(P, "pt", [128, 512], BF16, 5)
        rrot = rots(P, "rr", [128, 512], F32, 2)
        mrot = rots(P, "mm", [128, 512], F32, 2)
        grot = rots(P, "gg", [128, 512], BF16, 2)
        orot = rots(P, "oo", [128, 512], BF16, 2)
        for bi, (t0, ntok) in enumerate(G.blocks):
            keys = [0, 1] if bi == 0 else list(range(NT))
            tts = list(range(t0 // 128, (t0 + ntok) // 128))
            QTb = qrot.next()
            for r_ in range(2):
                P.dma(QTb.t[r_ * 64:(r_ + 1) * 64, r_ * 4:(r_ + 1) * 4, 0:ntok],
                      G.QT[r_ * 64:(r_ + 1) * 64, :, t0:t0 + ntok], reads=[G.QT_r[tt] for tt in tts], writes=[QTb])
            for c in range(4):
                accs = []
                for hh in range(2):
                    h = 2 * c + hh
                    p, r = h % 4, h // 4
                    acc = accrot.next()
                    vsl = slice(64, 192) if hh == 0 else slice(0, 128)

                    def pv(pt, kt, acc=acc, r=r, vsl=vsl):
                        MM(P, acc, acc.t[:, 0:ntok], VA, VA.t[:, kt, r, vsl], pt, pt.t[:, 0:ntok],
                           kt == keys[0], kt == keys[-1])
                    pend = []
                    for kt in keys:
                        sps = srot.next()
                        MM(P, sps, sps.t[:, 0:ntok], KT, KT.t[:, kt * 128:(kt + 1) * 128],
                           QTb, QTb.t[:, h, 0:ntok], True, True)
                        if len(pend) >= SKEW:
                            pv(*pend.pop(0))
                        pt = ptrot.next()
                        ACT(P, pt, pt.t[:, 0:ntok], sps, sps.t[:, 0:ntok], AF.Exp, scale=0.125)
                        pend.append((pt, kt))
                    for pp_ in pend:
                        pv(*pp_)
                    accs.append(acc)
                A, B = accs
                Rr = rrot.next()
                P.dve(lambda e, Rr=Rr, A=A: e.reciprocal(out=Rr.t[0:64, 0:ntok], in_=A.t[64:128, 0:ntok]),
                      reads=[A], writes=[Rr])
                P.dve(lambda e, Rr=Rr, B=B: e.reciprocal(out=Rr.t[64:128, 0:ntok], in_=B.t[0:64, 0:ntok]),
                      reads=[B], writes=[Rr])
                M = mrot.next()
                TT(P, "dve", M, M.t[0:64, 0:ntok], A, A.t[0:64, 0:ntok], Rr, Rr.t[0:64, 0:ntok], ALU.mult)
                TT(P, "dve", M, M.t[64:128, 0:ntok], B, B.t[64:128, 0:ntok], Rr, Rr.t[64:128, 0:ntok], ALU.mult)
                g = grot.next()
                P.dma(g.t[:, 0:ntok], G.GT[c, :, t0:t0 + ntok], reads=[G.GT_r[c][bi]], writes=[g])
                o = orot.next()
                TT(P, "pool", o, o.t[:, 0:ntok], M, M.t[:, 0:ntok], g, g.t[:, 0:ntok], ALU.mult)
                P.dma(G.MIXT[c, :, t0:t0 + ntok], o.t[:, 0:ntok], reads=[o], writes=[G.MIXT_r[c][bi]], q="pool")


def even_phase3(P, G, L, j):
    T = G.T
    segs_f = [(a, a + n) for (a, n) in G.blocks]
    segs_r = [segs_f[0]] + segs_f[:0:-1]
    with phase(P):
        u = P.sbuf("ru", [128, T], F32)
        y = P.sbuf("ry", [128, T], F32)
        yb = P.sbuf("ryb", [128, T], BF16)
        H = P.sbuf("rH", [128, T], F32)
        HB = u
        wrot = rots(P, "rw", [128, 512], F32, 2)
        wrd = [rots(P, "rw%d_" % d_, [128, 512], F32, 10) for d_ in range(2)]
        cols = rots(P, "rcol", [128, 17], F32, 2)
        wstg = rots(P, "rwst", [128, 128], F32, 2)
        wbd = rots(P, "rwbd", [128, 128], BF16, 4)
        grot = rots(P, "rg", [128, 512], BF16, 2)
        orot = rots(P, "ro_", [128, 512], BF16, 2)
        for c in range(4):
            cs_ = slice(c * 128, (c + 1) * 128)
            P.dma(u.ap(), G.UT[c], reads=G.UT_r[c], writes=[u])
            col = cols.next()
            P.dma(col.t[:, 0:4], G.rg_conv_w[j][:, cs_].rearrange("k p -> p k"), writes=[col],
                  allow_slow_non_contiguous=True)
            P.dma(col.t[:, 4:5], G.rg_conv_b[j][cs_].rearrange("(p o) -> p o", o=1), writes=[col])
            for d in range(2):
                P.dma(col.t[:, 5 + d:6 + d], G.rg_ba[j, d].rearrange("h i -> (h i)")[cs_].rearrange("(p o) -> p o", o=1),
                      writes=[col])
                P.dma(col.t[:, 7 + d:8 + d], G.rg_bx[j, d].rearrange("h i -> (h i)")[cs_].rearrange("(p o) -> p o", o=1),
                      writes=[col])
                P.dma(col.t[:, 9 + d:10 + d], G.rg_lambda[j, d][cs_].rearrange("(p o) -> p o", o=1), writes=[col])
            ACT(P, col, col.t[:, 11:13], col, col.t[:, 9:11], AF.Exp, scale=-1.0)
            ACT(P, col, col.t[:, 11:13], col, col.t[:, 11:13], AF.Ln, bias=G.onecol.t[:, 0:1], extra_r=[G.onecol])
            TS(P, "dve", col, col.t[:, 11:13], col, col.t[:, 11:13], -8.0, None, ALU.mult)
            for (a, b) in (segs_f[0], (CTX, T)):
                TS(P, "dve", y, y.t[:, a:b], u, u.t[:, a:b], col.t[:, 2:3], col.t[:, 4:5], ALU.mult, ALU.add,
                   extra_r=[col])
                STT(P, "dve", y, y.t[:, a + 2:b], u, u.t[:, a:b - 2], col.t[:, 0:1], y, y.t[:, a + 2:b], ALU.mult,
                    ALU.add, extra_r=[col])
                STT(P, "dve", y, y.t[:, a + 1:b], u, u.t[:, a:b - 1], col.t[:, 1:2], y, y.t[:, a + 1:b], ALU.mult,
                    ALU.add, extra_r=[col])
                STT(P, "dve", y, y.t[:, a:b - 1], u, u.t[:, a + 1:b], col.t[:, 3:4], y, y.t[:, a:b - 1], ALU.mult,
                    ALU.add, extra_r=[col])
            CP(P, "pool", yb, yb.ap(), y, y.ap())
            TS(P, "dve", col, col.t[:, 13:17], col, col.t[:, 5:9], -1.0, None, ALU.mult)

            def dir_gen(d):
                wts = []
                for wsrc in (G.rg_wa, G.rg_wx):
                    st = wstg.next()
                    P.pool(lambda e, st=st: e.memset(st.ap(), 0.0), writes=[st])
                    for hb in range(2):
                        P.dma(st.t[hb * 64:(hb + 1) * 64, hb * 64:(hb + 1) * 64], wsrc[j, d, 2 * c + hb], writes=[st])
                    wb = wbd.next()
                    CP(P, "pool", wb, wb.ap(), st, st.ap())
                    wts.append(wb)
                yield
                wr = wrd[d]
                for (a, b) in (segs_f if d == 0 else segs_r):
                    n = b - a
                    psr, psi = G.ps.next(), G.ps.next()
                    MM(P, psr, psr.t[:, 0:n], wts[0], wts[0].ap(), yb, yb.t[:, a:b], True, True)
                    MM(P, psi, psi.t[:, 0:n], wts[1], wts[1].ap(), yb, yb.t[:, a:b], True, True)
                    rr, ii, aa, a2, bt = [wr.next() for _ in range(5)]
                    ACT(P, rr, rr.t[:, 0:n], psr, psr.t[:, 0:n], AF.Exp, scale=-1.0, bias=col.t[:, 13 + d:14 + d],
                        extra_r=[col])
                    ACT(P, ii, ii.t[:, 0:n], psi, psi.t[:, 0:n], AF.Exp, scale=-1.0, bias=col.t[:, 15 + d:16 + d],
                        extra_r=[col])
                    for t_ in (rr, ii):
                        ACT(P, t_, t_.t[:, 0:n], t_, t_.t[:, 0:n], AF.Ln, bias=G.onecol.t[:, 0:1], extra_r=[G.onecol])
                        ACT(P, t_, t_.t[:, 0:n], t_, t_.t[:, 0:n], AF.Exp, scale=-1.0)
                    ACT(P, aa, aa.t[:, 0:n], rr, rr.t[:, 0:n], AF.Exp, scale=col.t[:, 11 + d:12 + d], extra_r=[col])
                    TT(P, "pool", a2, a2.t[:, 0:n], aa, aa.t[:, 0:n], aa, aa.t[:, 0:n], ALU.mult)
                    TS(P, "pool", a2, a2.t[:, 0:n], a2, a2.t[:, 0:n], -1.0, 1.0, ALU.mult, ALU.add)
                    yield
                    ACT(P, a2, a2.t[:, 0:n], a2, a2.t[:, 0:n], AF.Ln)
                    ACT(P, a2, a2.t[:, 0:n], a2, a2.t[:, 0:n], AF.Exp, scale=0.5)
                    TT(P, "dve", bt, bt.t[:, 0:n], ii, ii.t[:, 0:n], y, y.t[:, a:b], ALU.mult)
                    TT(P, "dve", bt, bt.t[:, 0:n], bt, bt.t[:, 0:n], a2, a2.t[:, 0:n], ALU.mult)
                    if d == 0:
                        init = 0.0 if a == 0 else H.t[:, a - 1:a]
                        P.dve(lambda e, a=a, b=b, n=n, aa=aa, bt=bt, init=init: e.tensor_tensor_scan(
                            out=H.t[:, a:b], data0=aa.t[:, 0:n], data1=bt.t[:, 0:n], initial=init,
                            op0=ALU.mult, op1=ALU.add), reads=[aa, bt, H], writes=[H])
                    else:
                        if a == 0:
                            init = 0.0
                        elif b == T:
                            init = HB.t[:, 0:1]
                        else:
                            init = HB.t[:, b:b + 1]
                        P.dve(lambda e, a=a, b=b, n=n, aa=aa, bt=bt, init=init: e.tensor_tensor_scan(
                            out=HB.t[:, a:b][:, ::-1], data0=aa.t[:, 0:n][:, ::-1], data1=bt.t[:, 0:n][:, ::-1],
                            initial=init, op0=ALU.mult, op1=ALU.add), reads=[aa, bt, HB], writes=[HB])
                    yield

            run_window([dir_gen(0), dir_gen(1)], 2)
            for bi, (t0, ntok) in enumerate(G.blocks):
                sm = wrot.next()
                TT(P, "dve", sm, sm.t[:, 0:ntok], H, H.t[:, t0:t0 + ntok], HB, HB.t[:, t0:t0 + ntok], ALU.add)
                g = grot.next()
                P.dma(g.t[:, 0:ntok], G.GT[4 + c, :, t0:t0 + ntok], reads=[G.GT_r[4 + c][bi]], writes=[g])
                o = orot.next()
                TT(P, "pool", o, o.t[:, 0:ntok], sm, sm.t[:, 0:ntok], g, g.t[:, 0:ntok], ALU.mult)
                P.dma(G.MIXT[4 + c, :, t0:t0 + ntok], o.t[:, 0:ntok], reads=[o], writes=[G.MIXT_r[4 + c][bi]], q="pool")


def odd_phase1(P, G, L, j):
    with phase(P):
        load_mod(P, G)
        W = norm_work(P)
        wbf = P.sbuf("owin", [128, 8, 2048], BF16)
        stg = rots(P, "owstg", [128, 2048], F32, 2)
        for k in range(8):
            s = stg.next()
            P.dma(s.ap(), G.od_w_in[j][k * 128:(k + 1) * 128, :], writes=[s])
            CP(P, "act", wbf, wbf.ap()[:, k, :], s, s.ap())
        gorot = rots(P, "ogo", [128, 512], BF16, 3)
        uorot = rots(P, "ouo", [128, 512], F32, 3)
        def blk_gen(bi, blk):
            t0, ntok = blk
            hT = W.hTrot.next()
            norm_block(P, G, W, blk, hT)
            yield
            for f in range(16):
                if f % 4 == 0 and f > 0:
                    yield
                ps = G.ps.next()
                for k in range(8):
                    MM(P, ps, ps.t[:, 0:ntok], wbf, wbf.t[:, k, f * 128:(f + 1) * 128], hT, hT.t[:, k, 0:ntok],
                       k == 0, k == 7)
                if f < 8:
                    o = uorot.next()
                    nch = ntok // 8
                    CP(P, "act", o, o.t[:, 0:ntok].rearrange("p (i c) -> p i c", i=8), ps,
                       ps.t[:, 0:ntok].rearrange("p (c i) -> p i c", i=8))
                    P.dma(G.UO[f, :, :, t0 // 8:t0 // 8 + nch], o.t[:, 0:ntok].rearrange("p (i c) -> p i c", i=8),
                          reads=[o], writes=[G.UO_r[f][bi]], q="pool")
                else:
                    o = gorot.next()
                    ACT(P, o, o.t[:, 0:ntok], ps, ps.t[:, 0:ntok], AF.Silu)
                    P.dma(G.GT[f - 8, :, t0:t0 + ntok], o.t[:, 0:ntok], reads=[o], writes=[G.GT_r[f - 8][bi]], q="pool")

        run_window([blk_gen(bi, blk) for bi, blk in enumerate(G.blocks)], 2)


def odd_phase2(P, G, L, j):
    S = G.S
    NX = S // 8
    NCH = 32 + NX
    NS = min(512, NX)
    XB = 34
    segs_x = [(a, min(a + NS, NX)) for a in range(0, NX, NS)]
    seg_c = (0, 32, 0)
    segs_f = [seg_c] + [(32 + a, 32 + b, XB + a) for (a, b) in segs_x]
    segs_r = [seg_c] + [(32 + a, 32 + b, XB + a) for (a, b) in segs_x[::-1]]
    NLV = 13
    with phase(P):
        lre = P.sbuf("s_lre", [128, 64], F32)
        lim = P.sbuf("s_lim", [128, 64], F32)
        dtt = P.sbuf("s_dt", [128, 64], F32)
        for d in range(2):
            for h in range(2):
                dst = (slice(h * 64, (h + 1) * 64), slice(d * 32, (d + 1) * 32))
                P.dma(lre.t[dst], G.s5_lambda_re[j, d].rearrange("(q h) p -> h p q", h=2)[h], writes=[lre],
                      allow_slow_non_contiguous=True)
                P.dma(lim.t[dst], G.s5_lambda_im[j, d].rearrange("(q h) p -> h p q", h=2)[h], writes=[lim],
                      allow_slow_non_contiguous=True)
                P.dma(dtt.t[dst], G.s5_log_step[j, d].rearrange("(q h) -> h q", h=2)[h].partition_broadcast(64),
                      writes=[dtt], allow_slow_non_contiguous=True)
        ACT(P, dtt, dtt.ap(), dtt, dtt.ap(), AF.Exp)
        mag = P.sbuf("s_mag", [128, 64], F32)
        th = P.sbuf("s_th", [128, 64], F32)
        xx = P.sbuf("s_xx", [128, 64], F32)
        TT(P, "dve", xx, xx.ap(), lre, lre.ap(), dtt, dtt.ap(), ALU.mult)
        TS(P, "dve", mag, mag.ap(), xx, xx.ap(), 1.0 / 720, None, ALU.mult)
        for cf in (1.0 / 120, 1.0 / 24, 1.0 / 6, 0.5, 1.0):
            STT(P, "dve", mag, mag.ap(), mag, mag.ap(), cf, xx, xx.ap(), ALU.add, ALU.mult)
        TS(P, "dve", mag, mag.ap(), mag, mag.ap(), 1.0, None, ALU.add)
        TT(P, "dve", th, th.ap(), lim, lim.ap(), dtt, dtt.ap(), ALU.mult)
        cc = P.sbuf("s_cc", [128, 64], F32)
        sn = P.sbuf("s_sn", [128, 64], F32)
        t1 = P.sbuf("s_t1", [128, 64], F32)
        t2 = P.sbuf("s_t2", [128, 64], F32)
        ACT(P, sn, sn.ap(), th, th.ap(), AF.Sin, scale=1.0 / 64)
        ACT(P, cc, cc.ap(), th, th.ap(), AF.Sin, scale=1.0 / 64, bias=G.hpicol.t[:, 0:1], extra_r=[G.hpicol])

        def renorm(c_t, c_ap, s_t, s_ap):
            TT(P, "dve", t1, t1.ap(), c_t, c_ap, c_t, c_ap, ALU.mult)
            TT(P, "dve", t2, t2.ap(), s_t, s_ap, s_t, s_ap, ALU.mult)
            TT(P, "dve", t1, t1.ap(), t1, t1.ap(), t2, t2.ap(), ALU.add)
            TS(P, "dve", t1, t1.ap(), t1, t1.ap(), -0.5, 1.5, ALU.mult, ALU.add)
            TT(P, "dve", c_t, c_ap, c_t, c_ap, t1, t1.ap(), ALU.mult)
            TT(P, "dve", s_t, s_ap, s_t, s_ap, t1, t1.ap(), ALU.mult)

        def square_cis(c_t, c_ap, s_t, s_ap, oc_t, oc_ap, os_t, os_ap):
            TT(P, "dve", t1, t1.ap(), c_t, c_ap, c_t, c_ap, ALU.mult)
            TT(P, "dve", t2, t2.ap(), s_t, s_ap, s_t, s_ap, ALU.mult)
            STT(P, "dve", os_t, os_ap, c_t, c_ap, 2.0, s_t, s_ap, ALU.mult, ALU.mult)
            TT(P, "dve", oc_t, oc_ap, t1, t1.ap(), t2, t2.ap(), ALU.subtract)
            renorm(oc_t, oc_ap, os_t, os_ap)
        c2 = P.sbuf("s_c2", [128, 64], F32)
        s2 = P.sbuf("s_s2", [128, 64], F32)
        cur = (cc, sn)
        nxt = (c2, s2)
        for _ in range(6):
            square_cis(cur[0], cur[0].ap(), cur[1], cur[1].ap(), nxt[0], nxt[0].ap(), nxt[1], nxt[1].ap())
            cur, nxt = nxt, cur
        cth, sth = cur
        pwc = P.sbuf("s_pwc", [128, NLV, 64], F32)
        pws = P.sbuf("s_pws", [128, NLV, 64], F32)
        npws = P.sbuf("s_npws", [128, NLV, 64], F32)
        CP(P, "dve", pwc, pwc.t[:, 0, :], cth, cth.ap())
        CP(P, "dve", pws, pws.t[:, 0, :], sth, sth.ap())
        for k in range(NLV - 1):
            square_cis(pwc, pwc.t[:, k, :], pws, pws.t[:, k, :], pwc, pwc.t[:, k + 1, :], pws, pws.t[:, k + 1, :])
        TS(P, "dve", npws, npws.ap(), pws, pws.ap(), -1.0, None, ALU.mult)
        PR = P.sbuf("s_PR", [128, 9, 64], F32)
        PI = P.sbuf("s_PI", [128, 9, 64], F32)
        NPR = P.sbuf("s_NPR", [128, 9, 64], F32)
        NPI = P.sbuf("s_NPI", [128, 9, 64], F32)
        are = P.sbuf("s_are", [128, 64], F32)
        aim = P.sbuf("s_aim", [128, 64], F32)
        TT(P, "dve", are, are.ap(), mag, mag.ap(), cth, cth.ap(), ALU.mult)
        TT(P, "dve", aim, aim.ap(), mag, mag.ap(), sth, sth.ap(), ALU.mult)
        P.dve(lambda e: e.memset(PR.t[:, 0, :], 1.0), writes=[PR])
        P.dve(lambda e: e.memset(PI.t[:, 0, :], 0.0), writes=[PI])
        for tau in range(8):
            TT(P, "dve", t1, t1.ap(), PR, PR.t[:, tau, :], are, are.ap(), ALU.mult)
            TT(P, "dve", t2, t2.ap(), PI, PI.t[:, tau, :], aim, aim.ap(), ALU.mult)
            TT(P, "dve", PR, PR.t[:, tau + 1, :], t1, t1.ap(), t2, t2.ap(), ALU.subtract)
            TT(P, "dve", t1, t1.ap(), PR, PR.t[:, tau, :], aim, aim.ap(), ALU.mult)
            TT(P, "dve", t2, t2.ap(), PI, PI.t[:, tau, :], are, are.ap(), ALU.mult)
            TT(P, "dve", PI, PI.t[:, tau + 1, :], t1, t1.ap(), t2, t2.ap(), ALU.add)
        TS(P, "dve", NPR, NPR.ap(), PR, PR.ap(), -1.0, None, ALU.mult)
        TS(P, "dve", NPI, NPI.ap(), PI, PI.ap(), -1.0, None, ALU.mult)
        mag8 = P.sbuf("s_mag8", [128, 64], F32)
        TT(P, "dve", mag8, mag8.ap(), mag, mag.ap(), mag, mag.ap(), ALU.mult)
        TT(P, "dve", mag8, mag8.ap(), mag8, mag8.ap(), mag8, mag8.ap(), ALU.mult)
        TT(P, "dve", mag8, mag8.ap(), mag8, mag8.ap(), mag8, mag8.ap(), ALU.mult)
        nr = P.sbuf("s_nr", [128, 64], F32)
        TS(P, "dve", nr, nr.ap(), are, are.ap(), -1.0, None, ALU.add)
        den = P.sbuf("s_den", [128, 64], F32)
        TT(P, "dve", den, den.ap(), lre, lre.ap(), lre, lre.ap(), ALU.mult)
        TT(P, "dve", t1, t1.ap(), lim, lim.ap(), lim, lim.ap(), ALU.mult)
        TT(P, "dve", den, den.ap(), den, den.ap(), t1, t1.ap(), ALU.add)
        P.dve(lambda e: e.reciprocal(out=den.ap(), in_=den.ap()), reads=[den], writes=[den])
        fre = P.sbuf("s_fre", [128, 64], F32)
        fim = P.sbuf("s_fim", [128, 64], F32)
        nfim = P.sbuf("s_nfim", [128, 64], F32)
        TT(P, "dve", t1, t1.ap(), nr, nr.ap(), lre, lre.ap(), ALU.mult)
        TT(P, "dve", t2, t2.ap(), aim, aim.ap(), lim, lim.ap(), ALU.mult)
        TT(P, "dve", fre, fre.ap(), t1, t1.ap(), t2, t2.ap(), ALU.add)
        TT(P, "dve", fre, fre.ap(), fre, fre.ap(), den, den.ap(), ALU.mult)
        TT(P, "dve", t1, t1.ap(), aim, aim.ap(), lre, lre.ap(), ALU.mult)
        TT(P, "dve", t2, t2.ap(), nr, nr.ap(), lim, lim.ap(), ALU.mult)
        TT(P, "dve", fim, fim.ap(), t1, t1.ap(), t2, t2.ap(), ALU.subtract)
        TT(P, "dve", fim, fim.ap(), fim, fim.ap(), den, den.ap(), ALU.mult)
        TS(P, "dve", nfim, nfim.ap(), fim, fim.ap(), -1.0, None, ALU.mult)

        ud = P.sbuf("s_ud", [128, 8, NCH], F32)
        ubd = P.sbuf("s_ubd", [128, 8, NCH], BF16)
        dcol = rots(P, "s_dcol", [128, 1], F32, 2)
        kacc = P.sbuf("s_kacc", [128, 15, 128], F32)
        kkb = P.sbuf("s_kkb", [128, 15, 128], BF16)
        ecos = rots(P, "s_ecos", [128, NS], F32, 2)
        esin = rots(P, "s_esin", [128, NS], F32, 2)
        rho = rots(P, "s_rho", [128, NS], F32, 2)
        braw = rots(P, "s_braw", [128, 2, 32], F32, 2)
        bbt = rots(P, "s_bbt", [128, 2, 32], F32, 2)
        craw = rots(P, "s_craw", [32, 2, 128], F32, 2)
        ctt = rots(P, "s_ct", [128, 2, 32], F32, 2)
        tmpA = rots(P, "s_tmpA", [128, 9, 32], F32, 8)
        baf = rots(P, "s_baf", [128, 8, 2, 32], BF16, 2)
        caf = rots(P, "s_caf", [128, 9, 2, 32], F32, 2)
        om = rots(P, "s_om", [128, 8, 2, 128], BF16, 2)
        ccp = rots(P, "s_ccp", [128, 9, 2, 128], BF16, 4)
        hsr = rots(P, "s_hs", [128, 2, NCH + 4], BF16, 4)
        wkd = [rots(P, "s_wk%d_" % d_, [128, NS], F32, 8) for d_ in range(2)]
        gl = rots(P, "s_gl", [128, 2], F32, 8)
        ygr = rots(P, "s_yg", [128, 512], BF16, 2)
        ygt = rots(P, "s_ygt", [128, 512], F32, 2)

        def lvl(n):
            k = n.bit_length() - 1
            assert (1 << k) == n
            return k + 3

        for c in range(8):
            P.dma(ud.ap(), G.UO[c], reads=G.UO_r[c], writes=[ud])
            CP(P, "act", ubd, ubd.ap(), ud, ud.ap())
            dc = dcol.next()
            P.dma(dc.ap(), G.s5_d[j][c * 128:(c + 1) * 128].rearrange("(p o) -> p o", o=1), writes=[dc])
            TS(P, "dve", ud, ud.ap(), ud, ud.ap(), dc.t[:, 0:1], None, ALU.mult, extra_r=[dc])
            P.pool(lambda e: e.memset(kacc.ap(), 0.0), writes=[kacc])
            def stream_gen(q, d, outl):
                gq_ = 8 * c + 2 * q
                pq = 4 * c + q
                rows = slice(32 * q, 32 * q + 32)
                colx = d * 32 + pq
                br = braw.next()
                P.pool(lambda e, br=br: e.memset(br.ap(), 0.0), writes=[br])
                for h in range(2):
                    P.dma(br.t[h * 64:(h + 1) * 64, 0, h * 16:(h + 1) * 16], G.s5_b_re[j, d, gq_ + h], writes=[br])
                    P.dma(br.t[h * 64:(h + 1) * 64, 1, h * 16:(h + 1) * 16], G.s5_b_im[j, d, gq_ + h], writes=[br])
                bt_ = bbt.next()
                frc, fic, nfic = fre.t[:, colx:colx + 1], fim.t[:, colx:colx + 1], nfim.t[:, colx:colx + 1]
                TS(P, "dve", bt_, bt_.t[:, 0, :], br, br.t[:, 0, :], frc, None, ALU.mult, extra_r=[fre])
                STT(P, "dve", bt_, bt_.t[:, 0, :], br, br.t[:, 1, :], nfic, bt_, bt_.t[:, 0, :], ALU.mult, ALU.add,
                    extra_r=[nfim])
                TS(P, "dve", bt_, bt_.t[:, 1, :], br, br.t[:, 1, :], frc, None, ALU.mult, extra_r=[fre])
                STT(P, "dve", bt_, bt_.t[:, 1, :], br, br.t[:, 0, :], fic, bt_, bt_.t[:, 1, :], ALU.mult, ALU.add,
                    extra_r=[fim])
                cr = craw.next()
                P.pool(lambda e, cr=cr: e.memset(cr.ap(), 0.0), writes=[cr])
                for h in range(2):
                    P.dma(cr.t[h * 16:(h + 1) * 16, 0, h * 64:(h + 1) * 64], G.s5_c_re[j, d, gq_ + h], writes=[cr])
                    P.dma(cr.t[h * 16:(h + 1) * 16, 1, h * 64:(h + 1) * 64], G.s5_c_im[j, d, gq_ + h], writes=[cr])
                tp2 = G.ps.next()
                TR(P, tp2, tp2.t[:, 0:32], cr, cr.t[:, 0, :], G.identf)
                TR(P, tp2, tp2.t[:, 32:64], cr, cr.t[:, 1, :], G.identf)
                ct = ctt.next()
                CP(P, "act", ct, ct.ap(), tp2, tp2.t[:, 0:64].rearrange("p (a m) -> p a m", a=2))
                def bc_p(tab, n):
                    return tab.t[:, 0:n, colx:colx + 1].to_broadcast([128, n, 32])

                def bc_x(t, a, n):
                    return t.t[:, a, :].unsqueeze(1).to_broadcast([128, n, 32])
                ba = baf.next()
                ca = caf.next()
                x1, x2 = tmpA.next(), tmpA.next()
                TT(P, "dve", x1, x1.t[:, 0:8, :], bt_, bc_x(bt_, 0, 8), PR, bc_p(PR, 8), ALU.mult)
                TT(P, "dve", x2, x2.t[:, 0:8, :], bt_, bc_x(bt_, 1, 8), PI, bc_p(PI, 8), ALU.mult)
                TT(P, "dve", ba, ba.t[:, :, 0, :], x1, x1.t[:, 0:8, :], x2, x2.t[:, 0:8, :], ALU.subtract)
                x3, x4 = tmpA.next(), tmpA.next()
                TT(P, "pool", x3, x3.t[:, 0:8, :], bt_, bc_x(bt_, 0, 8), PI, bc_p(PI, 8), ALU.mult)
                TT(P, "pool", x4, x4.t[:, 0:8, :], bt_, bc_x(bt_, 1, 8), PR, bc_p(PR, 8), ALU.mult)
                TT(P, "pool", ba, ba.t[:, :, 1, :], x3, x3.t[:, 0:8, :], x4, x4.t[:, 0:8, :], ALU.add)
                y1, y2 = tmpA.next(), tmpA.next()
                TT(P, "dve", y1, y1.ap(), ct, bc_x(ct, 0, 9), PR, bc_p(PR, 9), ALU.mult)
                TT(P, "dve", y2, y2.ap(), ct, bc_x(ct, 1, 9), PI, bc_p(PI, 9), ALU.mult)
                TT(P, "dve", ca, ca.t[:, :, 0, :], y1, y1.ap(), y2, y2.ap(), ALU.subtract)
                y3, y4 = tmpA.next(), tmpA.next()
                TT(P, "pool", y3, y3.ap(), ct, bc_x(ct, 0, 9), NPI, bc_p(NPI, 9), ALU.mult)
                TT(P, "pool", y4, y4.ap(), ct, bc_x(ct, 1, 9), NPR, bc_p(NPR, 9), ALU.mult)
                TT(P, "pool", ca, ca.t[:, :, 1, :], y3, y3.ap(), y4, y4.ap(), ALU.add)
                omt = om.next()
                P.act(lambda e, omt=omt: e.memzero(omt.ap()), writes=[omt])
                for half in range(2):
                    tpo = G.ps.next()
                    tpb = tpo.t[:].bitcast(BF16)
                    for tt_ in range(4):
                        tau = half * 4 + tt_
                        for a_ in range(2):
                            TR(P, tpo, tpb[0:32, (tt_ * 2 + a_) * 128:(tt_ * 2 + a_ + 1) * 128], ba, ba.t[:, tau, a_, :],
                               G.identb)
                    CP(P, "act", omt, omt.t[rows, half * 4:(half + 1) * 4, :, :], tpo,
                       tpb[0:32, 0:1024].rearrange("p (t a m) -> p t a m", t=4, a=2))
                cpt = ccp.next()
                P.act(lambda e, cpt=cpt: e.memzero(cpt.ap()), writes=[cpt])
                CP(P, "pool", cpt, cpt.t[:, :, :, rows], ca, ca.ap())
                psk = G.ps.next()
                for tau in range(8):
                    sl = slice(tau * 32, (tau + 1) * 32)
                    MM(P, psk, psk.t[0:32, sl], bt_, bt_.t[:, 0, :], ca, ca.t[:, tau, 0, :], tau == 0, False)
                for tau in range(8):
                    sl = slice(tau * 32, (tau + 1) * 32)
                    MM(P, psk, psk.t[0:32, sl], bt_, bt_.t[:, 1, :], ca, ca.t[:, tau, 1, :], False, tau == 7)
                if d == 0:
                    CP(P, "act", kacc, kacc.t[rows, 0, rows], psk, psk.t[0:32, 0:32])
                else:
                    TT(P, "dve", kacc, kacc.t[rows, 0, rows], psk, psk.t[0:32, 0:32], kacc, kacc.t[rows, 0, rows],
                       ALU.add)
                kb = 1 + 7 * d
                CP(P, "act", kacc, kacc.t[rows, kb:kb + 7, rows], psk,
                   psk.t[0:32, 32:256].rearrange("p (t m) -> p t m", t=7))
                yield
                ec, es = ecos.next(), esin.next()
                P.dve(lambda e, ec=ec: e.memset(ec.t[:, 0:1], 1.0), writes=[ec])
                P.dve(lambda e, es=es: e.memset(es.t[:, 0:1], 0.0), writes=[es])
                m = 1
                k = 3
                while m < NS:
                    ck, sk = pwc.t[:, k, colx:colx + 1], pws.t[:, k, colx:colx + 1]
                    nsk = npws.t[:, k, colx:colx + 1]
                    w1, w2 = wkd[d].next(), wkd[d].next()
                    w3 = wkd[d].next()
                    if m < 32:
                        TS(P, "dve", w1, w1.t[:, 0:m], ec, ec.t[:, 0:m], ck, None, ALU.mult, extra_r=[pwc])
                        TS(P, "dve", w2, w2.t[:, 0:m], es, es.t[:, 0:m], ck, None, ALU.mult, extra_r=[pwc])
                        TS(P, "dve", w3, w3.t[:, 0:m], ec, ec.t[:, 0:m], sk, None, ALU.mult, extra_r=[pws])
                    else:
                        ACT(P, w1, w1.t[:, 0:m], ec, ec.t[:, 0:m], AF.Copy, scale=ck, extra_r=[pwc])
                        ACT(P, w2, w2.t[:, 0:m], es, es.t[:, 0:m], AF.Copy, scale=ck, extra_r=[pwc])
                        ACT(P, w3, w3.t[:, 0:m], ec, ec.t[:, 0:m], AF.Copy, scale=sk, extra_r=[pws])
                    STT(P, "dve", ec, ec.t[:, m:2 * m], es, es.t[:, 0:m], nsk, w1, w1.t[:, 0:m], ALU.mult, ALU.add,
                        extra_r=[npws])
                    TT(P, "dve", es, es.t[:, m:2 * m], w2, w2.t[:, 0:m], w3, w3.t[:, 0:m], ALU.add)
                    m *= 2
                    k += 1
                    if m >= 32:
                        yield
                rh = rho.next()
                ACT(P, rh, rh.ap(), G.ones512, G.ones512.t[:, 0:NS], AF.Copy, scale=mag8.t[:, colx:colx + 1],
                    extra_r=[mag8])
                yield
                hs = hsr.next()
                P.act(lambda e, hs=hs: e.memzero(hs.ap()), writes=[hs])
                prev = None
                for si, (a, b, hcol) in enumerate(segs_f if d == 0 else segs_r):
                    if si > 0:
                        yield
                    n = b - a
                    rv = (lambda ap: ap) if d == 0 else (lambda ap: ap[:, ::-1])
                    pvr, pvi = G.ps.next(), G.ps.next()
                    for ip in range(8):
                        tau = 7 - ip if d == 0 else ip
                        MM(P, pvr, pvr.t[:, 0:n], omt, omt.t[:, tau, 0, :], ubd, ubd.t[:, ip, a:b], ip == 0, ip == 7)
                    for ip in range(8):
                        tau = 7 - ip if d == 0 else ip
                        MM(P, pvi, pvi.t[:, 0:n], omt, omt.t[:, tau, 1, :], ubd, ubd.t[:, ip, a:b], ip == 0, ip == 7)
                    m1, m2, m3, m4, gre, gim = [wkd[d].next() for _ in range(6)]
                    wre, wim = m1, m3
                    TT(P, "dve", m1, m1.t[:, 0:n], pvr, rv(pvr.t[:, 0:n]), ec, ec.t[:, 0:n], ALU.mult)
                    TT(P, "dve", m2, m2.t[:, 0:n], pvi, rv(pvi.t[:, 0:n]), es, es.t[:, 0:n], ALU.mult)
                    TT(P, "pool", wre, wre.t[:, 0:n], m1, m1.t[:, 0:n], m2, m2.t[:, 0:n], ALU.add)
                    TT(P, "dve", m3, m3.t[:, 0:n], pvi, rv(pvi.t[:, 0:n]), ec, ec.t[:, 0:n], ALU.mult)
                    TT(P, "dve", m4, m4.t[:, 0:n], pvr, rv(pvr.t[:, 0:n]), es, es.t[:, 0:n], ALU.mult)
                    TT(P, "pool", wim, wim.t[:, 0:n], m3, m3.t[:, 0:n], m4, m4.t[:, 0:n], ALU.subtract)
                    yield
                    if prev is None:
                        ire, iim = 0.0, 0.0
                        init_r = []
                    else:
                        pre, pim, pn = prev
                        kk = lvl(pn)
                        ck, sk = pwc.t[:, kk, colx:colx + 1], pws.t[:, kk, colx:colx + 1]
                        g_ = gl.next()
                        t_ = gl.next()
                        lr, li = pre.t[:, pn - 1:pn], pim.t[:, pn - 1:pn]
                        TS(P, "dve", t_, t_.t[:, 0:1], pre, lr, ck, None, ALU.mult, extra_r=[pwc])
                        TS(P, "dve", t_, t_.t[:, 1:2], pim, li, ck, None, ALU.mult, extra_r=[pwc])
                        TS(P, "dve", g_, g_.t[:, 0:1], pim, li, sk, None, ALU.mult, extra_r=[pws])
                        TT(P, "dve", g_, g_.t[:, 0:1], t_, t_.t[:, 0:1], g_, g_.t[:, 0:1], ALU.subtract)
                        STT(P, "dve", g_, g_.t[:, 1:2], pre, lr, sk, t_, t_.t[:, 1:2], ALU.mult, ALU.add,
                            extra_r=[pws])
                        ire, iim = g_.t[:, 0:1], g_.t[:, 1:2]
                        init_r = [g_]
                    P.dve(lambda e, gre=gre, wre=wre, rh=rh, ire=ire, n=n: e.tensor_tensor_scan(
                        out=gre.t[:, 0:n], data0=rh.t[:, 0:n], data1=wre.t[:, 0:n], initial=ire,
                        op0=ALU.mult, op1=ALU.add), reads=[rh, wre] + init_r, writes=[gre])
                    P.dve(lambda e, gim=gim, wim=wim, rh=rh, iim=iim, n=n: e.tensor_tensor_scan(
                        out=gim.t[:, 0:n], data0=rh.t[:, 0:n], data1=wim.t[:, 0:n], initial=iim,
                        op0=ALU.mult, op1=ALU.add), reads=[rh, wim] + init_r, writes=[gim])
                    prev = (gre, gim, n)
                    yield
                    o1, o2, o3, o4 = m2, m4, m1, m3
                    off = hcol + (1 if d == 0 else 0)
                    TT(P, "pool", o1, o1.t[:, 0:n], gre, gre.t[:, 0:n], ec, ec.t[:, 0:n], ALU.mult)
                    TT(P, "pool", o2, o2.t[:, 0:n], gim, gim.t[:, 0:n], es, es.t[:, 0:n], ALU.mult)
                    TT(P, "dve", hs, rv(hs.t[:, 0, off:off + n]), o1, o1.t[:, 0:n], o2, o2.t[:, 0:n], ALU.subtract)
                    TT(P, "pool", o3, o3.t[:, 0:n], gre, gre.t[:, 0:n], es, es.t[:, 0:n], ALU.mult)
                    TT(P, "pool", o4, o4.t[:, 0:n], gim, gim.t[:, 0:n], ec, ec.t[:, 0:n], ALU.mult)
                    TT(P, "dve", hs, rv(hs.t[:, 1, off:off + n]), o3, o3.t[:, 0:n], o4, o4.t[:, 0:n], ALU.add)
                    if si == 0:
                        if d == 0:
                            CP(P, "dve", hs, hs.t[:, :, XB:XB + 1], hs, hs.t[:, :, 32:33])
                        else:
                            CP(P, "dve", hs, hs.t[:, :, XB + NX:XB + NX + 1], hs, hs.t[:, :, 0:1])
                outl.append((hs, cpt))

            def gelu_blocks(bis):
                for bi in bis:
                    t0, ntok = G.blocks[bi]
                    nch = ntok // 8
                    ysl = ud.t[:, :, t0 // 8:t0 // 8 + nch]
                    g1, g2 = ygt.next(), ygt.next()
                    v3 = lambda t: t.t[:, 0:ntok].rearrange("p (i c) -> p i c", i=8)
                    ACT(P, g1, v3(g1), ud, ysl, AF.Square)
                    ACT(P, g1, g1.t[:, 0:ntok], g1, g1.t[:, 0:ntok], AF.Identity, scale=0.044715,
                        bias=G.onecol.t[:, 0:1], extra_r=[G.onecol])
                    TT(P, "pool", g1, v3(g1), g1, v3(g1), ud, ysl, ALU.mult)
                    ACT(P, g2, g2.t[:, 0:ntok], g1, g1.t[:, 0:ntok], AF.Sigmoid, scale=1.5957691216057308)
                    o = ygr.next()
                    TT(P, "dve", o, o.t[:, 0:ntok].rearrange("p (c i) -> p i c", i=8), g2, v3(g2), ud, ysl, ALU.mult)
                    P.dma(G.YG[c, :, t0:t0 + ntok], o.t[:, 0:ntok], reads=[o], writes=[G.YG_r[c][bi]], q="pool")

            def stageB_gen(q, streams):
                if q == 3:
                    CP(P, "pool", kkb, kkb.ap(), kacc, kacc.ap())
                order = segs_f if q < 3 else segs_f[1:] + segs_f[:1]
                for (a, b, hcol) in order:
                    n = b - a
                    for i in range(8):
                        py = G.ps.next()
                        mms = []
                        for d in range(2):
                            hs, cpt = streams[d]
                            tau = i + 1 if d == 0 else 8 - i
                            ro = hcol + (0 if d == 0 else 1)
                            mms.append((cpt, cpt.t[:, tau, 0, :], hs, hs.t[:, 0, ro:ro + n]))
                            mms.append((cpt, cpt.t[:, tau, 1, :], hs, hs.t[:, 1, ro:ro + n]))
                        if q == 3:
                            for ip in range(8):
                                if ip == i:
                                    ki = 0
                                elif ip < i:
                                    ki = i - ip
                                else:
                                    ki = 7 + (ip - i)
                                mms.append((kkb, kkb.t[:, ki, :], ubd, ubd.t[:, ip, a:b]))
                        for mi, (lt, lap, rt, rap) in enumerate(mms):
                            MM(P, py, py.t[:, 0:n], lt, lap, rt, rap, mi == 0, mi == len(mms) - 1)
                        TT(P, "dve", ud, ud.t[:, i, a:b], py, py.t[:, 0:n], ud, ud.t[:, i, a:b], ALU.add)
                        yield
                    if q == 3:
                        gelu_blocks([bi for bi, (t0, ntok) in enumerate(G.blocks) if a * 8 <= t0 < b * 8])

            def run_rr(gens):
                gens = list(gens)
                while gens:
                    for g_ in list(gens):
                        try:
                            next(g_)
                        except StopIteration:
                            gens.remove(g_)

            pendB = None
            for q in range(4):
                o0, o1 = [], []
                gl_ = [stream_gen(q, 0, o0), stream_gen(q, 1, o1)]
                if pendB is not None:
                    gl_.append(pendB)
                run_rr(gl_)
                pendB = stageB_gen(q, [o0[0], o1[0]])
            run_rr([pendB])
            if "YD" in G.debug:
                if c == 0:
                    G.YD = G.nc.dram_tensor("YD", [8, 128, 8, NCH], F32, kind="ExternalOutput").ap()
                P.dma(G.YD[c], ud.ap(), reads=[ud])
def odd_phase3(P, G, L, j):
    with phase(P):
        wbf = P.sbuf("gluw", [128, 8, 2048], BF16)
        stg = rots(P, "gwstg", [128, 2048], F32, 2)
        for k in range(8):
            s = stg.next()
            P.dma(s.ap(), G.glu_w[j][k * 128:(k + 1) * 128, :], writes=[s])
            CP(P, "pool", wbf, wbf.t[:, k, :], s, s.ap())
        gb = P.sbuf("glub", [128, 16], F32)
        P.dma(gb.ap(), G.glu_b[j].rearrange("(f p) -> p f", p=128), writes=[gb], allow_slow_non_contiguous=True)
        yrot = rots(P, "gyg", [128, 8, 512], BF16, 2)
        srot = rots(P, "gsg", [128, 8, 512], BF16, 2)
        arot = rots(P, "ga", [128, 512], F32, 3)
        brot = rots(P, "gb", [128, 512], F32, 3)
        orot = rots(P, "gout", [128, 512], BF16, 3)
        for bi, (t0, ntok) in enumerate(G.blocks):
            yg = yrot.next()
            P.dma(yg.t[:, :, 0:ntok], G.YG[:, :, t0:t0 + ntok].rearrange("c p t -> p c t"),
                  reads=[G.YG_r[c][bi] for c in range(8)], writes=[yg])
            sg = srot.next()
            P.dma(sg.t[:, :, 0:ntok], G.GT[:, :, t0:t0 + ntok].rearrange("c p t -> p c t"),
                  reads=[G.GT_r[c][bi] for c in range(8)], writes=[sg])
            for f in range(8):
                pa, pb = G.ps.next(), G.ps.next()
                for k in range(8):
                    MM(P, pa, pa.t[:, 0:ntok], wbf, wbf.t[:, k, f * 128:(f + 1) * 128], yg, yg.t[:, k, 0:ntok],
                       k == 0, k == 7)
                for k in range(8):
                    MM(P, pb, pb.t[:, 0:ntok], wbf, wbf.t[:, k, 1024 + f * 128:1024 + (f + 1) * 128], yg,
                       yg.t[:, k, 0:ntok], k == 0, k == 7)
                a_, b_ = arot.next(), brot.next()
                ACT(P, a_, a_.t[:, 0:ntok], pa, pa.t[:, 0:ntok], AF.Identity, bias=gb.t[:, f:f + 1], extra_r=[gb])
                ACT(P, b_, b_.t[:, 0:ntok], pb, pb.t[:, 0:ntok], AF.Sigmoid, bias=gb.t[:, 8 + f:9 + f], extra_r=[gb])
                TT(P, "dve", a_, a_.t[:, 0:ntok], a_, a_.t[:, 0:ntok], b_, b_.t[:, 0:ntok], ALU.mult)
                o = orot.next()
                TT(P, "pool", o, o.t[:, 0:ntok], a_, a_.t[:, 0:ntok], sg, sg.t[:, f, 0:ntok], ALU.mult)
                P.dma(G.MIXT[f, :, t0:t0 + ntok], o.t[:, 0:ntok], reads=[o], writes=[G.MIXT_r[f][bi]], q="pool")


def out_phase(P, G, L, w_dram, last):
    with phase(P):
        load_mod(P, G)
        wo = P.sbuf("wo", [128, 8, D], BF16)
        stg = rots(P, "wostg", [128, D], F32, 2)
        for k in range(8):
            s = stg.next()
            P.dma(s.ap(), w_dram[k * 128:(k + 1) * 128, :], writes=[s])
            CP(P, "pool", wo, wo.t[:, k, :], s, s.ap())
        mrot = rots(P, "mixT", [128, 8, 512], BF16, 2)
        xrot = rots(P, "oxt", [128, D], F32, 2)
        trot = rots(P, "otmp", [128, 512], F32, 2)
        orot = rots(P, "oxo", [128, D], F32, 2)
        for bi, (t0, ntok) in enumerate(G.blocks):
            if last and bi == 0:
                continue
            mixT = mrot.next()
            P.dma(mixT.t[:, :, 0:ntok], G.MIXT[:, :, t0:t0 + ntok].rearrange("c p t -> p c t"),
                  reads=[G.MIXT_r[c][bi] for c in range(8)], writes=[mixT])
            for tl in range(ntok // 128):
                tt = t0 // 128 + tl
                mod = G.modc if tt < 2 else G.modx
                xt = xrot.next()
                sap, sres = G.xsrc(tt)
                P.dma(xt.ap(), sap, reads=[sres], writes=[xt])
                xo = orot.next()
                for nh in range(2):
                    ps = G.ps.next()
                    for k in range(8):
                        MM(P, ps, ps.t[:, :], mixT, mixT.t[:, k, tl * 128:(tl + 1) * 128], wo,
                           wo.t[:, k, nh * 512:(nh + 1) * 512], k == 0, k == 7)
                    tmp = trot.next()
                    TT(P, "dve", tmp, tmp.ap(), ps, ps.t[:, :], mod,
                       mod.t[:, 2 * D + nh * 512:2 * D + (nh + 1) * 512], ALU.mult)
                    TT(P, "pool", xo, xo.t[:, nh * 512:(nh + 1) * 512], tmp, tmp.ap(), xt,
                       xt.t[:, nh * 512:(nh + 1) * 512], ALU.add)
                P.dma(G.XS[tt * 128:(tt + 1) * 128, :], xo.ap(), reads=[xo], writes=[G.XS_r[tt]], q="pool")


def final_phase(P, G):
    with phase(P):
        fw = P.sbuf("fw", [128, D], F32)
        P.dma(fw.ap(), G.final_norm_w.partition_broadcast(128), writes=[fw])
        xrot = rots(P, "fxt", [128, D], F32, 3)
        orot = rots(P, "fxo", [128, D], F32, 3)
        junk = P.sbuf("fjunk", [128, D], F32)
        ssrot = rots(P, "fss", [128, 1], F32, 4)
        for tt in range(2, G.NT):
            xt = xrot.next()
            P.dma(xt.ap(), G.XS[tt * 128:(tt + 1) * 128, :], reads=[G.XS_r[tt]], writes=[xt])
            ss = ssrot.next()
            ACT(P, junk, junk.ap(), xt, xt.ap(), AF.Square, accum=ss.t[:, 0:1], extra_w=[ss])
            RSQRT(P, ss, ss.t[:, 0:1], 1.0 / D, EPS)
            xo = orot.next()
            STT(P, "dve", xo, xo.ap(), xt, xt.ap(), ss.t[:, 0:1], fw, fw.ap(), ALU.mult, ALU.mult, extra_r=[ss])
            P.dma(G.out[(tt - 2) * 128:(tt - 1) * 128, :], xo.ap(), reads=[xo], q="pool")


USED_INPUTS = []


def build(S, depth=4, debug=(), stop_after=None):
    nc = bass.Bass("TRN2", target_bir_lowering=False)
    T = CTX + S
    NT = T // 128
    G = G_()
    G.debug = debug
    G.nc = nc
    G.S, G.T, G.NT = S, T, NT
    G.blocks = [(0, CTX)] + [(CTX + 512 * i, 512) for i in range(S // 512)]
    NB = len(G.blocks)

    def din(name, shape, dt=F32):
        if name not in USED_INPUTS:
            USED_INPUTS.append(name)
        return nc.dram_tensor(name, list(shape), dt, kind="ExternalInput").ap()

    def dscr(name, shape, dt):
        kind = "ExternalOutput" if name in debug else "Internal"
        return nc.dram_tensor(name, list(shape), dt, kind=kind).ap()

    G.x = din("x", [S, D])
    G.c = din("c", [D])
    G.ctx = din("ctx", [CTX, D])
    G.c_ctx = din("c_ctx", [D])
    G.ada_w = din("ada_w", [4, D, 3 * D])
    G.ada_b = din("ada_b", [4, 3 * D])
    G.ev_w_in = din("ev_w_in", [2, D, 2304])
    G.ev_w_out = din("ev_w_out", [2, D, D])
    G.q_norm_w = din("q_norm_w", [2, 64])
    G.k_norm_w = din("k_norm_w", [2, 64])
    G.rg_conv_w = din("rg_conv_w", [2, 4, 512])
    G.rg_conv_b = din("rg_conv_b", [2, 512])
    G.rg_wa = din("rg_wa", [2, 2, 8, 64, 64])
    G.rg_ba = din("rg_ba", [2, 2, 8, 64])
    G.rg_wx = din("rg_wx", [2, 2, 8, 64, 64])
    G.rg_bx = din("rg_bx", [2, 2, 8, 64])
    G.rg_lambda = din("rg_lambda", [2, 2, 512])
    G.final_norm_w = din("final_norm_w", [D])
    G.od_w_in = din("od_w_in", [2, D, 2048])
    G.s5_lambda_re = din("s5_lambda_re", [2, 2, 64, 64])
    G.s5_lambda_im = din("s5_lambda_im", [2, 2, 64, 64])
    G.s5_log_step = din("s5_log_step", [2, 2, 64])
    G.s5_b_re = din("s5_b_re", [2, 2, 64, 64, 16])
    G.s5_b_im = din("s5_b_im", [2, 2, 64, 64, 16])
    G.s5_c_re = din("s5_c_re", [2, 2, 64, 16, 64])
    G.s5_c_im = din("s5_c_im", [2, 2, 64, 16, 64])
    G.s5_d = din("s5_d", [2, D])
    G.glu_w = din("glu_w", [2, D, 2048])
    G.glu_b = din("glu_b", [2, 2048])
    G.od_w_out = din("od_w_out", [2, D, D])
    G.rope = din("rope", [S, 64])
    G.ident_in = din("ident", [128, 128])
    G.out = nc.dram_tensor("out", [S, D], F32, kind="ExternalOutput").ap()

    G.XS = dscr("XS", [T, D], F32)
    G.XS_r = [Res("XS%d" % i) for i in range(NT)]
    G.QT = dscr("QT", [128, 4, T], BF16)
    G.QT_r = [Res("QT%d" % i) for i in range(NT)]
    G.KT = dscr("KT", [128, T], BF16)
    G.KT_r = [Res("KT%d" % i) for i in range(NT)]
    G.VS = dscr("VS", [NT, 128, 2, 192], BF16)
    G.VS_r = [Res("VS%d" % i) for i in range(NT)]
    G.GT = dscr("GT", [8, 128, T], BF16)
    G.GT_r = [[Res("GT%d_%d" % (g, b)) for b in range(NB)] for g in range(8)]
    G.UT = dscr("UT", [4, 128, T], F32)
    G.UT_r = [[Res("UT%d_%d" % (g, b)) for b in range(NB)] for g in range(4)]
    G.UO = dscr("UO", [8, 128, 8, T // 8], F32)
    G.UO_r = [[Res("UO%d_%d" % (g, b)) for b in range(NB)] for g in range(8)]
    G.YG = dscr("YG", [8, 128, T], BF16)
    G.YG_r = [[Res("YG%d_%d" % (g, b)) for b in range(NB)] for g in range(8)]
    G.MIXT = dscr("MIXT", [8, 128, T], BF16)
    G.MIXT_r = [[Res("MX%d_%d" % (g, b)) for b in range(NB)] for g in range(8)]
    in_res = Res("inputs")

    P = Prog(nc)
    G.banks = [P.psum("ps%d" % i, [128, 512], F32) for i in range(8)]
    G.ps = Rot(G.banks)
    G.onecol = P.sbuf("onecol", [128, 1], F32)
    P.dve(lambda e: e.memset(G.onecol.ap(), 1.0), writes=[G.onecol])
    G.hpicol = P.sbuf("hpicol", [128, 1], F32)
    P.dve(lambda e: e.memset(G.hpicol.ap(), float(np.pi / 2)), writes=[G.hpicol])
    G.ones512 = P.sbuf("ones512", [128, 512], F32)
    P.dve(lambda e: e.memset(G.ones512.ap(), 1.0), writes=[G.ones512])
    G.MOD = dscr("MOD", [2, 3 * D], F32)
    G.MOD_r = Res("MOD")
    identf = P.sbuf("identf", [128, 128], F32)
    G.identb = P.sbuf("identb", [128, 128], BF16)
    P.dma(identf.ap(), G.ident_in, writes=[identf])
    CP(P, "dve", G.identb, G.identb.ap(), identf, identf.ap())
    G.identf = identf

    layer0 = [True]

    def xsrc(tt):
        if layer0[0]:
            if tt < 2:
                return G.ctx[tt * 128:(tt + 1) * 128, :], in_res
            return G.x[(tt - 2) * 128:(tt - 1) * 128, :], in_res
        return G.XS[tt * 128:(tt + 1) * 128, :], G.XS_r[tt]
    G.xsrc = xsrc

    def run_layers():
        for L in range(depth):
            layer0[0] = (L == 0)
            j = L // 2
            last = (L == depth - 1)
            if L % 2 == 0:
                seq = [("ada", lambda: ada_phase(P, G, L)), ("e1", lambda: even_phase1(P, G, L, j)),
                       ("e2", lambda: even_phase2(P, G, L, j)), ("e3", lambda: even_phase3(P, G, L, j)),
                       ("out", lambda: out_phase(P, G, L, G.ev_w_out[j], last))]
            else:
                seq = [("ada", lambda: ada_phase(P, G, L)), ("o1", lambda: odd_phase1(P, G, L, j)),
                       ("o2", lambda: odd_phase2(P, G, L, j)), ("o3", lambda: odd_phase3(P, G, L, j)),
                       ("out", lambda: out_phase(P, G, L, G.od_w_out[j], last))]
            for name, fn in seq:
                fn()
                if stop_after == (name, L):
                    return
    run_layers()
    if stop_after is None:
        final_phase(P, G)
    P.finish()
    return nc


def rope_table(S):
    t = np.arange(S)
    row = (t // 64).astype(np.float32)
    col = (t % 64).astype(np.float32)
    inv = (10000.0 ** (-np.arange(16, dtype=np.float32) / 16)).astype(np.float32)
    ang = np.concatenate([row[:, None] * inv, col[:, None] * inv], axis=-1).astype(np.float32)
    return np.concatenate([np.cos(ang), np.sin(ang)], axis=-1).astype(np.float32)


PER_BATCH = ("x", "c", "ctx")
_NC_CACHE = {}


def make_in_map(inp, b, S):
    m = {}
    for k, v in inp.items():
        v = np.asarray(v)
        if k == "x":
            m[k] = np.ascontiguousarray(v[b, :S])
        elif k in ("c", "ctx"):
            m[k] = np.ascontiguousarray(v[b])
        else:
            m[k] = np.ascontiguousarray(v)
    m["rope"] = rope_table(S)
    m["ident"] = np.eye(128, dtype=np.float32)
    return m


def kernel(**inputs):
    S = inputs["x"].shape[1]
    B = inputs["x"].shape[0]
    if S not in _NC_CACHE:
        _NC_CACHE[S] = build(S)
    nc = _NC_CACHE[S]
    names = set(USED_INPUTS)
    in_maps = []
    for b in range(B):
        m = make_in_map(inputs, b, S)
        in_maps.append({k: v for k, v in m.items() if k in names})
    res = run_bass_kernel_spmd(nc, in_maps, core_ids=list(range(B)))
    return np.stack([np.asarray(r["out"]) for r in res.results], axis=0).astype(np.float32)
```
